# Optimizing a Trainium2 kernel written in Bass

```python
import math
import jax
import jax.numpy as jnp
from jax import lax
import numpy as np

D_MODEL = 1024
BATCH = 8
SEQ = 4096
DEPTH = 2

GRID_W = 64
CTX_LEN = 256
D_FF = 2816
N_SUB = 3
N_MOD = 3 * N_SUB
MACARON_W = 0.5
ROPE_THETA = 10000.0
Q_BLOCK = 128
NEG_INF = -1e30
EPS = 1e-6

MLA_HEADS = 4
MLA_NOPE = 64
MLA_ROPE = 32
MLA_V = 64
MLA_Q_LORA = 192
MLA_KV_LORA = 128
MLA_W = MLA_HEADS * MLA_V
MLA_SCALE = (MLA_NOPE + MLA_ROPE) ** -0.5

HY_W = 256
HY_ORDER = 2
HY_EMB = 33
HY_FH = 64
HY_DECAY_TARGET = 1e-2
HY_FAST_PCT = 0.3
HY_SLOW_PCT = 1.5
SHORT_K = 3

SWA_HEADS = 4
SWA_KV_HEADS = 2
SWA_HD = 64
SWA_W = SWA_HEADS * SWA_HD
WINDOW = 128
SWA_BLOCK = WINDOW
SWA_SCALE = SWA_HD ** -0.5

S5_W = 256
S5_GC = 16
S5_GROUPS = S5_W // S5_GC
S5_P = 64
S5_DT_MIN = 1e-3
S5_DT_MAX = 1e-1

IN_SPLITS = (
    ('mla_ckv', MLA_KV_LORA),
    ('mla_krope', MLA_ROPE),
    ('swa_k', SWA_KV_HEADS * SWA_HD),
    ('swa_v', SWA_KV_HEADS * SWA_HD),
    ('s5_u', S5_W),
    ('mla_cq', MLA_Q_LORA),
    ('swa_q', SWA_W),
    ('hy', (HY_ORDER + 1) * HY_W),
)
N_SIDE = MLA_KV_LORA + MLA_ROPE + 2 * SWA_KV_HEADS * SWA_HD + S5_W
N_IN = N_SIDE + MLA_Q_LORA + SWA_W + (HY_ORDER + 1) * HY_W
BRANCH_PROJ = ('w_br_mla', 'w_br_hy', 'w_br_swa', 'w_br_s5')
N_BRANCH = len(BRANCH_PROJ)

kernel_name = 'hybrid_dit_parallel_mixers'


def rmsnorm(x, g):
    xf = x.astype(jnp.float32)
    y = xf * lax.rsqrt(jnp.mean(xf * xf, axis=-1, keepdims=True) + EPS)
    return (y * g.astype(jnp.float32)).astype(x.dtype)


def swiglu(u, w_up, w_down):
    a, b = jnp.split(u @ w_up, 2, axis=-1)
    return (jax.nn.silu(a) * b) @ w_down


def modulation(cvec, p):
    m = jax.nn.silu(cvec) @ p['w_ada'] + p['b_ada']
    return m.reshape(m.shape[:-1] + (N_MOD, D_MODEL))


def pre_mod(x, g, m, i):
    return rmsnorm(x, g) * (1.0 + m[..., None, 3 * i + 1, :]) + m[..., None, 3 * i, :]


def post_add(x, f, g, m, i, res_w):
    return x + res_w * m[..., None, 3 * i + 2, :] * rmsnorm(f, g)


def ffn_sublayer(x, m, p, i, w_up, w_down):
    u = pre_mod(x, p['norm_pre'][i], m, i)
    return post_add(x, swiglu(u, w_up, w_down), p['norm_post'][i], m, i, MACARON_W)


def split_cols(t):
    parts, off = {}, 0
    for name, size in IN_SPLITS:
        if off >= t.shape[-1]:
            break
        parts[name] = t[..., off:off + size]
        off += size
    return parts


def axial_rope(n_tokens, rot_dim):
    rows = n_tokens // GRID_W
    r = jnp.repeat(jnp.arange(rows, dtype=jnp.float32), GRID_W)
    col = jnp.tile(jnp.arange(GRID_W, dtype=jnp.float32), rows)
    n_freq = rot_dim // 4
    freqs = ROPE_THETA ** (-jnp.arange(n_freq, dtype=jnp.float32) / n_freq)
    ang = jnp.concatenate([r[:, None] * freqs, col[:, None] * freqs], axis=-1)
    return jnp.cos(ang), jnp.sin(ang)


def apply_rope(x, cos, sin):
    half = x.shape[-1] // 2
    x1, x2 = x[..., :half], x[..., half:]
    cos, sin = cos.astype(x.dtype), sin.astype(x.dtype)
    return jnp.concatenate([x1 * cos - x2 * sin, x1 * sin + x2 * cos], axis=-1)


def block_attention(q, k, v, scale):
    B, L, H, dq = q.shape
    nb = L // Q_BLOCK
    qb = jnp.moveaxis(q.reshape(B, nb, Q_BLOCK, H, dq), 1, 0)

    def one_block(qblk):
        s = jnp.einsum('bqhd,bkhd->bhqk', qblk, k).astype(jnp.float32) * scale
        pr = jax.nn.softmax(s, axis=-1).astype(v.dtype)
        return jnp.einsum('bhqk,bkhd->bqhd', pr, v)

    o = lax.map(one_block, qb)
    return jnp.moveaxis(o, 0, 1).reshape(B, L, -1)


def mla_queries(cq, p, rope):
    B, L, _ = cq.shape
    q = (rmsnorm(cq, p['mla_q_norm']) @ p['mla_w_uq']).reshape(B, L, MLA_HEADS, MLA_NOPE + MLA_ROPE)
    if rope is not None:
        q_rope = apply_rope(q[..., MLA_NOPE:], rope[0][:, None], rope[1][:, None])
        q = jnp.concatenate([q[..., :MLA_NOPE], q_rope], axis=-1)
    return q


def mla_keys_values(ckv, krope, p, rope):
    B, L, _ = ckv.shape
    kv = (rmsnorm(ckv, p['mla_kv_norm']) @ p['mla_w_ukv']).reshape(B, L, MLA_HEADS, MLA_NOPE + MLA_V)
    if rope is not None:
        krope = apply_rope(krope, rope[0], rope[1])
    k_rope = jnp.broadcast_to(krope[:, :, None, :], (B, L, MLA_HEADS, MLA_ROPE))
    k = jnp.concatenate([kv[..., :MLA_NOPE], k_rope], axis=-1)
    return k, kv[..., MLA_NOPE:]


def short_conv(x, w, b):
    L = x.shape[1]
    r = SHORT_K // 2
    xp = jnp.pad(x, ((0, 0), (r, r), (0, 0)))
    y = b
    for j in range(SHORT_K):
        y = y + xp[:, j:j + L] * w[j]
    return y


def hyena_filter_response(L, p):
    f32 = jnp.float32
    t = jnp.linspace(0.0, 1.0, L, dtype=f32)[:, None]
    bands = (HY_EMB - 1) // 2
    w = 2.0 * math.pi * jnp.arange(L, dtype=f32) / L
    fr = jnp.linspace(1e-4, bands - 1, bands, dtype=f32)
    ang = w[:, None] * fr[None, :]
    z = jnp.concatenate([t, jnp.cos(ang), -jnp.sin(ang)], axis=-1)
    freq = p['hy_f_freq'].astype(f32)
    h = jnp.sin(freq[0] * (z @ p['hy_f_w1'].astype(f32) + p['hy_f_b1'].astype(f32)))
    h = jnp.sin(freq[1] * (h @ p['hy_f_w2'].astype(f32) + p['hy_f_b2'].astype(f32)))
    h = (h @ p['hy_f_w3'].astype(f32)).reshape(L, HY_ORDER, 2, HY_W)
    deltas = jnp.abs(jnp.linspace(math.log(HY_DECAY_TARGET) / HY_SLOW_PCT,
                                  math.log(HY_DECAY_TARGET) / HY_FAST_PCT, HY_W, dtype=f32))
    h = h * jnp.exp(-t[:, :, None, None] * deltas)
    h_fwd, h_bwd = h[:, :, 0], h[:, :, 1]
    k_full = jnp.concatenate([h_fwd, jnp.zeros_like(h_fwd[:1]), h_bwd[1:][::-1]], axis=0)
    return jnp.fft.rfft(k_full, axis=0)


def fft_long_conv(z, kf, bias):
    L = z.shape[1]
    zf32 = z.astype(jnp.float32)
    y = jnp.fft.irfft(jnp.fft.rfft(zf32, n=2 * L, axis=1) * kf[None], n=2 * L, axis=1)[:, :L]
    return (y + zf32 * bias.astype(jnp.float32)).astype(z.dtype)


def hyena(proj, p):
    L = proj.shape[1]
    u = short_conv(proj, p['hy_conv_w'], p['hy_conv_b'])
    v, *gates = jnp.split(u, HY_ORDER + 1, axis=-1)
    kf = hyena_filter_response(L, p)
    z = v
    for o in range(HY_ORDER):
        z = gates[o] * fft_long_conv(z, kf[:, o], p['hy_bias'][o])
    return z


def sink_attention(q, k, v, sink):
    B, L, _, _ = q.shape
    g = SWA_HEADS // SWA_KV_HEADS
    qg = q.reshape(B, L, SWA_KV_HEADS, g, SWA_HD)
    s = jnp.einsum('bqkgd,bjkd->bkgqj', qg, k).astype(jnp.float32) * SWA_SCALE
    snk = jnp.broadcast_to(sink.astype(jnp.float32).reshape(SWA_KV_HEADS, g)[None, :, :, None, None],
                           s.shape[:-1] + (1,))
    pr = jax.nn.softmax(jnp.concatenate([s, snk], axis=-1), axis=-1)[..., :-1].astype(v.dtype)
    return jnp.einsum('bkgqj,bjkd->bqkgd', pr, v).reshape(B, L, SWA_W)


def banded_sink_attention(q, k, v, k_ctx, v_ctx, sink):
    B, S, _, _ = q.shape
    g = SWA_HEADS // SWA_KV_HEADS
    nb = S // SWA_BLOCK
    span = 3 * SWA_BLOCK
    qb = q.reshape(B, nb, SWA_BLOCK, SWA_KV_HEADS, g, SWA_HD)

    def band(t):
        tp = jnp.pad(t, ((0, 0), (SWA_BLOCK, SWA_BLOCK), (0, 0), (0, 0)))
        tp = tp.reshape(B, nb + 2, SWA_BLOCK, SWA_KV_HEADS, SWA_HD)
        return jnp.concatenate([tp[:, :-2], tp[:, 1:-1], tp[:, 2:]], axis=2)

    kw, vw = band(k), band(v)
    qpos = jnp.arange(S).reshape(nb, SWA_BLOCK)
    kpos = jnp.arange(nb)[:, None] * SWA_BLOCK - SWA_BLOCK + jnp.arange(span)[None, :]
    valid = ((jnp.abs(qpos[:, :, None] - kpos[:, None, :]) <= WINDOW)
             & (kpos[:, None, :] >= 0) & (kpos[:, None, :] < S))
    s_w = jnp.einsum('bnqkgd,bnjkd->bnkgqj', qb, kw).astype(jnp.float32) * SWA_SCALE
    s_w = jnp.where(valid[None, :, None, None], s_w, NEG_INF)
    s_c = jnp.einsum('bnqkgd,bjkd->bnkgqj', qb, k_ctx).astype(jnp.float32) * SWA_SCALE
    snk = jnp.broadcast_to(sink.astype(jnp.float32).reshape(SWA_KV_HEADS, g)[None, None, :, :, None, None],
                           s_w.shape[:-1] + (1,))
    pr = jax.nn.softmax(jnp.concatenate([s_w, s_c, snk], axis=-1), axis=-1).astype(v.dtype)
    n_ctx = k_ctx.shape[1]
    o = (jnp.einsum('bnkgqj,bnjkd->bnqkgd', pr[..., :span], vw)
         + jnp.einsum('bnkgqj,bjkd->bnqkgd', pr[..., span:span + n_ctx], v_ctx))
    return o.reshape(B, S, SWA_W)


def s5_discretize(p, d):
    f32 = jnp.float32
    a_re = p['s5_a_re'][d].astype(f32)
    a_im = p['s5_a_im'][d].astype(f32)
    dt = jnp.exp(p['s5_log_dt'][d].astype(f32))[:, None]
    mag = jnp.exp(dt * a_re)
    ab_re, ab_im = mag * jnp.cos(dt * a_im), mag * jnp.sin(dt * a_im)
    den = a_re * a_re + a_im * a_im
    f_re = ((ab_re - 1.0) * a_re + ab_im * a_im) / den
    f_im = (ab_im * a_re - (ab_re - 1.0) * a_im) / den
    return ab_re, ab_im, f_re, f_im


def complex_scan(ab_re, ab_im, x_re, x_im, h0, reverse):
    a_re = jnp.broadcast_to(ab_re, x_re.shape)
    a_im = jnp.broadcast_to(ab_im, x_re.shape)

    def combine(e1, e2):
        a1r, a1i, b1r, b1i = e1
        a2r, a2i, b2r, b2i = e2
        return (a2r * a1r - a2i * a1i, a2r * a1i + a2i * a1r,
                a2r * b1r - a2i * b1i + b2r, a2r * b1i + a2i * b1r + b2i)

    cr, ci, hr, hi = lax.associative_scan(combine, (a_re, a_im, x_re, x_im), reverse=reverse, axis=1)
    if h0 is not None:
        h0r, h0i = h0[0][:, None], h0[1][:, None]
        hr, hi = hr + cr * h0r - ci * h0i, hi + cr * h0i + ci * h0r
    return hr, hi


def s5_run(u, p, init):
    B, L, _ = u.shape
    ug = u.astype(jnp.float32).reshape(B, L, S5_GROUPS, S5_GC)
    states = []
    for d in range(2):
        ab_re, ab_im, f_re, f_im = s5_discretize(p, d)
        bu_re = jnp.einsum('blgc,gpc->blgp', ug, p['s5_b_re'][d].astype(jnp.float32))
        bu_im = jnp.einsum('blgc,gpc->blgp', ug, p['s5_b_im'][d].astype(jnp.float32))
        x_re = f_re * bu_re - f_im * bu_im
        x_im = f_re * bu_im + f_im * bu_re
        states.append(complex_scan(ab_re, ab_im, x_re, x_im, init[d], reverse=(d == 1)))
    return states


def s5_readout(u, states, p):
    B, L, _ = u.shape
    f32 = jnp.float32
    y = u.astype(f32) * p['s5_d'].astype(f32)
    for d, (hr, hi) in enumerate(states):
        yd = (jnp.einsum('blgp,gcp->blgc', hr, p['s5_c_re'][d].astype(f32))
              - jnp.einsum('blgp,gcp->blgc', hi, p['s5_c_im'][d].astype(f32)))
        y = y + yd.reshape(B, L, S5_W)
    g = jax.nn.gelu(y)
    out = g * jax.nn.sigmoid(g @ p['s5_glu_w'].astype(f32) + p['s5_glu_b'].astype(f32))
    return out.astype(u.dtype)


def merge_branches(u, outs, p):
    m = None
    for i, (o, name) in enumerate(zip(outs, BRANCH_PROJ)):
        gate = jax.nn.sigmoid(u @ p['w_gate'][i] + p['b_gate'][i])
        term = gate * (o @ p[name])
        m = term if m is None else m + term
    return m @ p['w_out']


def mixer_context(u, p, with_outputs):
    B, C, _ = u.shape
    w_in = p['w_in'] if with_outputs else p['w_in'][:, :N_SIDE]
    parts = split_cols(u @ w_in)
    k_mla, v_mla = mla_keys_values(parts['mla_ckv'], parts['mla_krope'], p, None)
    k_swa = parts['swa_k'].reshape(B, C, SWA_KV_HEADS, SWA_HD)
    v_swa = parts['swa_v'].reshape(B, C, SWA_KV_HEADS, SWA_HD)
    states = s5_run(parts['s5_u'], p, (None, None))
    (fr, fi), (br, bi) = states
    finals = ((fr[:, -1], fi[:, -1]), (br[:, 0], bi[:, 0]))
    side = (k_mla, v_mla, k_swa, v_swa, finals)
    if not with_outputs:
        return side, None
    o_mla = block_attention(mla_queries(parts['mla_cq'], p, None), k_mla, v_mla, MLA_SCALE)
    o_hy = hyena(parts['hy'], p)
    o_swa = sink_attention(parts['swa_q'].reshape(B, C, SWA_HEADS, SWA_HD), k_swa, v_swa, p['swa_sink'])
    o_s5 = s5_readout(parts['s5_u'], states, p)
    return side, merge_branches(u, (o_mla, o_hy, o_swa, o_s5), p)


def mixer_latent(u, side, p):
    B, S, _ = u.shape
    k_mla_c, v_mla_c, k_swa_c, v_swa_c, s5_h0 = side
    parts = split_cols(u @ p['w_in'])
    rope_m = axial_rope(S, MLA_ROPE)
    rope_w = axial_rope(S, SWA_HD)
    k_l, v_l = mla_keys_values(parts['mla_ckv'], parts['mla_krope'], p, rope_m)
    q_m = mla_queries(parts['mla_cq'], p, rope_m)
    o_mla = block_attention(q_m, jnp.concatenate([k_l, k_mla_c], axis=1),
                            jnp.concatenate([v_l, v_mla_c], axis=1), MLA_SCALE)
    o_hy = hyena(parts['hy'], p)
    cw, sw = rope_w[0][:, None], rope_w[1][:, None]
    q_w = apply_rope(parts['swa_q'].reshape(B, S, SWA_HEADS, SWA_HD), cw, sw)
    k_w = apply_rope(parts['swa_k'].reshape(B, S, SWA_KV_HEADS, SWA_HD), cw, sw)
    v_w = parts['swa_v'].reshape(B, S, SWA_KV_HEADS, SWA_HD)
    o_swa = banded_sink_attention(q_w, k_w, v_w, k_swa_c, v_swa_c, p['swa_sink'])
    states = s5_run(parts['s5_u'], p, s5_h0)
    o_s5 = s5_readout(parts['s5_u'], states, p)
    return merge_branches(u, (o_mla, o_hy, o_swa, o_s5), p)


def layer(xl, xc, ml, mc, p, ctx_out):
    xl = ffn_sublayer(xl, ml, p, 0, p['ffn1_up'], p['ffn1_down'])
    xc = ffn_sublayer(xc, mc, p, 0, p['ffn1_up'], p['ffn1_down'])
    ul = pre_mod(xl, p['norm_pre'][1], ml, 1)
    uc = pre_mod(xc, p['norm_pre'][1], mc, 1)
    side, oc = mixer_context(uc, p, ctx_out)
    ol = mixer_latent(ul, side, p)
    xl = post_add(xl, ol, p['norm_post'][1], ml, 1, 1.0)
    xl = ffn_sublayer(xl, ml, p, 2, p['ffn2_up'], p['ffn2_down'])
    if not ctx_out:
        return xl, None
    xc = post_add(xc, oc, p['norm_post'][1], mc, 1, 1.0)
    xc = ffn_sublayer(xc, mc, p, 2, p['ffn2_up'], p['ffn2_down'])
    return xl, xc


def setup_inputs(seed: int = 0) -> dict:
    key = jax.random.key(seed)
    keys = iter(jax.random.split(key, 64))
    f32 = jnp.float32

    def nrm(shape, std):
        return std * jax.random.normal(next(keys), shape, f32)

    def gain(shape):
        return 1.0 + nrm(shape, 0.02)

    inp = {}
    inp['x'] = nrm((BATCH, SEQ, D_MODEL), 1.0)
    inp['c'] = nrm((BATCH, D_MODEL), 1.0)
    inp['ctx'] = nrm((BATCH, CTX_LEN, D_MODEL), 1.0)
    inp['c_ctx'] = nrm((D_MODEL,), 1.0)
    inp['w_ada'] = nrm((DEPTH, D_MODEL, N_MOD * D_MODEL), 0.5 * D_MODEL ** -0.5)
    inp['b_ada'] = nrm((DEPTH, N_MOD * D_MODEL), 0.02)
    inp['norm_pre'] = gain((DEPTH, N_SUB, D_MODEL))
    inp['norm_post'] = gain((DEPTH, N_SUB, D_MODEL))
    inp['ffn1_up'] = nrm((DEPTH, D_MODEL, 2 * D_FF), D_MODEL ** -0.5)
    inp['ffn1_down'] = nrm((DEPTH, D_FF, D_MODEL), D_FF ** -0.5)
    inp['ffn2_up'] = nrm((DEPTH, D_MODEL, 2 * D_FF), D_MODEL ** -0.5)
    inp['ffn2_down'] = nrm((DEPTH, D_FF, D_MODEL), D_FF ** -0.5)
    inp['w_in'] = nrm((DEPTH, D_MODEL, N_IN), D_MODEL ** -0.5)
    inp['mla_q_norm'] = gain((DEPTH, MLA_Q_LORA))
    inp['mla_kv_norm'] = gain((DEPTH, MLA_KV_LORA))
    inp['mla_w_uq'] = nrm((DEPTH, MLA_Q_LORA, MLA_HEADS * (MLA_NOPE + MLA_ROPE)), MLA_Q_LORA ** -0.5)
    inp['mla_w_ukv'] = nrm((DEPTH, MLA_KV_LORA, MLA_HEADS * (MLA_NOPE + MLA_V)), MLA_KV_LORA ** -0.5)
    inp['hy_conv_w'] = nrm((DEPTH, SHORT_K, (HY_ORDER + 1) * HY_W), SHORT_K ** -0.5)
    inp['hy_conv_b'] = nrm((DEPTH, (HY_ORDER + 1) * HY_W), 0.02)
    inp['hy_f_w1'] = nrm((DEPTH, HY_EMB, HY_FH), HY_EMB ** -0.5)
    inp['hy_f_b1'] = nrm((DEPTH, HY_FH), 0.02)
    inp['hy_f_freq'] = gain((DEPTH, 2, HY_FH))
    inp['hy_f_w2'] = nrm((DEPTH, HY_FH, HY_FH), HY_FH ** -0.5)
    inp['hy_f_b2'] = nrm((DEPTH, HY_FH), 0.02)
    inp['hy_f_w3'] = nrm((DEPTH, HY_FH, HY_ORDER * 2 * HY_W), 0.1 * HY_FH ** -0.5)
    inp['hy_bias'] = nrm((DEPTH, HY_ORDER, HY_W), 0.5)
    inp['swa_sink'] = nrm((DEPTH, SWA_HEADS), 0.5)
    inp['s5_a_re'] = -0.5 + nrm((DEPTH, 2, S5_GROUPS, S5_P), 0.01)
    inp['s5_a_im'] = math.pi * jnp.arange(S5_P, dtype=f32) + nrm((DEPTH, 2, S5_GROUPS, S5_P), 0.01)
    inp['s5_log_dt'] = jax.random.uniform(next(keys), (DEPTH, 2, S5_GROUPS), f32,
                                          math.log(S5_DT_MIN), math.log(S5_DT_MAX))
    inp['s5_b_re'] = nrm((DEPTH, 2, S5_GROUPS, S5_P, S5_GC), (2 * S5_GC) ** -0.5)
    inp['s5_b_im'] = nrm((DEPTH, 2, S5_GROUPS, S5_P, S5_GC), (2 * S5_GC) ** -0.5)
    inp['s5_c_re'] = nrm((DEPTH, 2, S5_GROUPS, S5_GC, S5_P), S5_P ** -0.5)
    inp['s5_c_im'] = nrm((DEPTH, 2, S5_GROUPS, S5_GC, S5_P), S5_P ** -0.5)
    inp['s5_d'] = nrm((DEPTH, S5_W), 1.0)
    inp['s5_glu_w'] = nrm((DEPTH, S5_W, S5_W), S5_W ** -0.5)
    inp['s5_glu_b'] = nrm((DEPTH, S5_W), 0.02)
    inp['w_gate'] = nrm((DEPTH, N_BRANCH, D_MODEL, D_MODEL), D_MODEL ** -0.5)
    inp['b_gate'] = nrm((DEPTH, N_BRANCH, D_MODEL), 0.02)
    inp['w_br_mla'] = nrm((DEPTH, MLA_W, D_MODEL), MLA_W ** -0.5)
    inp['w_br_hy'] = nrm((DEPTH, HY_W, D_MODEL), HY_W ** -0.5)
    inp['w_br_swa'] = nrm((DEPTH, SWA_W, D_MODEL), SWA_W ** -0.5)
    inp['w_br_s5'] = nrm((DEPTH, S5_W, D_MODEL), S5_W ** -0.5)
    inp['w_out'] = nrm((DEPTH, D_MODEL, D_MODEL), D_MODEL ** -0.5)
    return inp


def reference(x, c, ctx, c_ctx, w_ada, b_ada, norm_pre, norm_post, ffn1_up, ffn1_down, ffn2_up, ffn2_down,
              w_in, mla_q_norm, mla_kv_norm, mla_w_uq, mla_w_ukv, hy_conv_w, hy_conv_b, hy_f_w1, hy_f_b1,
              hy_f_freq, hy_f_w2, hy_f_b2, hy_f_w3, hy_bias, swa_sink, s5_a_re, s5_a_im, s5_log_dt,
              s5_b_re, s5_b_im, s5_c_re, s5_c_im, s5_d, s5_glu_w, s5_glu_b, w_gate, b_gate,
              w_br_mla, w_br_hy, w_br_swa, w_br_s5, w_out):
    stacked = dict(w_ada=w_ada, b_ada=b_ada, norm_pre=norm_pre, norm_post=norm_post,
                   ffn1_up=ffn1_up, ffn1_down=ffn1_down, ffn2_up=ffn2_up, ffn2_down=ffn2_down,
                   w_in=w_in, mla_q_norm=mla_q_norm, mla_kv_norm=mla_kv_norm, mla_w_uq=mla_w_uq,
                   mla_w_ukv=mla_w_ukv, hy_conv_w=hy_conv_w, hy_conv_b=hy_conv_b, hy_f_w1=hy_f_w1,
                   hy_f_b1=hy_f_b1, hy_f_freq=hy_f_freq, hy_f_w2=hy_f_w2, hy_f_b2=hy_f_b2, hy_f_w3=hy_f_w3,
                   hy_bias=hy_bias, swa_sink=swa_sink, s5_a_re=s5_a_re, s5_a_im=s5_a_im,
                   s5_log_dt=s5_log_dt, s5_b_re=s5_b_re, s5_b_im=s5_b_im, s5_c_re=s5_c_re,
                   s5_c_im=s5_c_im, s5_d=s5_d, s5_glu_w=s5_glu_w, s5_glu_b=s5_glu_b,
                   w_gate=w_gate, b_gate=b_gate, w_br_mla=w_br_mla, w_br_hy=w_br_hy,
                   w_br_swa=w_br_swa, w_br_s5=w_br_s5, w_out=w_out)
    xl, xc = x, ctx
    for l in range(DEPTH):
        p = {name: arr[l] for name, arr in stacked.items()}
        ml = modulation(c, p)
        mc = modulation(c_ctx, p)
        xl, xc = layer(xl, xc, ml, mc, p, l < DEPTH - 1)
    return xl
```

```python
import math
import numpy as np
from contextlib import ExitStack
import concourse.bass as bass
import concourse.mybir as mybir
from concourse.bass_utils import run_bass_kernel_spmd

F32 = mybir.dt.float32
BF16 = mybir.dt.bfloat16
I32 = mybir.dt.int32
ALU = mybir.AluOpType
AF = mybir.ActivationFunctionType
NDS = 96

D = 1024
DFF = 2816
SEQ = 4096
CTX = 256
NTOK = SEQ + CTX
NT = NTOK // 128
DEPTH = 2
N_IN = 1888
TWO_PI = 2.0 * math.pi


class Buf:
    def __init__(self, name, h):
        self.name = name
        self.h = h
        self.last_w = None
        self.reads = {}
        self.dsem = {}

    def __getitem__(self, key):
        return self.h[key]


class K:
    def __init__(self):
        self.nc = bass.Bass("TRN2", target_bir_lowering=False)
        nc = self.nc
        self.E = dict(pe=nc.tensor, act=nc.scalar, dve=nc.vector, pool=nc.gpsimd, sp=nc.sync)
        self.esem = {k: nc.alloc_semaphore("s_" + k) for k in ("pe", "act", "dve", "pool")}
        self.ecnt = {k: 0 for k in self.esem}
        self.dsems = [nc.alloc_semaphore("d%d" % i) for i in range(NDS)]
        self.dcnt = [0] * NDS
        self.dfree = {"sw": list(range(0, 52)), "hw": list(range(52, NDS))}
        self.seen = {k: {} for k in self.E}
        self.uid = 0
        self.nins = 0
        self._d2d = {}
        self.dbg = set()

    def dram(self, name, shape, dt, kind="Internal"):
        if name in self.dbg:
            kind = "ExternalOutput"
        return self.nc.dram_tensor(name, list(shape), dt, kind=kind).ap()

    def stage(self):
        k = self

        class _S:
            def __enter__(s):
                s.es = ExitStack()
                s.es.__enter__()
                s.bufs = []
                return s

            def __exit__(s, *a):
                if a[0] is None:
                    k.barrier()
                for b in s.bufs:
                    for kind, si in b.dsem.items():
                        k.dfree[kind].append(si)
                    b.dsem = {}
                s.es.__exit__(*a)
                return False

            def sb(s, shape, dt, name=None):
                k.uid += 1
                nm = (name or "t") + "_%d" % k.uid
                h = s.es.enter_context(k.nc.sbuf_tensor(nm, list(shape), dt))
                b = Buf(nm, h)
                s.bufs.append(b)
                return b

            def ps(s, shape, dt=F32, name=None):
                k.uid += 1
                nm = (name or "p") + "_%d" % k.uid
                h = s.es.enter_context(k.nc.psum_tensor(nm, list(shape), dt))
                b = Buf(nm, h)
                s.bufs.append(b)
                return b

        return _S()

    def _semh(self, key):
        return self.esem[key[1]] if key[0] == "e" else self.dsems[key[1]]

    def _wait(self, eng, ev):
        if ev is None:
            return
        key, val = ev
        if self.seen[eng].get(key, 0) >= val:
            return
        self.E[eng].wait_ge(self._semh(key), val)
        self.seen[eng][key] = val

    def _deps(self, eng, reads, writes):
        pe = ("e", "pe")
        for b in reads:
            if not (eng == "pe" and b.last_w and b.last_w[0] == pe):
                self._wait(eng, b.last_w)
        for b in writes:
            if not (eng == "pe" and b.last_w and b.last_w[0] == pe):
                self._wait(eng, b.last_w)
            for key, val in list(b.reads.items()):
                if eng == "pe" and key == pe:
                    continue
                self._wait(eng, (key, val))

    def _mark(self, ev, reads, writes):
        for b in reads:
            b.reads[ev[0]] = max(b.reads.get(ev[0], 0), ev[1])
        for b in writes:
            b.last_w = ev
            b.reads = {}

    def op(self, eng, fn, reads=(), writes=()):
        reads = [b for b in reads if isinstance(b, Buf)]
        writes = [b for b in writes if isinstance(b, Buf)]
        self._deps(eng, reads, writes)
        ins = fn(self.E[eng])
        self.ecnt[eng] += 1
        ev = (("e", eng), self.ecnt[eng])
        ins.then_inc(self.esem[eng], 1)
        self._mark(ev, reads, writes)
        self.nins += 1
        return ins

    def dma(self, q, out, in_, reads=(), writes=()):
        reads = [b for b in reads if isinstance(b, Buf)]
        writes = [b for b in writes if isinstance(b, Buf)]
        self._deps(q, reads, writes)
        bs = reads + writes
        kind = "sw" if q == "pool" else "hw"
        if bs:
            b0 = bs[0]
            if kind not in b0.dsem:
                b0.dsem[kind] = self.dfree[kind].pop(0)
            si = b0.dsem[kind]
        else:
            if self._d2d.get(kind) is None:
                self._d2d[kind] = self.dfree[kind].pop(0)
            si = self._d2d[kind]
        ins = self.E[q].dma_start(out=out, in_=in_)
        self.dcnt[si] += 16
        ins.then_inc(self.dsems[si], 16)
        ev = (("d", si), self.dcnt[si])
        self._mark(ev, reads, writes)
        self.nins += 1
        return ins

    def barrier(self):
        for eng in ("sp", "pe", "act", "dve", "pool"):
            for e2 in ("pe", "act", "dve", "pool"):
                if e2 != eng and self.ecnt[e2] > 0:
                    self._wait(eng, (("e", e2), self.ecnt[e2]))
            for i in range(NDS):
                if self.dcnt[i] > 0:
                    self._wait(eng, (("d", i), self.dcnt[i]))

    def mm(self, out, lhsT, rhs, start=True, stop=True, R=(), W=()):
        return self.op("pe", lambda e: e.matmul(out, lhsT, rhs, start=start, stop=stop), R, W)

    def tr(self, out, in_, ident, R=(), W=()):
        return self.op("pe", lambda e: e.transpose(out, in_, ident), R, W)

    def act(self, out, in_, func, R=(), W=(), **kw):
        return self.op("act", lambda e: e.activation(out, in_, func, **kw), R, W)

    def tt(self, out, in0, in1, op, R=(), W=(), eng="dve"):
        return self.op(eng, lambda e: e.tensor_tensor(out, in0, in1, op), R, W)

    def ts(self, out, in0, s1, s2, op0, op1=None, R=(), W=(), eng="dve"):
        if op1 is None:
            return self.op(eng, lambda e: e.tensor_scalar(out, in0, s1, None, op0), R, W)
        return self.op(eng, lambda e: e.tensor_scalar(out, in0, s1, s2, op0, op1), R, W)

    def stt(self, out, in0, scalar, in1, op0, op1, R=(), W=(), eng="dve"):
        return self.op(eng, lambda e: e.scalar_tensor_tensor(out, in0, scalar, in1, op0, op1), R, W)

    def cp(self, out, in_, R=(), W=(), eng="dve"):
        if eng == "act":
            return self.op("act", lambda e: e.copy(out, in_), R, W)
        return self.op(eng, lambda e: e.tensor_copy(out, in_), R, W)

    def memset(self, ap, val, W=(), eng="dve"):
        return self.op(eng, lambda e: e.memset(ap, val), (), W)

    def recip(self, out, in_, R=(), W=()):
        return self.op("dve", lambda e: e.reciprocal(out, in_), R, W)


def _axial(n_tokens, rot_dim):
    rows = n_tokens // 64
    r = np.repeat(np.arange(rows, dtype=np.float64), 64)
    col = np.tile(np.arange(64, dtype=np.float64), rows)
    n_freq = rot_dim // 4
    freqs = 10000.0 ** (-np.arange(n_freq, dtype=np.float64) / n_freq)
    ang = np.concatenate([r[:, None] * freqs, col[:, None] * freqs], axis=-1)
    return np.concatenate([np.cos(ang), np.sin(ang)], axis=-1).astype(np.float32)


def _hy_consts(L):
    N = 2 * L
    pos = np.zeros(N, dtype=np.int64)
    pos[:L] = np.arange(L)
    pos[L + 1:] = N - np.arange(L + 1, N)
    t = np.linspace(0.0, 1.0, L)[pos]
    bands = 16
    w = 2.0 * np.pi * pos / L
    fr = np.linspace(1e-4, bands - 1, bands)
    ang = w[:, None] * fr[None, :]
    z = np.concatenate([t[:, None], np.cos(ang), -np.sin(ang)], axis=-1)
    deltas = np.abs(np.linspace(math.log(1e-2) / 1.5, math.log(1e-2) / 0.3, 256))
    dec = np.exp(-t[:, None] * deltas[None, :])
    dec[L] = 0.0
    return np.ascontiguousarray(z.T).astype(np.float32), dec.astype(np.float32)


def _fft_consts(N1):
    N = 128 * N1
    a = np.arange(N1)
    ph = 2.0 * np.pi * ((a[:, None] * a[None, :]) % N1) / N1
    w1f = np.concatenate([np.cos(ph), -np.sin(ph)], axis=1)
    w1i = np.zeros((2 * N1, 2 * N1))
    w1i[:N1, :N1] = np.cos(ph)
    w1i[N1:, :N1] = -np.sin(ph)
    w1i[:N1, N1:] = np.sin(ph)
    w1i[N1:, N1:] = np.cos(ph)
    tl = np.arange(128)[:, None, None]
    f1 = np.arange(N1)[None, :, None]
    f2 = np.arange(128)[None, None, :]
    phi = 2.0 * np.pi * ((tl * (f1 + N1 * f2)) % N) / N
    mt = np.stack([np.cos(phi), -np.sin(phi), np.sin(phi)], axis=2)
    return w1f.astype(np.float32), w1i.astype(np.float32), np.ascontiguousarray(mt).astype(np.float32)


def _w1ib():
    w = np.zeros((128, 2, 2, 128))
    t1 = np.arange(64)
    for f2 in range(128):
        b = f2 % 2
        fh = f2 // 2
        ph = 2.0 * np.pi * ((t1 * fh) % 64) / 64.0
        w[f2, b, 0, 0:64] = np.cos(ph)
        w[f2, b, 0, 64:128] = np.sin(ph)
        w[f2, b, 1, 0:64] = -np.sin(ph)
        w[f2, b, 1, 64:128] = np.cos(ph)
    return w.astype(np.float32)


def host_consts():
    C = {}
    C["c_ident"] = np.eye(128, dtype=np.float32)
    C["c_ropem"] = _axial(SEQ, 32)
    C["c_ropew"] = _axial(SEQ, 64)
    kk = np.arange(128)[:, None, None]
    rel = np.arange(6)[None, :, None]
    qq = np.arange(512)[None, None, :]
    C["c_mask"] = (np.abs(qq - (128 * (rel - 1) + kk)) <= 128).astype(np.float32)
    C["c_w1ib"] = _w1ib()
    nn = np.arange(512)
    phd = 2.0 * np.pi * ((nn[:, None] * nn[None, :]) % 512) / 512.0
    C["c_dft512"] = np.stack([np.cos(phd), -np.sin(phd)], axis=1).astype(np.float32)
    for nm, L in (("l", SEQ), ("c", CTX)):
        zt, dec = _hy_consts(L)
        C["c_zt" + nm] = zt
        C["c_dec" + nm] = dec
        w1f, w1i, mt = _fft_consts(2 * L // 128)
        C["c_w1f" + nm] = w1f
        C["c_w1i" + nm] = w1i
        C["c_mt" + nm] = mt
    return C


WEIGHT_SPECS = [
    ("w_ada", (DEPTH, D, 9 * D)), ("b_ada", (DEPTH, 9 * D)), ("norm_pre", (DEPTH, 3, D)), ("norm_post", (DEPTH, 3, D)),
    ("ffn1_up", (DEPTH, D, 2 * DFF)), ("ffn1_down", (DEPTH, DFF, D)), ("ffn2_up", (DEPTH, D, 2 * DFF)),
    ("ffn2_down", (DEPTH, DFF, D)), ("w_in", (DEPTH, D, N_IN)), ("mla_q_norm", (DEPTH, 192)), ("mla_kv_norm", (DEPTH, 128)),
    ("mla_w_uq", (DEPTH, 192, 384)), ("mla_w_ukv", (DEPTH, 128, 512)), ("hy_conv_w", (DEPTH, 3, 768)),
    ("hy_conv_b", (DEPTH, 768)), ("hy_f_w1", (DEPTH, 33, 64)), ("hy_f_b1", (DEPTH, 64)), ("hy_f_freq", (DEPTH, 2, 64)),
    ("hy_f_w2", (DEPTH, 64, 64)), ("hy_f_b2", (DEPTH, 64)), ("hy_f_w3", (DEPTH, 64, 1024)), ("hy_bias", (DEPTH, 2, 256)),
    ("swa_sink", (DEPTH, 4)), ("s5_a_re", (DEPTH, 2, 16, 64)), ("s5_a_im", (DEPTH, 2, 16, 64)), ("s5_log_dt", (DEPTH, 2, 16)),
    ("s5_b_re", (DEPTH, 2, 16, 64, 16)), ("s5_b_im", (DEPTH, 2, 16, 64, 16)), ("s5_c_re", (DEPTH, 2, 16, 16, 64)),
    ("s5_c_im", (DEPTH, 2, 16, 16, 64)), ("s5_d", (DEPTH, 256)), ("s5_glu_w", (DEPTH, 256, 256)), ("s5_glu_b", (DEPTH, 256)),
    ("w_gate", (DEPTH, 4, D, D)), ("b_gate", (DEPTH, 4, D)), ("w_br_mla", (DEPTH, 256, D)), ("w_br_hy", (DEPTH, 256, D)),
    ("w_br_swa", (DEPTH, 256, D)), ("w_br_s5", (DEPTH, 256, D)), ("w_out", (DEPTH, D, D)),
]


class MK:
    def __init__(self, dbg=(), stop_after=None, only=None):
        self.only = only
        self.k = K()
        self.k.dbg = set(dbg)
        self.stop_after = stop_after
        k = self.k
        self.es = ExitStack()
        self.es.enter_context(k.nc.allow_low_precision("bf16 matmul operands, fp32 accumulation"))
        self.es.enter_context(k.nc.allow_non_contiguous_dma("small strided parameter loads"))
        I = lambda n, s: k.dram(n, s, F32, kind="ExternalInput")
        self.x = I("x", (SEQ, D))
        self.c = I("c", (D,))
        self.ctx = I("ctx", (CTX, D))
        self.c_ctx = I("c_ctx", (D,))
        self.w = {n: I(n, s) for n, s in WEIGHT_SPECS}
        self.cst = {n: I(n, v.shape) for n, v in host_consts().items()}
        self.out = k.dram("out", (SEQ, D), F32, kind="ExternalOutput")
        S = lambda n, s, dt=F32: k.dram(n, s, dt)
        self.Xs = S("Xs", (NTOK, D))
        self.MODV = [S("MODV%d" % l, (2, 9, D)) for l in range(DEPTH)]
        self.UT = S("UT", (8, 128, NTOK), BF16)
        self.QTm = S("QTm", (4, 96, NTOK), BF16)
        self.KTmN = S("KTmN", (4, 64, NTOK), BF16)
        self.KTmR = S("KTmR", (32, NTOK), BF16)
        self.Vm = S("Vm", (NTOK, 256), BF16)
        self.QTs = S("QTs", (4, 64, NTOK), BF16)
        self.KTs = S("KTs", (2, 64, NTOK), BF16)
        self.Vs = S("Vs", (NTOK, 128), BF16)
        self.U5T = S("U5T", (256, NTOK))
        self.HYP = {"l": S("HYPl", (SEQ + 2, 768)), "c": S("HYPc", (CTX + 2, 768))}
        self.SC = {"l": S("SCl", (SEQ, 768)), "c": S("SCc", (CTX, 768))}
        self.KFULL = {"l": S("KFULLl", (2, 2 * SEQ, 256)), "c": S("KFULLc", (2, 2 * CTX, 256))}
        self.KF = {"l": S("KFl", (2, 128, 64, 512)), "c": S("KFc", (2, 128, 4, 512))}
        self.A1 = S("A1", (128, 128 * 256), BF16)
        self.B1 = S("B1", (128, 128 * 256), BF16)
        self.Y = S("Yf", (2 * SEQ, 512), BF16)
        self.Z1 = {"l": S("Z1l", (SEQ, 256)), "c": S("Z1c", (CTX, 256))}
        self.OHY = S("OHY", (NTOK, 256))
        self.OTm = S("OTm", (256, NTOK), BF16)
        self.OTs = S("OTs", (256, NTOK), BF16)
        self.OT5 = S("OT5", (256, NTOK), BF16)
        self.YG = S("YG", (256, NTOK))
        self.S5FIN = S("S5FIN", (2, 8, 128, 2))
        self.wbf = {}
        for key in (("ffn2_up", 0), ("ffn2_down", 0), ("ffn1_up", 1), ("ffn1_down", 1), ("ffn2_up", 1), ("ffn2_down", 1)):
            shp = (D, 2 * DFF) if key[0].endswith("up") else (DFF, D)
            self.wbf[key] = S("wbf_%s_%d" % key, shp, BF16)
        self.precast_plan = {0: [("ffn2_up", 0), ("ffn2_down", 0), ("ffn1_up", 1), ("ffn1_down", 1)],
                             1: [("ffn2_up", 1), ("ffn2_down", 1)]}

    def rstd_of(self, s, pieces, n, junk, ss, eps=1e-6):
        k = self.k
        rows = pieces[0][0].shape[0]
        k.memset(ss[:rows, :], 0.0, W=[ss])
        for i, (ap, b) in enumerate(pieces):
            m = ap.shape[-1]
            k.act(junk[:rows, :m], ap, AF.Square, R=[b], W=[junk, ss], accum_out=ss[:rows, 1 + i:2 + i])
        if len(pieces) == 2:
            k.tt(ss[:rows, 1:2], ss[:rows, 1:2], ss[:rows, 2:3], ALU.add, R=[ss], W=[ss])
        k.ts(ss[:rows, 0:1], ss[:rows, 1:2], 1.0 / n, eps, ALU.mult, ALU.add, R=[ss], W=[ss])
        k.recip(ss[:rows, 0:1], ss[:rows, 0:1], R=[ss], W=[ss])
        k.act(ss[:rows, 0:1], ss[:rows, 0:1], AF.Sqrt, R=[ss], W=[ss])

    def load_ident(self, s):
        k = self.k
        idf = s.sb([128, 128], F32, "idf")
        idb = s.sb([128, 128], BF16, "idb")
        k.dma("sp", idf[:], self.cst["c_ident"][:, :], writes=[idf])
        k.dma("pool", idb[:], self.cst["c_ident"][:, :], writes=[idb])
        return idf, idb

    def bvec(self, s, dst, src_row_ap, rows=128):
        self.k.dma("sp", dst[:rows], src_row_ap.partition_broadcast(rows), writes=[dst])

    def range_reduce(self, s, arg, shape, tmpf, tmpi):
        k = self.k
        k.ts(tmpf[:], arg[:], 1.0 / TWO_PI, None, ALU.mult, R=[arg], W=[tmpf])
        k.cp(tmpi[:], tmpf[:], R=[tmpf], W=[tmpi])
        k.cp(tmpf[:], tmpi[:], R=[tmpi], W=[tmpf])
        k.stt(arg[:], tmpf[:], -TWO_PI, arg[:], ALU.mult, ALU.add, R=[tmpf, arg], W=[arg])

    def stage_init(self):
        k = self.k
        with k.stage() as s:
            k.dma("sp", self.Xs[0:CTX, :], self.ctx[:, :])
            k.dma("sp", self.Xs[CTX:NTOK, :], self.x[:, :])
            z = s.sb([2, 768], F32)
            k.memset(z[:], 0.0, W=[z])
            for nm, L in (("l", SEQ), ("c", CTX)):
                k.dma("sp", self.HYP[nm][0:1, :], z[0:1, :], reads=[z])
                k.dma("sp", self.HYP[nm][L + 1:L + 2, :], z[1:2, :], reads=[z])

    def stage_mod(self, l):
        k = self.k
        with k.stage() as s:
            cin = s.sb([128, 8, 2], F32)
            k.dma("sp", cin[:, :, 0], self.c.rearrange("(kc p) -> p kc", p=128), writes=[cin])
            k.dma("sp", cin[:, :, 1], self.c_ctx.rearrange("(kc p) -> p kc", p=128), writes=[cin])
            scb = s.sb([128, 8, 2], BF16)
            k.act(scb[:], cin[:], AF.Silu, R=[cin], W=[scb])
            M = s.sb([2, 9 * D], F32)
            pp = [s.ps([2, 512], F32) for _ in range(2)]
            CG = 3 * D
            wsets = [[s.sb([128, CG], BF16, "wada") for _ in range(8)] for _ in range(2)]
            bbs = [s.sb([2, CG], F32, "bb") for _ in range(2)]
            for cg in range(3):
                wa = wsets[cg % 2]
                bb = bbs[cg % 2]
                for kc in range(8):
                    k.dma("pool", wa[kc][:], self.w["w_ada"][l, kc * 128:(kc + 1) * 128, cg * CG:(cg + 1) * CG],
                          writes=[wa[kc]])
                self.bvec(s, bb, self.w["b_ada"][l:l + 1, cg * CG:(cg + 1) * CG], rows=2)
                for cb in range(6):
                    p = pp[cb % 2]
                    for kc in range(8):
                        k.mm(p[:], scb[:, kc, :], wa[kc][:, cb * 512:(cb + 1) * 512], start=kc == 0, stop=kc == 7,
                             R=[scb, wa[kc]], W=[p])
                    c0 = cg * CG + cb * 512
                    k.tt(M[:, c0:c0 + 512], p[:], bb[:, cb * 512:(cb + 1) * 512], ALU.add, R=[p, bb], W=[M])
            npre = s.sb([2, 3 * D], F32)
            npost = s.sb([2, 3 * D], F32)
            self.bvec(s, npre, self.w["norm_pre"][l:l + 1].rearrange("o i d -> o (i d)"), rows=2)
            self.bvec(s, npost, self.w["norm_post"][l:l + 1].rearrange("o i d -> o (i d)"), rows=2)
            MV = M
            for i in range(3):
                resw = 1.0 if i == 1 else 0.5
                sl = lambda j: slice((3 * i + j) * D, (3 * i + j + 1) * D)
                k.stt(MV[:, sl(1)], M[:, sl(1)], 1.0, npre[:, i * D:(i + 1) * D], ALU.add, ALU.mult, R=[M, npre], W=[MV])
                k.stt(MV[:, sl(2)], M[:, sl(2)], resw, npost[:, i * D:(i + 1) * D], ALU.mult, ALU.mult, R=[M, npost], W=[MV])
            k.dma("sp", self.MODV[l].rearrange("c i d -> c (i d)"), MV[:], reads=[MV])

    def stage_ffn(self, l, sub, wup_name, wdn_name, tiles, final=False):
        k = self.k
        w_up = self.w[wup_name]
        w_dn = self.w[wdn_name]
        with k.stage() as s:
            idf, idb = self.load_ident(s)
            up_pre = self.wbf.get((wup_name, l))
            dn_pre = self.wbf.get((wdn_name, l))
            wup = []
            for kc in range(8):
                t = s.sb([128, 2 * DFF], BF16, "wup")
                if up_pre is not None:
                    k.dma("sp" if kc % 2 == 0 else "act", t[:], up_pre[kc * 128:(kc + 1) * 128, :], writes=[t])
                else:
                    k.dma("pool", t[:], w_up[l, kc * 128:(kc + 1) * 128, :], writes=[t])
                wup.append(t)
            wdn = []
            for j in range(22):
                t = s.sb([128, D], BF16, "wdn")
                if dn_pre is not None:
                    k.dma("sp" if j % 2 == 0 else "act", t[:], dn_pre[j * 128:(j + 1) * 128, :], writes=[t])
                else:
                    k.dma("pool", t[:], w_dn[l, j * 128:(j + 1) * 128, :], writes=[t])
                wdn.append(t)

            def WA(j, kc):
                return wup[kc][:, j * 128:(j + 1) * 128], wup[kc]

            def WB(j, kc):
                return wup[kc][:, DFF + j * 128:DFF + (j + 1) * 128], wup[kc]

            def WD(j, half):
                return wdn[j][:, half * 512:(half + 1) * 512], wdn[j]

            G = s.sb([128, D], F32, "G")
            Sh = s.sb([128, D], F32, "Sh")
            GP = s.sb([128, D], F32, "GP")
            xa = [s.sb([128, D], F32, "xa") for _ in range(2)]
            xe = [s.sb([128, D], F32, "xe") for _ in range(1)]
            u32 = s.sb([128, D], F32, "u32")
            ub = s.sb([128, D], BF16, "ub")
            MT_ = 512
            uT = s.sb([128, 8, MT_], BF16, "uT")
            gT = s.sb([128, 22, MT_], BF16, "gT")
            sa = [s.sb([128, MT_], F32, "sa") for _ in range(2)]
            ss = s.sb([128, 4], F32, "ss")
            ptr = s.ps([128, 4, 128], BF16, "ptr")
            pU = [s.ps([128, MT_], F32, "pU") for _ in range(3)]
            pfb = [s.ps([128, D], F32, "pf") for _ in range(2)]
            macros = []
            for ti in tiles:
                cls = 1 if ti < 2 else 0
                if macros and macros[-1][0] == cls and len(macros[-1][1]) < 4:
                    macros[-1][1].append(ti)
                else:
                    macros.append((cls, [ti]))
            uTs = [uT, s.sb([128, 8, MT_], BF16, "uT2")]
            state = dict(xcnt=0, ucnt=0, cls_pre=None, cls_down=None)

            def pre_tile(mi, t_i):
                cls, grp = macros[mi]
                ti = grp[t_i]
                if cls != state["cls_pre"]:
                    state["cls_pre"] = cls
                    mv = self.MODV[l]
                    self.bvec(s, Sh, mv[cls, 3 * sub + 0:3 * sub + 1, :])
                    self.bvec(s, G, mv[cls, 3 * sub + 1:3 * sub + 2, :])
                ut = uTs[mi % 2]
                xt = xa[state["xcnt"] % 2]; state["xcnt"] += 1
                k.dma("sp", xt[:], self.Xs[ti * 128:(ti + 1) * 128, :], writes=[xt])
                self.rstd_of(s, [(xt[:, 0:512], xt), (xt[:, 512:1024], xt)], D, u32, ss)
                k.stt(u32[:], xt[:], ss[:, 0:1], G[:], ALU.mult, ALU.mult, R=[xt, ss, G], W=[u32])
                k.tt(ub[:], u32[:], Sh[:], ALU.add, R=[u32, Sh], W=[ub])
                for g in range(2):
                    for q in range(4):
                        kc = g * 4 + q
                        k.tr(ptr[:, q, :], ub[:, kc * 128:(kc + 1) * 128], idb[:], R=[ub, idb], W=[ptr])
                    k.cp(ut[:, g * 4:(g + 1) * 4, t_i * 128:(t_i + 1) * 128], ptr[:], R=[ptr], W=[ut], eng="act")

            for t_i in range(len(macros[0][1])):
                pre_tile(0, t_i)
            for mi, (cls, grp) in enumerate(macros):
                n = 128 * len(grp)
                ut = uTs[mi % 2]
                nxt = list(range(len(macros[mi + 1][1]))) if mi + 1 < len(macros) else []
                for j in range(22):
                    pa = pU[state["ucnt"] % 3]; pb = pU[(state["ucnt"] + 1) % 3]; state["ucnt"] += 2
                    for kc in range(8):
                        wap, wbuf = WA(j, kc)
                        k.mm(pa[:, 0:n], wap, ut[:, kc, 0:n], start=kc == 0, stop=kc == 7, R=[wbuf, ut], W=[pa])
                    for kc in range(8):
                        wbp, wbuf = WB(j, kc)
                        k.mm(pb[:, 0:n], wbp, ut[:, kc, 0:n], start=kc == 0, stop=kc == 7, R=[wbuf, ut], W=[pb])
                    sj = sa[j % 2]
                    k.act(sj[:, 0:n], pa[:, 0:n], AF.Silu, R=[pa], W=[sj])
                    k.tt(gT[:, j, 0:n], sj[:, 0:n], pb[:, 0:n], ALU.mult, R=[sj, pb], W=[gT])
                    if nxt and j in (3, 8, 13, 18):
                        pre_tile(mi + 1, nxt.pop(0))
                while nxt:
                    pre_tile(mi + 1, nxt.pop(0))
                if cls != state["cls_down"]:
                    state["cls_down"] = cls
                    self.bvec(s, GP, self.MODV[l][cls, 3 * sub + 2:3 * sub + 3, :])
                for t_i, ti in enumerate(grp):
                    xt = xe[0]
                    k.dma("sp", xt[:], self.Xs[ti * 128:(ti + 1) * 128, :], writes=[xt])
                    pf = pfb[t_i % 2]
                    for half in range(2):
                        for j in range(22):
                            wdp, wbuf = WD(j, half)
                            k.mm(pf[:, half * 512:(half + 1) * 512], gT[:, j, t_i * 128:(t_i + 1) * 128],
                                 wdp, start=j == 0, stop=j == 21, R=[gT, wbuf], W=[pf])
                    self.rstd_of(s, [(pf[:, 0:512], pf), (pf[:, 512:1024], pf)], D, u32, ss)
                    for half in range(2):
                        hs = slice(half * 512, (half + 1) * 512)
                        k.stt(u32[:, hs], pf[:, hs], ss[:, 0:1], GP[:, hs], ALU.mult, ALU.mult, R=[pf, ss, GP], W=[u32])
                    k.tt(xt[:], xt[:], u32[:], ALU.add, R=[xt, u32], W=[xt])
                    if final and ti >= 2:
                        k.dma("sp", self.out[(ti - 2) * 128:(ti - 1) * 128, :], xt[:], reads=[xt])
                    else:
                        k.dma("sp", self.Xs[ti * 128:(ti + 1) * 128, :], xt[:], reads=[xt])

    def rope_tm(self, s, src, dst, off, nh, hd, half, cs, tmp1, tmp2):
        k = self.k
        v = lambda b, a, w: b[:, off:off + nh * hd].rearrange("p (h d) -> p h d", h=nh)[:, :, a:a + w]
        x1 = v(src, 0, half); x2 = v(src, half, half)
        o1 = v(dst, 0, half); o2 = v(dst, half, half)
        t1 = tmp1[:, 0:nh * half].rearrange("p (h d) -> p h d", h=nh)
        t2 = tmp2[:, 0:nh * half].rearrange("p (h d) -> p h d", h=nh)
        cb = cs[:, 0:half].unsqueeze(1).broadcast_to([128, nh, half])
        sb_ = cs[:, half:2 * half].unsqueeze(1).broadcast_to([128, nh, half])
        k.tt(t1, x1, cb, ALU.mult, R=[src, cs], W=[tmp1])
        k.tt(t2, x2, sb_, ALU.mult, R=[src, cs], W=[tmp2])
        k.tt(o1, t1, t2, ALU.subtract, R=[tmp1, tmp2], W=[dst])
        k.tt(t1, x1, sb_, ALU.mult, R=[src, cs], W=[tmp1])
        k.tt(t2, x2, cb, ALU.mult, R=[src, cs], W=[tmp2])
        k.tt(o2, t1, t2, ALU.add, R=[tmp1, tmp2], W=[dst])

    def stage_proj(self, l, tiles, ctx_q):
        k = self.k
        W = self.w
        with k.stage() as s:
            idf, idb = self.load_ident(s)
            win = []
            for kc in range(8):
                t = s.sb([128, N_IN], BF16, "win")
                k.dma("pool", t[:], W["w_in"][l, kc * 128:(kc + 1) * 128, :], writes=[t])
                win.append(t)
            gkv = s.sb([128, 1], F32); gq = s.sb([128, 2], F32)
            k.dma("sp", gkv[:], W["mla_kv_norm"][l].rearrange("(p o) -> p o", o=1), writes=[gkv])
            k.dma("sp", gq[:, 0:1], W["mla_q_norm"][l, 0:128].rearrange("(p o) -> p o", o=1), writes=[gq])
            k.dma("sp", gq[0:64, 1:2], W["mla_q_norm"][l, 128:192].rearrange("(p o) -> p o", o=1), writes=[gq])
            wtmp = s.sb([128, 512], F32)
            wukv = s.sb([128, 512], BF16)
            k.dma("sp", wtmp[:], W["mla_w_ukv"][l], writes=[wtmp])
            k.ts(wukv[:], wtmp[:], gkv[:, 0:1], None, ALU.mult, R=[wtmp, gkv], W=[wukv])
            wuq = s.sb([128, 2, 384], BF16)
            k.dma("sp", wtmp[:, 0:384], W["mla_w_uq"][l, 0:128, :], writes=[wtmp])
            k.ts(wuq[:, 0, :], wtmp[:, 0:384], gq[:, 0:1], None, ALU.mult, R=[wtmp, gq], W=[wuq])
            k.dma("sp", wtmp[0:64, 0:384], W["mla_w_uq"][l, 128:192, :], writes=[wtmp])
            k.ts(wuq[0:64, 1, :], wtmp[0:64, 0:384], gq[0:64, 1:2], None, ALU.mult, R=[wtmp, gq], W=[wuq])
            G = s.sb([128, D], F32, "G"); Sh = s.sb([128, D], F32, "Sh")
            xb = [s.sb([128, D], F32, "x") for _ in range(2)]
            u32 = s.sb([128, D], F32, "u32")
            ub = s.sb([128, D], BF16, "ub")
            uT = [s.sb([128, 8, 128], BF16, "uT") for _ in range(2)]
            P = s.sb([128, 2048], F32, "P")
            junk = s.sb([128, 512], F32, "junk")
            ss = s.sb([128, 4], F32, "ss")
            t1 = s.sb([128, 128], F32, "t1"); t2 = s.sb([128, 128], F32, "t2")
            cm = s.sb([128, 32], F32, "cm"); cw = s.sb([128, 64], F32, "cw")
            ckvn = s.sb([128, 128], BF16, "ckvn"); ckvT = s.sb([128, 128], BF16, "ckvT")
            kTn = s.sb([64, 4, 128], BF16, "kTn")
            vmb = s.sb([128, 256], BF16, "vmb")
            krb = s.sb([128, 32], BF16, "krb"); krT = s.sb([32, 128], BF16, "krT")
            cqn = s.sb([128, 192], BF16, "cqn"); cqT = s.sb([128, 2, 128], BF16, "cqT")
            q32 = s.sb([128, 384], F32, "q32"); qb = s.sb([128, 384], BF16, "qb"); qT = s.sb([96, 4, 128], BF16, "qT")
            ksb = s.sb([128, 128], BF16, "ksb"); ksT = s.sb([64, 2, 128], BF16, "ksT")
            vsb = s.sb([128, 128], BF16, "vsb")
            qsb = s.sb([128, 256], BF16, "qsb"); qsT = s.sb([64, 4, 128], BF16, "qsT")
            u5T = s.sb([128, 2, 128], F32, "u5T")
            ptb = [s.ps([128, 4, 128], BF16, "ptb") for _ in range(2)]
            ptf = s.ps([128, 2, 128], F32, "ptf")
            pp = [s.ps([128, 512], F32, "pp") for _ in range(2)]
            pm = [s.ps([128, 512], F32, "pm") for _ in range(2)]
            curcls = None
            for it, ti in enumerate(tiles):
                cls = 1 if ti < 2 else 0
                lat = cls == 0
                tok = slice(ti * 128, (ti + 1) * 128)
                if cls != curcls:
                    curcls = cls
                    self.bvec(s, Sh, self.MODV[l][cls, 3:4, :])
                    self.bvec(s, G, self.MODV[l][cls, 4:5, :])
                xt = xb[it % 2]
                k.dma("sp", xt[:], self.Xs[tok, :], writes=[xt])
                if lat:
                    k.dma("sp", cm[:], self.cst["c_ropem"][(ti - 2) * 128:(ti - 1) * 128, :], writes=[cm])
                    k.dma("sp", cw[:], self.cst["c_ropew"][(ti - 2) * 128:(ti - 1) * 128, :], writes=[cw])
                self.rstd_of(s, [(xt[:, 0:512], xt), (xt[:, 512:1024], xt)], D, junk, ss)
                k.stt(u32[:], xt[:], ss[:, 0:1], G[:], ALU.mult, ALU.mult, R=[xt, ss, G], W=[u32])
                k.tt(ub[:], u32[:], Sh[:], ALU.add, R=[u32, Sh], W=[ub])
                ut = uT[it % 2]
                for g in range(2):
                    pt = ptb[g]
                    for q in range(4):
                        kc = g * 4 + q
                        k.tr(pt[:, q, :], ub[:, kc * 128:(kc + 1) * 128], idb[:], R=[ub, idb], W=[pt])
                    k.cp(ut[:, g * 4:(g + 1) * 4, :], pt[:], R=[pt], W=[ut], eng="act")
                k.dma("sp", self.UT[:, :, tok].rearrange("kc p t -> p kc t"), ut[:], reads=[ut])
                for cb in range(4):
                    c0 = cb * 512; c1 = min(N_IN, c0 + 512)
                    p = pp[cb % 2]
                    for kc in range(8):
                        k.mm(p[:, 0:c1 - c0], ut[:, kc, :], win[kc][:, c0:c1], start=kc == 0, stop=kc == 7, R=[ut, win[kc]], W=[p])
                    k.cp(P[:, c0:c1], p[:, 0:c1 - c0], R=[p], W=[P], eng="act" if cb % 2 else "dve")
                self.rstd_of(s, [(P[:, 0:128], P)], 128, junk, ss)
                k.ts(ckvn[:], P[:, 0:128], ss[:, 0:1], None, ALU.mult, R=[P, ss], W=[ckvn])
                pt = ptb[0]
                k.tr(pt[:, 0, :], ckvn[:], idb[:], R=[ckvn, idb], W=[pt])
                k.cp(ckvT[:], pt[:, 0, :], R=[pt], W=[ckvT], eng="act")
                pk = pm[0]
                for h in range(4):
                    k.mm(pk[:, h * 128:(h + 1) * 128], wukv[:, h * 128:(h + 1) * 128], ckvT[:], R=[wukv, ckvT], W=[pk])
                k.cp(kTn[:], pk[0:64, :].rearrange("p (h t) -> p h t", h=4), R=[pk], W=[kTn], eng="act")
                k.dma("sp", self.KTmN[:, :, tok].rearrange("h d t -> d h t"), kTn[:], reads=[kTn])
                pv = pm[1]
                k.mm(pv[:], ckvT[:], wukv[:], R=[ckvT, wukv], W=[pv])
                k.cp(vmb[:].rearrange("p (h d) -> p h d", h=4), pv[:].rearrange("p (h d) -> p h d", h=4)[:, :, 64:128],
                     R=[pv], W=[vmb])
                k.dma("sp", self.Vm[tok, :], vmb[:], reads=[vmb])
                if lat:
                    kr32 = u32
                    self.rope_tm2(P, 128, kr32, 0, 1, 32, 16, cm, t1, t2)
                    k.cp(krb[:], kr32[:, 0:32], R=[kr32], W=[krb], eng="act")
                else:
                    k.cp(krb[:], P[:, 128:160], R=[P], W=[krb])
                pt = ptb[1]
                k.tr(pt[0:32, 0, :], krb[:], idb[:], R=[krb, idb], W=[pt])
                k.cp(krT[:], pt[0:32, 0, :], R=[pt], W=[krT], eng="act")
                k.dma("sp", self.KTmR[:, tok], krT[:], reads=[krT])
                if lat or ctx_q:
                    self.rstd_of(s, [(P[:, 672:864], P)], 192, junk, ss)
                    k.ts(cqn[:], P[:, 672:864], ss[:, 0:1], None, ALU.mult, R=[P, ss], W=[cqn])
                    pt = ptb[0]
                    k.tr(pt[:, 0, :], cqn[:, 0:128], idb[:], R=[cqn, idb], W=[pt])
                    k.tr(pt[0:64, 1, :], cqn[:, 128:192], idb[:], R=[cqn, idb], W=[pt])
                    k.cp(cqT[:, 0, :], pt[:, 0, :], R=[pt], W=[cqT], eng="act")
                    k.cp(cqT[0:64, 1, :], pt[0:64, 1, :], R=[pt], W=[cqT], eng="act")
                    pq = pm[0]
                    k.mm(pq[:, 0:384], cqT[:, 0, :], wuq[:, 0, :], start=True, stop=False, R=[cqT, wuq], W=[pq])
                    k.mm(pq[:, 0:384], cqT[0:64, 1, :], wuq[0:64, 1, :], start=False, stop=True, R=[cqT, wuq], W=[pq])
                    k.cp(q32[:], pq[:, 0:384], R=[pq], W=[q32], eng="act")
                    if lat:
                        k.cp(qb[:], q32[:], R=[q32], W=[qb])
                        q32r = u32
                        self.rope_tm2(q32, 0, q32r, 0, 4, 96, 16, cm, t1, t2)
                        k.cp(qb[:].rearrange("p (h d) -> p h d", h=4)[:, :, 64:96],
                             q32r[:, 0:384].rearrange("p (h d) -> p h d", h=4)[:, :, 64:96], R=[q32r], W=[qb])
                    else:
                        k.cp(qb[:], q32[:], R=[q32], W=[qb])
                    pt = ptb[1]
                    for h in range(4):
                        k.tr(pt[0:96, h, :], qb[:, h * 96:(h + 1) * 96], idb[:], R=[qb, idb], W=[pt])
                    k.cp(qT[:], pt[0:96, :, :], R=[pt], W=[qT], eng="act")
                    k.dma("sp", self.QTm[:, :, tok].rearrange("h d t -> d h t"), qT[:], reads=[qT])
                if lat:
                    k32 = u32
                    self.rope_tm2(P, 160, k32, 0, 2, 64, 32, cw, t1, t2)
                    k.cp(ksb[:], k32[:, 0:128], R=[k32], W=[ksb], eng="act")
                else:
                    k.cp(ksb[:], P[:, 160:288], R=[P], W=[ksb])
                pt = ptb[0]
                for h in range(2):
                    k.tr(pt[0:64, h, :], ksb[:, h * 64:(h + 1) * 64], idb[:], R=[ksb, idb], W=[pt])
                k.cp(ksT[:], pt[0:64, 0:2, :], R=[pt], W=[ksT], eng="act")
                k.dma("sp", self.KTs[:, :, tok].rearrange("h d t -> d h t"), ksT[:], reads=[ksT])
                k.cp(vsb[:], P[:, 288:416], R=[P], W=[vsb], eng="act")
                k.dma("sp", self.Vs[tok, :], vsb[:], reads=[vsb])
                if lat or ctx_q:
                    if lat:
                        qs32 = u32
                        self.rope_tm2(P, 864, qs32, 0, 4, 64, 32, cw, t1, t2)
                        k.cp(qsb[:], qs32[:, 0:256], R=[qs32], W=[qsb], eng="act")
                    else:
                        k.cp(qsb[:], P[:, 864:1120], R=[P], W=[qsb])
                    pt = ptb[1]
                    for h in range(4):
                        k.tr(pt[0:64, h, :], qsb[:, h * 64:(h + 1) * 64], idb[:], R=[qsb, idb], W=[pt])
                    k.cp(qsT[:], pt[0:64, :, :], R=[pt], W=[qsT], eng="act")
                    k.dma("sp", self.QTs[:, :, tok].rearrange("h d t -> d h t"), qsT[:], reads=[qsT])
                for c2 in range(2):
                    k.tr(ptf[:, c2, :], P[:, 416 + c2 * 128:416 + (c2 + 1) * 128], idf[:], R=[P, idf], W=[ptf])
                k.cp(u5T[:], ptf[:], R=[ptf], W=[u5T], eng="act")
                k.dma("sp", self.U5T[:, tok].rearrange("(c p) t -> p c t", p=128), u5T[:], reads=[u5T])
                if lat:
                    r0 = (ti - 2) * 128 + 1
                    k.dma("sp", self.HYP["l"][r0:r0 + 128, :], P[:, 1120:1888], reads=[P])
                else:
                    r0 = ti * 128 + 1
                    k.dma("sp", self.HYP["c"][r0:r0 + 128, :], P[:, 1120:1888], reads=[P])

    def rope_tm2(self, src, soff, dst, doff, nh, hd, half, cs, tmp1, tmp2):
        k = self.k
        r0 = hd - 2 * half
        v = lambda b, off, a: b[:, off:off + nh * hd].rearrange("p (h d) -> p h d", h=nh)[:, :, r0 + a:r0 + a + half]
        x1 = v(src, soff, 0); x2 = v(src, soff, half)
        o1 = v(dst, doff, 0); o2 = v(dst, doff, half)
        t1 = tmp1[:, 0:nh * half].rearrange("p (h d) -> p h d", h=nh)
        t2 = tmp2[:, 0:nh * half].rearrange("p (h d) -> p h d", h=nh)
        cb = cs[:, 0:half].unsqueeze(1).broadcast_to([128, nh, half])
        sb_ = cs[:, half:2 * half].unsqueeze(1).broadcast_to([128, nh, half])
        k.tt(t1, x1, cb, ALU.mult, R=[src, cs], W=[tmp1])
        k.tt(t2, x2, sb_, ALU.mult, R=[src, cs], W=[tmp2])
        k.tt(o1, t1, t2, ALU.subtract, R=[tmp1, tmp2], W=[dst])
        k.tt(t1, x1, sb_, ALU.mult, R=[src, cs], W=[tmp1])
        k.tt(t2, x2, cb, ALU.mult, R=[src, cs], W=[tmp2])
        k.tt(o2, t1, t2, ALU.add, R=[tmp1, tmp2], W=[dst])

    def stage_attn(self, l, kind, ctx_q):
        k = self.k
        with k.stage() as s:
            if kind == "mla":
                dq, scale, OT = 96, 96.0 ** -0.5, self.OTm
            else:
                dq, scale, OT = 64, 0.125, self.OTs
            if kind == "swa":
                mask = s.sb([128, 6, 512], BF16)
                k.dma("pool", mask[:], self.cst["c_mask"][:, :, :], writes=[mask])
                sk = s.sb([64, 4], F32)
                self.bvec(s, sk, self.w["swa_sink"][l:l + 1, :], rows=64)
                es = s.sb([64, 4], F32)
                k.act(es[:], sk[:], AF.Exp, R=[sk], W=[es])
            KT = [s.sb([dq, NTOK], BF16, "KT") for _ in range(2)]
            QT = [s.sb([dq, NTOK], BF16, "QT") for _ in range(2)]
            V = [s.sb([128, NT, 128], BF16, "V") for _ in range(2)]
            for v_ in V:
                k.memset(v_[:, :, 64:128], 1.0, W=[v_])
            NS = 4
            pS = [s.ps([128, 512], F32, "pS") for _ in range(NS)]
            pO = [s.ps([128, 512], F32, "pO") for _ in range(2)]
            PT = [s.sb([128, 512], BF16, "PT") for _ in range(NS)]
            rd = [s.sb([64, 512], F32, "rd") for _ in range(2)]
            on = [s.sb([64, 512], BF16, "on") for _ in range(2)]
            qblocks = []
            if ctx_q:
                qblocks.append((0, CTX, [(0, None), (1, None)]))
            for qb in range(SEQ // 512):
                if kind == "mla":
                    keys = [(t, None) for t in range(NT)]
                else:
                    keys = [(0, None), (1, None)]
                    n0 = qb * 4
                    for idx in range(n0 - 1, n0 + 5):
                        if 0 <= idx < SEQ // 128:
                            keys.append((2 + idx, idx - n0 + 1))
                qblocks.append((CTX + qb * 512, 512, keys))

            def load_head(h):
                kt, qt, v = KT[h % 2], QT[h % 2], V[h % 2]
                if kind == "mla":
                    k.dma("sp", kt[0:64, :], self.KTmN[h], writes=[kt])
                    k.dma("sp", kt[64:96, :], self.KTmR[:, :], writes=[kt])
                    k.dma("sp", qt[:], self.QTm[h], writes=[qt])
                    k.dma("sp", v[:, :, 0:64], self.Vm[:, h * 64:(h + 1) * 64].rearrange("(n p) d -> p n d", p=128), writes=[v])
                else:
                    kv = h // 2
                    k.dma("sp", kt[:], self.KTs[kv], writes=[kt])
                    k.dma("sp", qt[:], self.QTs[h], writes=[qt])
                    k.dma("sp", v[:, :, 0:64], self.Vs[:, kv * 64:(kv + 1) * 64].rearrange("(n p) d -> p n d", p=128), writes=[v])

            load_head(0)
            bi = 0
            cnt = 0
            LOOK = 2
            for h in range(4):
                if h + 1 < 4:
                    load_head(h + 1)
                kt_, qt_, v_ = KT[h % 2], QT[h % 2], V[h % 2]
                for (q0, nq, keys) in qblocks:
                    po = pO[bi % 2]
                    nk = len(keys)
                    slots = {}
                    for step in range(nk + LOOK):
                        if step < nk:
                            ktile, rel = keys[step]
                            sl_ = cnt % NS
                            cnt += 1
                            slots[step] = sl_
                            pst = pS[sl_]; pt = PT[sl_]
                            k.mm(pst[:, 0:nq], kt_[:, ktile * 128:(ktile + 1) * 128], qt_[:, q0:q0 + nq], R=[kt_, qt_], W=[pst])
                            k.act(pt[:, 0:nq], pst[:, 0:nq], AF.Exp, R=[pst], W=[pt], scale=scale)
                            if rel is not None:
                                k.tt(pt[:, 0:nq], pt[:, 0:nq], mask[:, rel, 0:nq], ALU.mult, R=[pt, mask], W=[pt])
                        j = step - LOOK
                        if j >= 0:
                            ktile, rel = keys[j]
                            pt = PT[slots[j]]
                            k.mm(po[:, 0:nq], v_[:, ktile, :], pt[:, 0:nq], start=j == 0, stop=j == nk - 1, R=[v_, pt], W=[po])
                    r = rd[bi % 2]; o = on[bi % 2]
                    if kind == "swa":
                        k.ts(r[:, 0:nq], po[64:128, 0:nq], es[:, h:h + 1], None, ALU.add, R=[po, es], W=[r])
                    else:
                        k.cp(r[:, 0:nq], po[64:128, 0:nq], R=[po], W=[r])
                    k.recip(r[:, 0:nq], r[:, 0:nq], R=[r], W=[r])
                    k.tt(o[:, 0:nq], po[0:64, 0:nq], r[:, 0:nq], ALU.mult, R=[po, r], W=[o])
                    k.dma("sp", OT[h * 64:(h + 1) * 64, q0:q0 + nq], o[:, 0:nq], reads=[o])
                    bi += 1

    def stage_shortconv(self, l, nm, L):
        k = self.k
        with k.stage() as s:
            w = [s.sb([128, 768], F32, "scw") for _ in range(3)]
            b = s.sb([128, 768], F32, "scb")
            for j in range(3):
                self.bvec(s, w[j], self.w["hy_conv_w"][l, j:j + 1, :])
            self.bvec(s, b, self.w["hy_conv_b"][l:l + 1, :])
            xs = [[s.sb([128, 768], F32, "scx") for _ in range(3)] for _ in range(2)]
            acc = [s.sb([128, 768], F32, "acc") for _ in range(2)]
            tmp = s.sb([128, 768], F32, "tmp")
            for ti in range(L // 128):
                x = xs[ti % 2]; a = acc[ti % 2]
                for j in range(3):
                    k.dma("sp", x[j][:], self.HYP[nm][ti * 128 + j:ti * 128 + j + 128, :], writes=[x[j]])
                k.tt(a[:], x[0][:], w[0][:], ALU.mult, R=[x[0], w[0]], W=[a])
                k.tt(tmp[:], x[1][:], w[1][:], ALU.mult, R=[x[1], w[1]], W=[tmp])
                k.tt(a[:], a[:], tmp[:], ALU.add, R=[a, tmp], W=[a])
                k.tt(tmp[:], x[2][:], w[2][:], ALU.mult, R=[x[2], w[2]], W=[tmp])
                k.tt(a[:], a[:], tmp[:], ALU.add, R=[a, tmp], W=[a])
                k.tt(a[:], a[:], b[:], ALU.add, R=[a, b], W=[a])
                k.dma("sp", self.SC[nm][ti * 128:(ti + 1) * 128, :], a[:], reads=[a])

    def stage_filter(self, l, nm, L):
        k = self.k
        N = 2 * L
        with k.stage() as s:
            W = self.w
            w1 = s.sb([33, 64], F32); w2 = s.sb([64, 64], F32); w3 = s.sb([64, 1024], F32)
            k.dma("sp", w1[:], W["hy_f_w1"][l], writes=[w1])
            k.dma("sp", w2[:], W["hy_f_w2"][l], writes=[w2])
            k.dma("sp", w3[:], W["hy_f_w3"][l], writes=[w3])
            bf = s.sb([64, 4], F32)
            k.dma("sp", bf[:, 0:1], W["hy_f_b1"][l].rearrange("(p o) -> p o", o=1), writes=[bf])
            k.dma("sp", bf[:, 1:2], W["hy_f_b2"][l].rearrange("(p o) -> p o", o=1), writes=[bf])
            k.dma("sp", bf[:, 2:4], W["hy_f_freq"][l].rearrange("j p -> p j"), writes=[bf])
            zt = s.sb([33, N], F32, "zt")
            a1 = s.sb([64, N], F32, "a1"); h1 = s.sb([64, N], F32, "h1")
            tf = s.sb([64, N], F32, "tf"); tiq = s.sb([64, N], I32, "tiq")
            dec = [s.sb([128, 256], F32, "dec") for _ in range(3)]
            kt = [s.sb([128, 2, 256], F32, "kt") for _ in range(3)]
            pq = [s.ps([64, 512], F32, "pq") for _ in range(2)]
            p3 = [[s.ps([128, 512], F32, "p3") for _ in range(2)] for _ in range(2)]
            zc = self.cst["c_zt" + nm]; dc = self.cst["c_dec" + nm]
            k.dma("sp", zt[:], zc[:, :], writes=[zt])
            nb = N // 512
            for layer_i in range(2):
                wm = w1 if layer_i == 0 else w2
                src = zt if layer_i == 0 else h1
                for blk in range(nb):
                    p = pq[blk % 2]
                    k.mm(p[:], wm[:], src[:, blk * 512:(blk + 1) * 512], R=[wm, src], W=[p])
                    k.ts(a1[:, blk * 512:(blk + 1) * 512], p[:], bf[:, layer_i:layer_i + 1], bf[:, 2 + layer_i:3 + layer_i],
                         ALU.add, ALU.mult, R=[p, bf], W=[a1])
                self.range_reduce(s, a1, None, tf, tiq)
                k.act(h1[:], a1[:], AF.Sin, R=[a1], W=[h1])
            cnt = 0
            for n0 in range(0, N, 128):
                dirn = 0 if n0 < L else 1
                dt_ = dec[cnt % 3]; ko = kt[cnt % 3]; pp = p3[cnt % 2]
                k.dma("sp", dt_[:], dc[n0:n0 + 128, :], writes=[dt_])
                for o in range(2):
                    k.mm(pp[o][:], h1[:, n0:n0 + 128], w3[:, o * 512:(o + 1) * 512], R=[h1, w3], W=[pp[o]])
                    k.tt(ko[:, o, :], pp[o][:, dirn * 256:(dirn + 1) * 256], dt_[:], ALU.mult, R=[pp[o], dt_], W=[ko])
                k.dma("sp", self.KFULL[nm][:, n0:n0 + 128, :].rearrange("o n c -> n o c"), ko[:], reads=[ko])
                cnt += 1

    def load_fft_tabs(self, s, nm, N1):
        k = self.k
        w1f = s.sb([N1, 2 * N1], F32)
        k.dma("sp", w1f[:], self.cst["c_w1f" + nm][:, :], writes=[w1f])
        w1i = s.sb([2 * N1, 2 * N1], BF16)
        k.dma("pool", w1i[:], self.cst["c_w1i" + nm][:, :], writes=[w1i])
        w1fb = s.sb([N1, 2 * N1], BF16)
        k.dma("pool", w1fb[:], self.cst["c_w1f" + nm][:, :], writes=[w1fb])
        mt = s.sb([128, N1, 3, 128], BF16)
        k.dma("pool", mt[:], self.cst["c_mt" + nm][:, :, :, :], writes=[mt])
        self.tabs = dict(w1f=w1f, w1i=w1i, mt=mt, w1fb=w1fb)

    def stage_fft_s1(self, nm, N1, pieces, tabname, dst):
        k = self.k
        with k.stage() as s:
            tab = self.tabs[tabname]
            idt = BF16 if tabname == "w1i" else F32
            Kt = sum(p.shape[0] for p in pieces)
            M = 2 * N1
            xin = [s.sb([Kt, 8, 256], idt, "xin") for _ in range(2)]
            xo = [s.sb([M, 2048], BF16, "xo") for _ in range(2)]
            pp = [s.ps([M, 512], F32, "pp") for _ in range(4)]
            for ch in range(16):
                xi = xin[ch % 2]
                r0 = 0
                for p in pieces:
                    k.dma("sp", xi[r0:r0 + p.shape[0], :, :], p[:, ch * 8:(ch + 1) * 8, :], writes=[xi])
                    r0 += p.shape[0]
                o = xo[ch % 2]
                xf = xi[:, :, :].rearrange("p a c -> p (a c)")
                for q in range(4):
                    k.mm(pp[q][:], tab[0:Kt, :], xf[:, q * 512:(q + 1) * 512], R=[tab, xi], W=[pp[q]])
                    k.cp(o[:, q * 512:(q + 1) * 512], pp[q][:], R=[pp[q]], W=[o], eng="act" if q % 2 else "dve")
                k.dma("sp", dst[0:M, ch * 2048:(ch + 1) * 2048], o[:], reads=[o])

    def stage_fft_s2(self, nm, N1, src, mode, **kw):
        k = self.k
        N = 128 * N1
        FC = min(8, N1)
        with k.stage() as s:
            MT = self.tabs["mt"]
            ab = [s.sb([128, FC, 2, 256], BF16, "ab") for _ in range(2)]
            pX = [s.ps([128, 512], F32, "pX") for _ in range(4)]
            if mode in ("filter", "conv"):
                XS = [s.sb([128, FC, 512], F32, "XS") for _ in range(2)]
            if mode == "conv":
                KFt = [s.sb([128, FC, 512], F32, "KFt") for _ in range(2)]
                YS = [s.sb([128, FC, 512], BF16, "YS") for _ in range(2)]
                ta = s.sb([128, FC, 256], F32, "ta"); tb = s.sb([128, FC, 256], F32, "tb")
            if mode == "inv":
                zt = [s.sb([64, FC, 256], F32, "zt") for _ in range(2)]
                gt = [s.sb([64, FC, 256], F32, "gt") for _ in range(2)]
                yv = [s.sb([64, FC, 256], F32, "yv") for _ in range(2)]
                bz = s.sb([64, 256], F32, "bz")
                self.bvec(s, bz, kw["bias"], rows=64)
                zsrc = kw["z"].rearrange("(t2 t1) c -> t2 t1 c", t1=N1)
                gsrc = kw["gate"].rearrange("(t2 t1) c -> t2 t1 c", t1=N1)
                dstv = kw["dst"].rearrange("(t2 t1) c -> t2 t1 c", t1=N1)
            for ci in range(N1 // FC):
                f0 = ci * FC
                a = ab[ci % 2]
                for r in range(2):
                    k.dma("sp", a[:, :, r, :], src[r * N1 + f0:r * N1 + f0 + FC, :].rearrange("f (tl c) -> tl f c", c=256),
                          writes=[a])
                if mode == "conv":
                    kf = KFt[ci % 2]
                    k.dma("sp", kf[:], kw["kf"][:, f0:f0 + FC, :], writes=[kf])
                if mode == "inv":
                    z = zt[ci % 2]; g = gt[ci % 2]
                    k.dma("sp", z[:], zsrc[:, f0:f0 + FC, :], writes=[z])
                    k.dma("sp", g[:], gsrc[:, f0:f0 + FC, :], writes=[g])
                for fi in range(FC):
                    f1 = f0 + fi
                    p = pX[fi % 4]
                    if mode == "inv":
                        k.mm(p[0:64, 0:256], MT[:, f1, 0, 0:64], a[:, fi, 0, :], start=True, stop=False, R=[MT, a], W=[p])
                        k.mm(p[0:64, 0:256], MT[:, f1, 1, 0:64], a[:, fi, 1, :], start=False, stop=True, R=[MT, a], W=[p])
                        k.act(yv[ci % 2][:, fi, :], p[0:64, 0:256], AF.Copy, R=[p], W=[yv[ci % 2]], scale=1.0 / N)
                    else:
                        k.mm(p[:, 0:256], MT[:, f1, 0, :], a[:, fi, 0, :], start=True, stop=False, R=[MT, a], W=[p])
                        k.mm(p[:, 0:256], MT[:, f1, 2, :], a[:, fi, 1, :], start=False, stop=True, R=[MT, a], W=[p])
                        k.mm(p[:, 256:512], MT[:, f1, 1, :], a[:, fi, 0, :], start=True, stop=False, R=[MT, a], W=[p])
                        k.mm(p[:, 256:512], MT[:, f1, 0, :], a[:, fi, 1, :], start=False, stop=True, R=[MT, a], W=[p])
                        k.cp(XS[ci % 2][:, fi, :], p[:], R=[p], W=[XS[ci % 2]], eng="act")
                if mode == "filter":
                    k.dma("sp", kw["dst"][:, f0:f0 + FC, :], XS[ci % 2][:], reads=[XS[ci % 2]])
                elif mode == "conv":
                    X = XS[ci % 2]; Yt = YS[ci % 2]
                    Xr, Xi = X[:, :, 0:256], X[:, :, 256:512]
                    Kr, Ki = kf[:, :, 0:256], kf[:, :, 256:512]
                    k.tt(ta[:], Xr, Kr, ALU.mult, R=[X, kf], W=[ta])
                    k.tt(tb[:], Xi, Ki, ALU.mult, R=[X, kf], W=[tb])
                    k.tt(Yt[:, :, 0:256], ta[:], tb[:], ALU.subtract, R=[ta, tb], W=[Yt])
                    k.tt(ta[:], Xr, Ki, ALU.mult, R=[X, kf], W=[ta])
                    k.tt(tb[:], Xi, Kr, ALU.mult, R=[X, kf], W=[tb])
                    k.tt(Yt[:, :, 256:512], ta[:], tb[:], ALU.add, R=[ta, tb], W=[Yt])
                    k.dma("sp", self.Y[0:N, :].rearrange("(f2 f1) x -> f2 f1 x", f1=N1)[:, f0:f0 + FC, :], Yt[:], reads=[Yt])
                else:
                    y = yv[ci % 2]
                    k.tt(z[:], z[:], bz[:].unsqueeze(1).broadcast_to([64, FC, 256]), ALU.mult, R=[z, bz], W=[z])
                    k.tt(y[:], y[:], z[:], ALU.add, R=[y, z], W=[y])
                    k.tt(y[:], y[:], g[:], ALU.mult, R=[y, g], W=[y])
                    k.dma("sp", dstv[:, f0:f0 + FC, :], y[:], reads=[y])

    def stage_hconv(self, l, mode, src3, K, o, **kw):
        k = self.k
        N1, N, CGW = 64, 8192, 64
        NG = 256 // CGW
        with k.stage() as s:
            T = self.tabs
            MT = T["mt"]; w1f = T["w1fb"]; idb = T["idb"]; w1ib = T["w1ib"]
            zins = [s.sb([K, 64, CGW], BF16, "zin") for _ in range(2)]
            A_sb = s.sb([128, 128 * CGW], BF16, "A_sb")
            AT = s.sb([128, CGW, 128], BF16, "AT")
            XS = [s.sb([128, 16, 128], F32, "XS") for _ in range(2)]
            pA = [s.ps([128, 512], F32, "pA") for _ in range(2)]
            pT = [s.ps([128, 4, 128], BF16, "pT") for _ in range(2)]
            pX = [s.ps([128, 4, 128], F32, "pX") for _ in range(2)]
            if mode == "conv":
                B_sb = s.sb([128, 128 * CGW], BF16, "B_sb")
                BT = s.sb([128, CGW, 128], BF16, "BT")
                KFq = [s.sb([128, 16, 128], F32, "KFq") for _ in range(2)]
                ta = s.sb([128, 16, 64], F32, "ta"); tb = s.sb([128, 16, 64], F32, "tb")
                YS = s.sb([128, 64, 128], BF16, "YS")
                yv = s.sb([64, 16, CGW], F32, "yv")
                zt = s.sb([64, 16, CGW], F32, "zt")
                gt = s.sb([64, 16, CGW], F32, "gt")
                pY = [s.ps([64, 8, CGW], F32, "pY") for _ in range(2)]
                bz = s.sb([64, 256], F32, "bz")
                self.bvec(s, bz, kw["bias"], rows=64)
                zsrc = kw["z"].rearrange("(t2 t1) c -> t2 t1 c", t1=N1)
                gsrc = kw["gate"].rearrange("(t2 t1) c -> t2 t1 c", t1=N1)
                dstv = kw["dst"].rearrange("(t2 t1) c -> t2 t1 c", t1=N1)
            KFo = self.KF["l"][o]
            st = dict(cnt=0)

            def step_AB(cg):
                cs = slice(cg * CGW, (cg + 1) * CGW)
                for hf in range(2):
                    zin = zins[hf]
                    k.dma("pool", zin[:], src3[:, hf * 64:(hf + 1) * 64, cs], writes=[zin])
                    zf = zin[:, :, :].rearrange("p t c -> p (t c)")
                    for q in range(8):
                        p = pA[q % 2]
                        k.mm(p[:], w1f[0:K, :], zf[:, q * 512:(q + 1) * 512], R=[w1f, zin], W=[p])
                        c0 = hf * 4096 + q * 512
                        k.cp(A_sb[:, c0:c0 + 512], p[:], R=[p], W=[A_sb], eng="act" if q % 2 else "dve")
                Av = A_sb[:, :].rearrange("p (t c) -> p t c", c=CGW)
                for c4 in range(CGW // 4):
                    pt = pT[c4 % 2]
                    for q in range(4):
                        k.tr(pt[:, q, :], Av[:, :, c4 * 4 + q], idb[:], R=[A_sb, idb], W=[pt])
                    k.cp(AT[:, c4 * 4:(c4 + 1) * 4, :], pt[:], R=[pt], W=[AT], eng="act" if c4 % 2 else "dve")

            def step_CD(cg):
                for fq in range(4):
                    X = XS[st["cnt"] % 2]
                    if mode == "conv":
                        Kq = KFq[st["cnt"] % 2]
                        k.dma("sp", Kq[:, :, 0:64], KFo[:, fq * 16:(fq + 1) * 16, cg * CGW:(cg + 1) * CGW], writes=[Kq])
                        k.dma("sp", Kq[:, :, 64:128], KFo[:, fq * 16:(fq + 1) * 16, 256 + cg * CGW:256 + (cg + 1) * CGW], writes=[Kq])
                    for f4 in range(4):
                        p = pX[f4 % 2]
                        for q in range(4):
                            f1 = fq * 16 + f4 * 4 + q
                            Are = AT[:, :, f1]; Aim = AT[:, :, 64 + f1]
                            k.mm(p[:, q, 0:64], MT[:, f1, 0, :], Are, start=True, stop=False, R=[MT, AT], W=[p])
                            k.mm(p[:, q, 0:64], MT[:, f1, 2, :], Aim, start=False, stop=True, R=[MT, AT], W=[p])
                            k.mm(p[:, q, 64:128], MT[:, f1, 1, :], Are, start=True, stop=False, R=[MT, AT], W=[p])
                            k.mm(p[:, q, 64:128], MT[:, f1, 0, :], Aim, start=False, stop=True, R=[MT, AT], W=[p])
                        k.cp(X[:, f4 * 4:(f4 + 1) * 4, :], p[:], R=[p], W=[X], eng="act")
                    if mode == "filter":
                        k.dma("sp", KFo[:, fq * 16:(fq + 1) * 16, cg * CGW:(cg + 1) * CGW], X[:, :, 0:64], reads=[X])
                        k.dma("sp", KFo[:, fq * 16:(fq + 1) * 16, 256 + cg * CGW:256 + (cg + 1) * CGW], X[:, :, 64:128], reads=[X])
                    else:
                        Xr, Xi = X[:, :, 0:64], X[:, :, 64:128]
                        Kr, Ki = Kq[:, :, 0:64], Kq[:, :, 64:128]
                        Yq = YS[:, fq * 16:(fq + 1) * 16, :]
                        k.tt(ta[:], Xr, Kr, ALU.mult, R=[X, Kq], W=[ta])
                        k.tt(tb[:], Xi, Ki, ALU.mult, R=[X, Kq], W=[tb])
                        k.tt(Yq[:, :, 0:64], ta[:], tb[:], ALU.subtract, R=[ta, tb], W=[YS])
                        k.tt(ta[:], Xr, Ki, ALU.mult, R=[X, Kq], W=[ta])
                        k.tt(tb[:], Xi, Kr, ALU.mult, R=[X, Kq], W=[tb])
                        k.tt(Yq[:, :, 64:128], ta[:], tb[:], ALU.add, R=[ta, tb], W=[YS])
                    st["cnt"] += 1

            def step_EFGH(cg):
                cs = slice(cg * CGW, (cg + 1) * CGW)
                for b in range(2):
                    for g8 in range(8):
                        p = pA[g8 % 2]
                        k.mm(p[:], w1ib[:, b, 0, :], YS[:, g8 * 8:(g8 + 1) * 8, 0:64], start=True, stop=False, R=[w1ib, YS], W=[p])
                        k.mm(p[:], w1ib[:, b, 1, :], YS[:, g8 * 8:(g8 + 1) * 8, 64:128], start=False, stop=True, R=[w1ib, YS], W=[p])
                        c0 = b * 4096 + g8 * 512
                        k.cp(B_sb[:, c0:c0 + 512], p[:], R=[p], W=[B_sb], eng="act" if g8 % 2 else "dve")
                Bv = B_sb[:, :].rearrange("p (t c) -> p t c", c=CGW)
                for c4 in range(CGW // 4):
                    pt = pT[c4 % 2]
                    for q in range(4):
                        k.tr(pt[:, q, :], Bv[:, :, c4 * 4 + q], idb[:], R=[B_sb, idb], W=[pt])
                    k.cp(BT[:, c4 * 4:(c4 + 1) * 4, :], pt[:], R=[pt], W=[BT], eng="act" if c4 % 2 else "dve")
                for tq in range(4):
                    k.dma("sp", zt[:], zsrc[:, tq * 16:(tq + 1) * 16, cs], writes=[zt])
                    k.dma("sp", gt[:], gsrc[:, tq * 16:(tq + 1) * 16, cs], writes=[gt])
                    for t8 in range(2):
                        p = pY[t8 % 2]
                        for q in range(8):
                            t1 = tq * 16 + t8 * 8 + q
                            k.mm(p[:, q, :], MT[:, t1, 0, 0:64], BT[:, :, t1], start=True, stop=False, R=[MT, BT], W=[p])
                            k.mm(p[:, q, :], MT[:, t1, 1, 0:64], BT[:, :, 64 + t1], start=False, stop=True, R=[MT, BT], W=[p])
                        k.act(yv[:, t8 * 8:(t8 + 1) * 8, :], p[:], AF.Copy, R=[p], W=[yv], scale=1.0 / N)
                    k.tt(zt[:], zt[:], bz[:, cs].unsqueeze(1).broadcast_to([64, 16, CGW]), ALU.mult, R=[zt, bz], W=[zt])
                    k.tt(yv[:], yv[:], zt[:], ALU.add, R=[yv, zt], W=[yv])
                    k.tt(yv[:], yv[:], gt[:], ALU.mult, R=[yv, gt], W=[yv])
                    k.dma("sp", dstv[:, tq * 16:(tq + 1) * 16, cs], yv[:], reads=[yv])

            if mode == "filter":
                for cg in range(NG):
                    step_AB(cg)
                    step_CD(cg)
            else:
                step_AB(0)
                for cg in range(NG):
                    step_CD(cg)
                    if cg + 1 < NG:
                        step_AB(cg + 1)
                    step_EFGH(cg)

    def stage_hconv_ctx(self, l, ohy_rows):
        k = self.k
        N = 512
        with k.stage() as s:
            DF = s.sb([128, 4, 2, 512], BF16, "DF")
            k.dma("pool", DF[:], self.cst["c_dft512"].rearrange("(nc p) r f -> p nc r f", p=128), writes=[DF])
            kin = s.sb([128, 4, 256], BF16, "kin")
            KFc = [s.sb([128, 4, 512], F32, "KFc") for _ in range(2)]
            sc = s.sb([128, 2, 768], F32, "sc")
            k.dma("sp", sc[:], self.SC["c"].rearrange("(tc p) c -> p tc c", p=128), writes=[sc])
            zb = s.sb([128, 2, 256], BF16, "zb")
            z32 = s.sb([128, 2, 256], F32, "z32")
            XS = s.sb([128, 4, 512], F32, "XS")
            YS = s.sb([128, 4, 512], BF16, "YS")
            ta = s.sb([128, 4, 256], F32, "ta"); tb = s.sb([128, 4, 256], F32, "tb")
            yv = s.sb([128, 2, 256], F32, "yv")
            b0 = s.sb([128, 256], F32, "b0"); b1 = s.sb([128, 256], F32, "b1")
            self.bvec(s, b0, self.w["hy_bias"][l, 0:1, :])
            self.bvec(s, b1, self.w["hy_bias"][l, 1:2, :])
            bzs = [b0, b1]
            pX = [s.ps([128, 512], F32, "pX") for _ in range(4)]
            pYo = [s.ps([128, 256], F32, "pYo") for _ in range(2)]

            def fwd(src, nchunks, dst_copy):
                for fc in range(4):
                    p = pX[fc]
                    for r in range(2):
                        for nc_ in range(nchunks):
                            k.mm(p[:, r * 256:(r + 1) * 256], DF[:, nc_, r, fc * 128:(fc + 1) * 128], src[:, nc_, :],
                                 start=nc_ == 0, stop=nc_ == nchunks - 1, R=[DF, src], W=[p])
                    k.cp(dst_copy[:, fc, :], p[:], R=[p], W=[dst_copy], eng="act")

            for o in range(2):
                k.dma("pool", kin[:], self.KFULL["c"][o].rearrange("(nc p) c -> p nc c", p=128), writes=[kin])
                fwd(kin, 4, KFc[o])
            for o in range(2):
                if o == 0:
                    k.cp(z32[:], sc[:, :, 0:256], R=[sc], W=[z32])
                k.cp(zb[:], z32[:], R=[z32], W=[zb])
                fwd(zb, 2, XS)
                Kf = KFc[o]
                Xr, Xi = XS[:, :, 0:256], XS[:, :, 256:512]
                Kr, Ki = Kf[:, :, 0:256], Kf[:, :, 256:512]
                k.tt(ta[:], Xr, Kr, ALU.mult, R=[XS, Kf], W=[ta])
                k.tt(tb[:], Xi, Ki, ALU.mult, R=[XS, Kf], W=[tb])
                k.tt(YS[:, :, 0:256], ta[:], tb[:], ALU.subtract, R=[ta, tb], W=[YS])
                k.tt(ta[:], Xr, Ki, ALU.mult, R=[XS, Kf], W=[ta])
                k.tt(tb[:], Xi, Kr, ALU.mult, R=[XS, Kf], W=[tb])
                k.tt(YS[:, :, 256:512], ta[:], tb[:], ALU.add, R=[ta, tb], W=[YS])
                for tc in range(2):
                    p = pYo[tc]
                    for fc in range(4):
                        k.mm(p[:], DF[:, fc, 0, tc * 128:(tc + 1) * 128], YS[:, fc, 0:256], start=fc == 0, stop=False, R=[DF, YS], W=[p])
                        k.mm(p[:], DF[:, fc, 1, tc * 128:(tc + 1) * 128], YS[:, fc, 256:512], start=False, stop=fc == 3, R=[DF, YS], W=[p])
                    k.act(yv[:, tc, :], p[:], AF.Copy, R=[p], W=[yv], scale=1.0 / N)
                    k.tt(ta[:, tc, :], z32[:, tc, :], bzs[o][:], ALU.mult, R=[z32, bzs[o]], W=[ta])
                    k.tt(yv[:, tc, :], yv[:, tc, :], ta[:, tc, :], ALU.add, R=[yv, ta], W=[yv])
                    k.tt(yv[:, tc, :], yv[:, tc, :], sc[:, tc, 256 * (o + 1):256 * (o + 2)], ALU.mult, R=[yv, sc], W=[yv])
                if o == 0:
                    k.cp(z32[:], yv[:], R=[yv], W=[z32])
                else:
                    k.dma("sp", ohy_rows.rearrange("(tc p) c -> p tc c", p=128), yv[:], reads=[yv])

    def hyena_seq(self, l, nm, L, ohy_rows):
        N1 = 2 * L // 128
        ns = self.k.nc.named_scope
        with ns("hsc_%s%d" % (nm, l)):
            self.stage_shortconv(l, nm, L)
        with ns("hflt_%s%d" % (nm, l)):
            self.stage_filter(l, nm, L)
        v3 = lambda ap: ap.rearrange("(th tl) c -> th tl c", tl=128)
        if nm == "c":
            with ns("hcc_%d" % l):
                self.stage_hconv_ctx(l, ohy_rows)
            return
        with self.k.stage() as outer:
            self.load_fft_tabs(outer, nm, N1)
            if nm == "l":
                k = self.k
                idf, idb = self.load_ident(outer)
                w1ib = outer.sb([128, 2, 2, 128], BF16)
                k.dma("pool", w1ib[:], self.cst["c_w1ib"][:, :, :, :], writes=[w1ib])
                self.tabs["idb"] = idb
                self.tabs["w1ib"] = w1ib
                for o in range(2):
                    with ns("hff%d_%d" % (o, l)):
                        self.stage_hconv(l, "filter", v3(self.KFULL[nm][o]), 64, o)
                for o in range(2):
                    zin = self.SC[nm][:, 0:256] if o == 0 else self.Z1[nm][:, :]
                    gate = self.SC[nm][:, 256 * (o + 1):256 * (o + 2)]
                    dst = self.Z1[nm][:, :] if o == 0 else ohy_rows
                    with ns("hcv%d_%d" % (o, l)):
                        self.stage_hconv(l, "conv", v3(zin), 32, o, bias=self.w["hy_bias"][l, o:o + 1, :], z=zin, gate=gate, dst=dst)
                return
            for o in range(2):
                self.stage_fft_s1(nm, N1, [v3(self.KFULL[nm][o])], "w1f", self.A1)
                self.stage_fft_s2(nm, N1, self.A1, "filter", dst=self.KF[nm][o])
            for o in range(2):
                zin = self.SC[nm][:, 0:256] if o == 0 else self.Z1[nm][:, :]
                gate = self.SC[nm][:, 256 * (o + 1):256 * (o + 2)]
                dst = self.Z1[nm][:, :] if o == 0 else ohy_rows
                self.stage_fft_s1(nm, N1, [v3(zin)], "w1f", self.A1)
                self.stage_fft_s2(nm, N1, self.A1, "conv", kf=self.KF[nm][o])
                yv = self.Y[0:128 * N1, :].rearrange("(fh fl) (r c) -> r fh fl c", fl=128, r=2)
                self.stage_fft_s1(nm, N1, [yv[0], yv[1]], "w1i", self.B1)
                self.stage_fft_s2(nm, N1, self.B1, "inv", bias=self.w["hy_bias"][l, o:o + 1, :], z=zin, gate=gate, dst=dst)

    def stage_s5(self, l, ctx_out):
        k = self.k
        W = self.w
        PI2 = math.pi / 2.0
        with k.stage() as s:
            for (wn, wl) in self.precast_plan.get(l, []):
                src = W[wn][wl]
                dst = self.wbf[(wn, wl)]
                rows = src.shape[0]
                for r0 in range(0, rows, 256):
                    k.dma("pool", dst[r0:r0 + 256, :], src[r0:r0 + 256, :])
            idf, idb = self.load_ident(s)
            NLV = 12
            prm = {}
            pers = {}
            for d in range(2):
                pers[d] = dict(r=s.sb([128, 8], F32), CK=s.sb([128, 8, NLV + 1], F32), SK=s.sb([128, 8, NLV + 1], F32),
                               FBT=[s.sb([32, 8, 128], F32) for _ in range(2)], CX=[s.sb([128, 8, 32], F32) for _ in range(2)])
            s_outer = s
            s2cm = k.stage()
            s = s2cm.__enter__()
            tq = s.sb([128, 8], F32); tq2 = s.sb([128, 8], F32); tqi = s.sb([128, 8], I32)
            t3 = s.sb([128, 8, 16], F32); t4 = s.sb([128, 8, 16], F32)
            ptf = s.ps([32, 128], F32, "ptf")
            ptc = s.ps([128, 32], F32, "ptc")
            for d in range(2):
                are = s.sb([128, 8], F32); aim = s.sb([128, 8], F32); ldt = s.sb([128, 8], F32)
                k.dma("sp", are[:], W["s5_a_re"][l, d].rearrange("(j two) p -> (two p) j", two=2), writes=[are])
                k.dma("sp", aim[:], W["s5_a_im"][l, d].rearrange("(j two) p -> (two p) j", two=2), writes=[aim])
                ldv = W["s5_log_dt"][l, d:d + 1, :].rearrange("o (j two) -> o two j", two=2)
                for two in range(2):
                    k.dma("sp", ldt[two * 64:(two + 1) * 64, :], ldv[:, two, :].partition_broadcast(64), writes=[ldt])
                dt = s.sb([128, 8], F32); r = pers[d]['r']; th = s.sb([128, 8], F32)
                k.act(dt[:], ldt[:], AF.Exp, R=[ldt], W=[dt])
                k.tt(tq[:], dt[:], are[:], ALU.mult, R=[dt, are], W=[tq])
                k.act(r[:], tq[:], AF.Exp, R=[tq], W=[r])
                k.tt(th[:], dt[:], aim[:], ALU.mult, R=[dt, aim], W=[th])
                CK = pers[d]['CK']; SK = pers[d]['SK']
                k.cp(tq[:], th[:], R=[th], W=[tq])
                self.range_reduce(s, tq, None, tq2, tqi)
                k.act(SK[:, :, 0], tq[:], AF.Sin, R=[tq], W=[SK])
                k.ts(tq[:], th[:], PI2, None, ALU.add, R=[th], W=[tq])
                self.range_reduce(s, tq, None, tq2, tqi)
                k.act(CK[:, :, 0], tq[:], AF.Sin, R=[tq], W=[CK])
                for lv in range(NLV):
                    k.tt(tq[:], CK[:, :, lv], CK[:, :, lv], ALU.mult, R=[CK], W=[tq])
                    k.tt(tq2[:], SK[:, :, lv], SK[:, :, lv], ALU.mult, R=[SK], W=[tq2])
                    k.tt(CK[:, :, lv + 1], tq[:], tq2[:], ALU.subtract, R=[tq, tq2], W=[CK])
                    k.tt(tq[:], CK[:, :, lv], SK[:, :, lv], ALU.mult, R=[CK, SK], W=[tq])
                    k.ts(SK[:, :, lv + 1], tq[:], 2.0, None, ALU.mult, R=[tq], W=[SK])
                abre = s.sb([128, 8], F32); abim = s.sb([128, 8], F32)
                k.tt(abre[:], r[:], CK[:, :, 0], ALU.mult, R=[r, CK], W=[abre])
                k.ts(abre[:], abre[:], -1.0, None, ALU.add, R=[abre], W=[abre])
                k.tt(abim[:], r[:], SK[:, :, 0], ALU.mult, R=[r, SK], W=[abim])
                den = s.sb([128, 8], F32)
                k.tt(den[:], are[:], are[:], ALU.mult, R=[are], W=[den])
                k.tt(tq[:], aim[:], aim[:], ALU.mult, R=[aim], W=[tq])
                k.tt(den[:], den[:], tq[:], ALU.add, R=[den, tq], W=[den])
                k.recip(den[:], den[:], R=[den], W=[den])
                fre = s.sb([128, 8], F32); fim = s.sb([128, 8], F32)
                k.tt(fre[:], abre[:], are[:], ALU.mult, R=[abre, are], W=[fre])
                k.tt(tq[:], abim[:], aim[:], ALU.mult, R=[abim, aim], W=[tq])
                k.tt(fre[:], fre[:], tq[:], ALU.add, R=[fre, tq], W=[fre])
                k.tt(fre[:], fre[:], den[:], ALU.mult, R=[fre, den], W=[fre])
                k.tt(fim[:], abim[:], are[:], ALU.mult, R=[abim, are], W=[fim])
                k.tt(tq[:], abre[:], aim[:], ALU.mult, R=[abre, aim], W=[tq])
                k.tt(fim[:], fim[:], tq[:], ALU.subtract, R=[fim, tq], W=[fim])
                k.tt(fim[:], fim[:], den[:], ALU.mult, R=[fim, den], W=[fim])
                Bre = s.sb([128, 8, 16], F32); Bim = s.sb([128, 8, 16], F32)
                k.dma("sp", Bre[:], W["s5_b_re"][l, d].rearrange("(j two) p c -> (two p) j c", two=2), writes=[Bre])
                k.dma("sp", Bim[:], W["s5_b_im"][l, d].rearrange("(j two) p c -> (two p) j c", two=2), writes=[Bim])
                frb = fre[:].unsqueeze(2).broadcast_to([128, 8, 16])
                fib = fim[:].unsqueeze(2).broadcast_to([128, 8, 16])
                FBX = [s.sb([128, 8, 32], F32) for _ in range(2)]
                for t_ in FBX:
                    k.memset(t_[:], 0.0, W=[t_])
                k.tt(t3[:], Bre[:], frb, ALU.mult, R=[Bre, fre], W=[t3])
                k.tt(t4[:], Bim[:], fib, ALU.mult, R=[Bim, fim], W=[t4])
                k.tt(t3[:], t3[:], t4[:], ALU.subtract, R=[t3, t4], W=[t3])
                k.cp(FBX[0][0:64, :, 0:16], t3[0:64], R=[t3], W=[FBX[0]])
                k.cp(FBX[0][64:128, :, 16:32], t3[64:128], R=[t3], W=[FBX[0]])
                k.tt(t3[:], Bim[:], frb, ALU.mult, R=[Bim, fre], W=[t3])
                k.tt(t4[:], Bre[:], fib, ALU.mult, R=[Bre, fim], W=[t4])
                k.tt(t3[:], t3[:], t4[:], ALU.add, R=[t3, t4], W=[t3])
                k.cp(FBX[1][0:64, :, 0:16], t3[0:64], R=[t3], W=[FBX[1]])
                k.cp(FBX[1][64:128, :, 16:32], t3[64:128], R=[t3], W=[FBX[1]])
                FBT = pers[d]['FBT']
                for ri in range(2):
                    for j in range(8):
                        k.tr(ptf[:], FBX[ri][:, j, :], idf[:], R=[FBX[ri], idf], W=[ptf])
                        k.cp(FBT[ri][:, j, :], ptf[:], R=[ptf], W=[FBT[ri]])
                CX = pers[d]['CX']
                for ri, nmc in enumerate(("s5_c_re", "s5_c_im")):
                    CXT = s.sb([32, 8, 128], F32)
                    k.memset(CXT[:], 0.0, W=[CXT])
                    cv = W[nmc][l, d].rearrange("(j two) c p -> two c j p", two=2)
                    for two in range(2):
                        k.dma("sp", CXT[two * 16:(two + 1) * 16, :, two * 64:(two + 1) * 64], cv[two], writes=[CXT])
                    for j in range(8):
                        k.tr(ptc[:], CXT[:, j, :], idf[0:32, 0:32], R=[CXT, idf], W=[ptc])
                        if ri == 0:
                            k.cp(CX[0][:, j, :], ptc[:], R=[ptc], W=[CX[0]])
                        else:
                            k.ts(CX[1][:, j, :], ptc[:], -1.0, None, ALU.mult, R=[ptc], W=[CX[1]])
                prm[d] = dict(r=r, CK=CK, SK=SK, FBT=FBT, CX=CX)
            s2cm.__exit__(None, None, None)
            s = s_outer
            Dv = s.sb([32, 8], F32)
            k.dma("sp", Dv[:], W["s5_d"][l].rearrange("(j c) -> c j", c=32), writes=[Dv])
            Cts = [s.sb([128, SEQ], F32, "Ct") for _ in range(1)]
            Sts = [s.sb([128, SEQ], F32, "St") for _ in range(1)]
            tA = s.sb([128, SEQ], F32, "tA"); tB = s.sb([128, SEQ], F32, "tB")
            mr = s.sb([128, SEQ], F32, "mr"); mi = s.sb([128, SEQ], F32, "mi")
            gr = s.sb([128, SEQ], F32, "gr"); gi = s.sb([128, SEQ], F32, "gi")
            u5 = {"c": s.sb([32, CTX], F32, "u5c"), "l": s.sb([32, SEQ], F32, "u5l")}
            Ya = {"c": s.sb([32, CTX], F32, "Yac"), "l": s.sb([32, SEQ], F32, "Yal")}
            ta = s.sb([128, 512], F32, "ta"); tb = s.sb([128, 512], F32, "tb")
            tc = s.sb([128, 512], F32, "tc"); td = s.sb([128, 512], F32, "td")
            fin = s.sb([128, 2], F32, "fin"); g0 = s.sb([128, 2], F32, "g0"); tg = s.sb([128, 2], F32, "tg")
            pxr = [s.ps([128, 512], F32, "pxr") for _ in range(2)]
            pxi = [s.ps([128, 512], F32, "pxi") for _ in range(2)]
            py = [s.ps([32, 512], F32, "py") for _ in range(2)]
            seqs = (("c", 0, CTX), ("l", CTX, SEQ))
            it = 0
            for j in range(8):
                for nm, t0, Ls in seqs:
                    k.dma("sp", u5[nm][:], self.U5T[32 * j:32 * j + 32, t0:t0 + Ls], writes=[u5[nm]])
                for d in range(2):
                    P = prm[d]
                    Ct = Cts[0]; St = Sts[0]
                    it += 1
                    k.memset(Ct[:, 0:1], 1.0, W=[Ct])
                    k.memset(St[:, 0:1], 0.0, W=[St])
                    for lv in range(NLV):
                        n = 1 << lv
                        ck = P["CK"][:, j, lv:lv + 1]; sk = P["SK"][:, j, lv:lv + 1]
                        if n >= 512:
                            k.act(tA[:, 0:n], St[:, 0:n], AF.Copy, R=[St, P["SK"]], W=[tA], scale=sk)
                            k.act(tB[:, 0:n], Ct[:, 0:n], AF.Copy, R=[Ct, P["SK"]], W=[tB], scale=sk)
                        else:
                            k.ts(tA[:, 0:n], St[:, 0:n], sk, None, ALU.mult, R=[St, P["SK"]], W=[tA])
                            k.ts(tB[:, 0:n], Ct[:, 0:n], sk, None, ALU.mult, R=[Ct, P["SK"]], W=[tB])
                        k.stt(Ct[:, n:2 * n], Ct[:, 0:n], ck, tA[:, 0:n], ALU.mult, ALU.subtract, R=[Ct, P["CK"], tA], W=[Ct])
                        k.stt(St[:, n:2 * n], St[:, 0:n], ck, tB[:, 0:n], ALU.mult, ALU.add, R=[St, P["CK"], tB], W=[St])
                    for nm, t0, Ls in seqs:
                        rev = d == 1
                        tv = (lambda T, a, n: T[:, Ls - a - n:Ls - a][:, ::-1]) if rev else (lambda T, a, n: T[:, a:a + n])
                        nb = (Ls + 511) // 512
                        for b in range(nb):
                            b0 = b * 512; n = min(512, Ls - b0)
                            xr = pxr[b % 2]; xi = pxi[b % 2]
                            k.mm(xr[:, 0:n], P["FBT"][0][:, j, :], u5[nm][:, b0:b0 + n], R=[P["FBT"][0], u5[nm]], W=[xr])
                            k.mm(xi[:, 0:n], P["FBT"][1][:, j, :], u5[nm][:, b0:b0 + n], R=[P["FBT"][1], u5[nm]], W=[xi])
                            Cv = tv(Ct, b0, n); Sv = tv(St, b0, n)
                            k.tt(ta[:, 0:n], xr[:, 0:n], Cv, ALU.mult, R=[xr, Ct], W=[ta])
                            k.tt(tb[:, 0:n], xi[:, 0:n], Sv, ALU.mult, R=[xi, St], W=[tb])
                            k.tt(tc[:, 0:n], xi[:, 0:n], Cv, ALU.mult, R=[xi, Ct], W=[tc])
                            k.tt(td[:, 0:n], xr[:, 0:n], Sv, ALU.mult, R=[xr, St], W=[td])
                            k.tt(mr[:, b0:b0 + n], ta[:, 0:n], tb[:, 0:n], ALU.add, R=[ta, tb], W=[mr])
                            k.tt(mi[:, b0:b0 + n], tc[:, 0:n], td[:, 0:n], ALU.subtract, R=[tc, td], W=[mi])
                        if nm == "c":
                            i0r, i0i = 0.0, 0.0
                            rds = []
                        else:
                            c1 = P["CK"][:, j, 0:1]; s1 = P["SK"][:, j, 0:1]
                            k.ts(tg[:, 0:1], fin[:, 1:2], s1, None, ALU.mult, R=[fin, P["SK"]], W=[tg])
                            k.stt(g0[:, 0:1], fin[:, 0:1], c1, tg[:, 0:1], ALU.mult, ALU.subtract, R=[fin, P["CK"], tg], W=[g0])
                            k.ts(tg[:, 1:2], fin[:, 0:1], s1, None, ALU.mult, R=[fin, P["SK"]], W=[tg])
                            k.stt(g0[:, 1:2], fin[:, 1:2], c1, tg[:, 1:2], ALU.mult, ALU.add, R=[fin, P["CK"], tg], W=[g0])
                            i0r, i0i = g0[:, 0:1], g0[:, 1:2]
                            rds = [g0]
                        sv = (lambda T: T[:, 0:Ls][:, ::-1]) if rev else (lambda T: T[:, 0:Ls])
                        rbc = P["r"][:, j:j + 1].to_broadcast([128, Ls])
                        k.op("dve", lambda e, a=sv(gr), b_=rbc, c_=sv(mr), i_=i0r: e.tensor_tensor_scan(a, b_, c_, i_, ALU.mult, ALU.add),
                             [P["r"], mr] + rds, [gr])
                        k.op("dve", lambda e, a=sv(gi), b_=rbc, c_=sv(mi), i_=i0i: e.tensor_tensor_scan(a, b_, c_, i_, ALU.mult, ALU.add),
                             [P["r"], mi] + rds, [gi])
                        Cv = tv(Ct, 0, Ls); Sv = tv(St, 0, Ls)
                        sl = slice(0, Ls)
                        k.tt(tA[:, sl], gr[:, sl], Cv, ALU.mult, R=[gr, Ct], W=[tA])
                        k.tt(tB[:, sl], gi[:, sl], Sv, ALU.mult, R=[gi, St], W=[tB])
                        k.tt(mr[:, sl], tA[:, sl], tB[:, sl], ALU.subtract, R=[tA, tB], W=[mr])
                        k.tt(tA[:, sl], gi[:, sl], Cv, ALU.mult, R=[gi, Ct], W=[tA])
                        k.tt(tB[:, sl], gr[:, sl], Sv, ALU.mult, R=[gr, St], W=[tB])
                        k.tt(mi[:, sl], tA[:, sl], tB[:, sl], ALU.add, R=[tA, tB], W=[mi])
                        if nm == "c":
                            col = 0 if rev else Ls - 1
                            k.cp(fin[:, 0:1], mr[:, col:col + 1], R=[mr], W=[fin])
                            k.cp(fin[:, 1:2], mi[:, col:col + 1], R=[mi], W=[fin])
                            if "S5FIN" in self.k.dbg:
                                k.dma("sp", self.S5FIN[d, j], fin[:], reads=[fin])
                        if nm == "l" or ctx_out:
                            for b in range(nb):
                                b0 = b * 512; n = min(512, Ls - b0)
                                p = py[b % 2]
                                k.mm(p[:, 0:n], P["CX"][0][:, j, :], mr[:, b0:b0 + n], start=True, stop=False, R=[P["CX"][0], mr], W=[p])
                                k.mm(p[:, 0:n], P["CX"][1][:, j, :], mi[:, b0:b0 + n], start=False, stop=True, R=[P["CX"][1], mi], W=[p])
                                if d == 0:
                                    k.cp(Ya[nm][:, b0:b0 + n], p[:, 0:n], R=[p], W=[Ya[nm]], eng="act")
                                else:
                                    k.tt(Ya[nm][:, b0:b0 + n], Ya[nm][:, b0:b0 + n], p[:, 0:n], ALU.add, R=[Ya[nm], p], W=[Ya[nm]])
                for nm, t0, Ls in seqs:
                    if nm == "c" and not ctx_out:
                        continue
                    y = Ya[nm]
                    k.stt(y[:], u5[nm][:], Dv[:, j:j + 1], y[:], ALU.mult, ALU.add, R=[u5[nm], Dv, y], W=[y])
                    k.act(y[:], y[:], AF.Gelu_apprx_tanh, R=[y], W=[y])
                    k.dma("sp", self.YG[32 * j:32 * j + 32, t0:t0 + Ls], y[:], reads=[y])

    def stage_glu(self, l, t0, t1):
        k = self.k
        with k.stage() as s:
            wg = [s.sb([128, 256], BF16, "wg") for _ in range(2)]
            for kc in range(2):
                k.dma("pool", wg[kc][:], self.w["s5_glu_w"][l, kc * 128:(kc + 1) * 128, :], writes=[wg[kc]])
            bg = s.sb([128, 2], F32)
            k.dma("sp", bg[:], self.w["s5_glu_b"][l].rearrange("(c p) -> p c", p=128), writes=[bg])
            g32 = [s.sb([128, 2, 512], F32, "g32") for _ in range(2)]
            gb = [s.sb([128, 2, 512], BF16, "gb") for _ in range(2)]
            sg = s.sb([128, 512], F32, "sg")
            ob = [s.sb([128, 2, 512], BF16, "ob") for _ in range(2)]
            pz = [s.ps([128, 512], F32, "pz") for _ in range(2)]
            bi = 0
            for b0 in range(t0, t1, 512):
                n = min(512, t1 - b0)
                g = g32[bi % 2]; gbb = gb[bi % 2]; o = ob[bi % 2]
                k.dma("sp", g[:, :, 0:n], self.YG[:, b0:b0 + n].rearrange("(c p) t -> p c t", p=128), writes=[g])
                k.cp(gbb[:, :, 0:n], g[:, :, 0:n], R=[g], W=[gbb])
                for oc in range(2):
                    p = pz[oc]
                    for kc in range(2):
                        k.mm(p[:, 0:n], wg[kc][:, oc * 128:(oc + 1) * 128], gbb[:, kc, 0:n], start=kc == 0, stop=kc == 1,
                             R=[wg[kc], gbb], W=[p])
                    k.act(sg[:, 0:n], p[:, 0:n], AF.Sigmoid, R=[p, bg], W=[sg], bias=bg[:, oc:oc + 1], scale=1.0)
                    k.tt(o[:, oc, 0:n], g[:, oc, 0:n], sg[:, 0:n], ALU.mult, R=[g, sg], W=[o])
                k.dma("sp", self.OT5[:, b0:b0 + n].rearrange("(c p) t -> p c t", p=128), o[:, :, 0:n], reads=[o])
                bi += 1

    def stage_merge(self, l, t0, t1):
        k = self.k
        W = self.w
        with k.stage() as s:
            idf, idb = self.load_ident(s)
            wgt = [[None] * 8 for _ in range(4)]
            for i in range(4):
                for kc in range(8):
                    t = s.sb([128, D], BF16, "wgt")
                    k.dma("pool", t[:], W["w_gate"][l, i, kc * 128:(kc + 1) * 128, :], writes=[t])
                    wgt[i][kc] = t
            wbr = [[None] * 2 for _ in range(4)]
            for i, nm in enumerate(("w_br_mla", "w_br_hy", "w_br_swa", "w_br_s5")):
                for kc in range(2):
                    t = s.sb([128, D], BF16, "wbr")
                    k.dma("pool", t[:], W[nm][l, kc * 128:(kc + 1) * 128, :], writes=[t])
                    wbr[i][kc] = t
            wo = []
            for kc in range(8):
                t = s.sb([128, D], BF16, "wo")
                k.dma("pool", t[:], W["w_out"][l, kc * 128:(kc + 1) * 128, :], writes=[t])
                wo.append(t)
            bgt = s.sb([128, 4, 8], F32)
            k.dma("sp", bgt[:], W["b_gate"][l].rearrange("i (fc p) -> p i fc", p=128), writes=[bgt])
            GP = s.sb([128, D], F32, "GP")
            NB = 512
            ut = s.sb([128, 8, NB], BF16, "ut")
            ot = [s.sb([128, 2, NB], BF16, "ot") for _ in range(4)]
            hy32 = s.sb([128, 256], F32, "hy32"); hyb = s.sb([128, 256], BF16, "hyb")
            mT = s.sb([128, 8, NB], BF16, "mT")
            macc = s.sb([128, NB], F32, "macc")
            sg = [s.sb([128, NB], F32, "sg") for _ in range(2)]
            tm = s.sb([128, NB], F32, "tm")
            xt = [s.sb([128, D], F32, "xt") for _ in range(2)]
            u32 = s.sb([128, D], F32, "u32")
            junk = s.sb([128, 512], F32, "junk"); ss = s.sb([128, 4], F32, "ss")
            pg = [s.ps([128, NB], F32, "pg") for _ in range(2)]
            ppj = [s.ps([128, NB], F32, "ppj") for _ in range(2)]
            po = s.ps([128, D], F32, "po")
            ptb = s.ps([128, 2, 128], BF16, "ptb")
            OTs_ = (self.OTm, None, self.OTs, self.OT5)
            curcls = None
            blocks = []
            if t0 < CTX:
                blocks.append((t0, CTX - t0))
            for b0 in range(max(t0, CTX), t1, NB):
                blocks.append((b0, min(NB, t1 - b0)))
            for b0, n in blocks:
                cls = 1 if b0 < CTX else 0
                if cls != curcls:
                    curcls = cls
                    self.bvec(s, GP, self.MODV[l][cls, 5:6, :])
                k.dma("sp", ut[:, :, 0:n], self.UT[:, :, b0:b0 + n].rearrange("kc p t -> p kc t"), writes=[ut])
                for i in (0, 2, 3):
                    k.dma("sp", ot[i][:, :, 0:n], OTs_[i][:, b0:b0 + n].rearrange("(c p) t -> p c t", p=128), writes=[ot[i]])
                for st in range(n // 128):
                    k.dma("sp", hy32[:], self.OHY[b0 + st * 128:b0 + (st + 1) * 128, :], writes=[hy32])
                    k.cp(hyb[:], hy32[:], R=[hy32], W=[hyb])
                    for c2 in range(2):
                        k.tr(ptb[:, c2, :], hyb[:, c2 * 128:(c2 + 1) * 128], idb[:], R=[hyb, idb], W=[ptb])
                    k.cp(ot[1][:, :, st * 128:(st + 1) * 128], ptb[:], R=[ptb], W=[ot[1]], eng="act")
                for fc in range(8):
                    fs = slice(fc * 128, (fc + 1) * 128)
                    for i in range(4):
                        g = pg[i % 2]; pj = ppj[i % 2]
                        for kc in range(8):
                            k.mm(g[:, 0:n], wgt[i][kc][:, fs], ut[:, kc, 0:n], start=kc == 0, stop=kc == 7, R=[wgt[i][kc], ut], W=[g])
                        for kc in range(2):
                            k.mm(pj[:, 0:n], wbr[i][kc][:, fs], ot[i][:, kc, 0:n], start=kc == 0, stop=kc == 1, R=[wbr[i][kc], ot[i]], W=[pj])
                        sgi = sg[i % 2]
                        k.act(sgi[:, 0:n], g[:, 0:n], AF.Sigmoid, R=[g, bgt], W=[sgi], bias=bgt[:, i, fc:fc + 1], scale=1.0)
                        if i == 0:
                            k.tt(macc[:, 0:n], sgi[:, 0:n], pj[:, 0:n], ALU.mult, R=[sgi, pj], W=[macc])
                        else:
                            k.tt(tm[:, 0:n], sgi[:, 0:n], pj[:, 0:n], ALU.mult, R=[sgi, pj], W=[tm])
                            if i < 3:
                                k.tt(macc[:, 0:n], macc[:, 0:n], tm[:, 0:n], ALU.add, R=[macc, tm], W=[macc])
                            else:
                                k.tt(mT[:, fc, 0:n], macc[:, 0:n], tm[:, 0:n], ALU.add, R=[macc, tm], W=[mT])
                for st in range(n // 128):
                    tok = slice(b0 + st * 128, b0 + (st + 1) * 128)
                    x = xt[st % 2]
                    k.dma("sp", x[:], self.Xs[tok, :], writes=[x])
                    for half in range(2):
                        for fc in range(8):
                            k.mm(po[:, half * 512:(half + 1) * 512], mT[:, fc, st * 128:(st + 1) * 128],
                                 wo[fc][:, half * 512:(half + 1) * 512], start=fc == 0, stop=fc == 7, R=[mT, wo[fc]], W=[po])
                    self.rstd_of(s, [(po[:, 0:512], po), (po[:, 512:1024], po)], D, junk, ss)
                    for half in range(2):
                        hs = slice(half * 512, (half + 1) * 512)
                        k.stt(u32[:, hs], po[:, hs], ss[:, 0:1], GP[:, hs], ALU.mult, ALU.mult, R=[po, ss, GP], W=[u32])
                    k.tt(x[:], x[:], u32[:], ALU.add, R=[x, u32], W=[x])
                    k.dma("sp", self.Xs[tok, :], x[:], reads=[x])

    def build(self):
        self.stage_init()
        all_tiles = list(range(NT))
        lat_tiles = list(range(2, NT))
        for l in range(DEPTH):
            ctx_out = l < DEPTH - 1
            stages = [
                ("mod", lambda: self.stage_mod(l)),
                ("ffn1", lambda: self.stage_ffn(l, 0, "ffn1_up", "ffn1_down", all_tiles)),
                ("proj", lambda: self.stage_proj(l, all_tiles, ctx_q=ctx_out)),
                ("mla", lambda: self.stage_attn(l, "mla", ctx_out)),
                ("swa", lambda: self.stage_attn(l, "swa", ctx_out)),
                ("s5", lambda: self.stage_s5(l, ctx_out)),
                ("glu", lambda: self.stage_glu(l, 0 if ctx_out else CTX, NTOK)),
                ("hyl", lambda: self.hyena_seq(l, "l", SEQ, self.OHY[CTX:NTOK, :])),
                ("hyc", (lambda: self.hyena_seq(l, "c", CTX, self.OHY[0:CTX, :])) if ctx_out else None),
                ("merge", lambda: self.stage_merge(l, 0 if ctx_out else CTX, NTOK)),
                ("ffn2", lambda: self.stage_ffn(l, 2, "ffn2_up", "ffn2_down", all_tiles if ctx_out else lat_tiles,
                                                final=(l == DEPTH - 1))),
            ]
            for nm, fn in stages:
                if self.only is not None and (nm, l) not in self.only:
                    continue
                if fn is not None:
                    if nm in ("hyl", "hyc"):
                        fn()
                    else:
                        with self.k.nc.named_scope("%s_%d" % (nm, l)):
                            fn()
                snap = "Xs_%s_%d" % (nm, l)
                if snap in self.k.dbg:
                    t = self.k.dram(snap, (NTOK, D), F32)
                    with self.k.stage():
                        self.k.dma("sp", t[:, :], self.Xs[:, :])
                if self.stop_after == (nm, l):
                    self.es.close()
                    return
        self.es.close()


def build_program(dbg=(), stop_after=None, only=None):
    m = MK(dbg=dbg, stop_after=stop_after, only=only)
    m.build()
    return m


_CACHE = {}


def kernel(**inputs):
    x = np.ascontiguousarray(np.asarray(inputs["x"], dtype=np.float32))
    c = np.asarray(inputs["c"], dtype=np.float32)
    ctx = np.asarray(inputs["ctx"], dtype=np.float32)
    B = x.shape[0]
    consts = host_consts()
    shared = {"c_ctx": np.ascontiguousarray(np.asarray(inputs["c_ctx"], dtype=np.float32))}
    for n, _ in WEIGHT_SPECS:
        shared[n] = np.ascontiguousarray(np.asarray(inputs[n], dtype=np.float32))
    shared.update(consts)
    in_maps = []
    for b in range(B):
        m = {"x": np.ascontiguousarray(x[b]), "c": np.ascontiguousarray(c[b]), "ctx": np.ascontiguousarray(ctx[b])}
        m.update(shared)
        in_maps.append(m)
    prog = build_program()
    res = run_bass_kernel_spmd(prog.k.nc, in_maps, core_ids=list(range(B)))
    out = np.stack([np.asarray(r["out"], dtype=np.float32) for r in res.results], axis=0)
    return out
```

```python
import math
import numpy as np
from contextlib import ExitStack
import concourse.bass as bass
import concourse.mybir as mybir
from concourse.bass_utils import run_bass_kernel_spmd

F32 = mybir.dt.float32
BF16 = mybir.dt.bfloat16
I32 = mybir.dt.int32
ALU = mybir.AluOpType
AF = mybir.ActivationFunctionType
NDS = 96

D = 1024
DFF = 2816
SEQ = 4096
CTX = 256
NTOK = SEQ + CTX
NT = NTOK // 128
DEPTH = 2
N_IN = 1888
TWO_PI = 2.0 * math.pi


class Buf:
    def __init__(self, name, h):
        self.name = name
        self.h = h
        self.last_w = None
        self.reads = {}
        self.dsem = {}

    def __getitem__(self, key):
        return self.h[key]


class K:
    def __init__(self):
        self.nc = bass.Bass("TRN2", target_bir_lowering=False)
        nc = self.nc
        self.E = dict(pe=nc.tensor, act=nc.scalar, dve=nc.vector, pool=nc.gpsimd, sp=nc.sync)
        self.esem = {k: nc.alloc_semaphore("s_" + k) for k in ("pe", "act", "dve", "pool")}
        self.ecnt = {k: 0 for k in self.esem}
        self.dsems = [nc.alloc_semaphore("d%d" % i) for i in range(NDS)]
        self.dcnt = [0] * NDS
        self.dfree = {"sw": list(range(0, 52)), "hw": list(range(52, NDS))}
        self.seen = {k: {} for k in self.E}
        self.uid = 0
        self.nins = 0
        self._d2d = {}
        self.dbg = set()

    def dram(self, name, shape, dt, kind="Internal"):
        if name in self.dbg:
            kind = "ExternalOutput"
        return self.nc.dram_tensor(name, list(shape), dt, kind=kind).ap()

    def stage(self):
        k = self

        class _S:
            def __enter__(s):
                s.es = ExitStack()
                s.es.__enter__()
                s.bufs = []
                return s

            def __exit__(s, *a):
                if a[0] is None:
                    k.barrier()
                for b in s.bufs:
                    for kind, si in b.dsem.items():
                        k.dfree[kind].append(si)
                    b.dsem = {}
                s.es.__exit__(*a)
                return False

            def sb(s, shape, dt, name=None):
                k.uid += 1
                nm = (name or "t") + "_%d" % k.uid
                h = s.es.enter_context(k.nc.sbuf_tensor(nm, list(shape), dt))
                b = Buf(nm, h)
                s.bufs.append(b)
                return b

            def ps(s, shape, dt=F32, name=None):
                k.uid += 1
                nm = (name or "p") + "_%d" % k.uid
                h = s.es.enter_context(k.nc.psum_tensor(nm, list(shape), dt))
                b = Buf(nm, h)
                s.bufs.append(b)
                return b

        return _S()

    def _semh(self, key):
        return self.esem[key[1]] if key[0] == "e" else self.dsems[key[1]]

    def _wait(self, eng, ev):
        if ev is None:
            return
        key, val = ev
        if self.seen[eng].get(key, 0) >= val:
            return
        self.E[eng].wait_ge(self._semh(key), val)
        self.seen[eng][key] = val

    def _deps(self, eng, reads, writes):
        pe = ("e", "pe")
        for b in reads:
            if not (eng == "pe" and b.last_w and b.last_w[0] == pe):
                self._wait(eng, b.last_w)
        for b in writes:
            if not (eng == "pe" and b.last_w and b.last_w[0] == pe):
                self._wait(eng, b.last_w)
            for key, val in list(b.reads.items()):
                if eng == "pe" and key == pe:
                    continue
                self._wait(eng, (key, val))

    def _mark(self, ev, reads, writes):
        for b in reads:
            b.reads[ev[0]] = max(b.reads.get(ev[0], 0), ev[1])
        for b in writes:
            b.last_w = ev
            b.reads = {}

    def op(self, eng, fn, reads=(), writes=()):
        reads = [b for b in reads if isinstance(b, Buf)]
        writes = [b for b in writes if isinstance(b, Buf)]
        self._deps(eng, reads, writes)
        ins = fn(self.E[eng])
        self.ecnt[eng] += 1
        ev = (("e", eng), self.ecnt[eng])
        ins.then_inc(self.esem[eng], 1)
        self._mark(ev, reads, writes)
        self.nins += 1
        return ins

    def dma(self, q, out, in_, reads=(), writes=()):
        reads = [b for b in reads if isinstance(b, Buf)]
        writes = [b for b in writes if isinstance(b, Buf)]
        self._deps(q, reads, writes)
        bs = reads + writes
        kind = "sw" if q == "pool" else "hw"
        if bs:
            b0 = bs[0]
            if kind not in b0.dsem:
                b0.dsem[kind] = self.dfree[kind].pop(0)
            si = b0.dsem[kind]
        else:
            if self._d2d.get(kind) is None:
                self._d2d[kind] = self.dfree[kind].pop(0)
            si = self._d2d[kind]
        ins = self.E[q].dma_start(out=out, in_=in_)
        self.dcnt[si] += 16
        ins.then_inc(self.dsems[si], 16)
        ev = (("d", si), self.dcnt[si])
        self._mark(ev, reads, writes)
        self.nins += 1
        return ins

    def barrier(self):
        for eng in ("sp", "pe", "act", "dve", "pool"):
            for e2 in ("pe", "act", "dve", "pool"):
                if e2 != eng and self.ecnt[e2] > 0:
                    self._wait(eng, (("e", e2), self.ecnt[e2]))
            for i in range(NDS):
                if self.dcnt[i] > 0:
                    self._wait(eng, (("d", i), self.dcnt[i]))

    def mm(self, out, lhsT, rhs, start=True, stop=True, R=(), W=()):
        return self.op("pe", lambda e: e.matmul(out, lhsT, rhs, start=start, stop=stop), R, W)

    def tr(self, out, in_, ident, R=(), W=()):
        return self.op("pe", lambda e: e.transpose(out, in_, ident), R, W)

    def act(self, out, in_, func, R=(), W=(), **kw):
        return self.op("act", lambda e: e.activation(out, in_, func, **kw), R, W)

    def tt(self, out, in0, in1, op, R=(), W=(), eng="dve"):
        return self.op(eng, lambda e: e.tensor_tensor(out, in0, in1, op), R, W)

    def ts(self, out, in0, s1, s2, op0, op1=None, R=(), W=(), eng="dve"):
        if op1 is None:
            return self.op(eng, lambda e: e.tensor_scalar(out, in0, s1, None, op0), R, W)
        return self.op(eng, lambda e: e.tensor_scalar(out, in0, s1, s2, op0, op1), R, W)

    def stt(self, out, in0, scalar, in1, op0, op1, R=(), W=(), eng="dve"):
        return self.op(eng, lambda e: e.scalar_tensor_tensor(out, in0, scalar, in1, op0, op1), R, W)

    def cp(self, out, in_, R=(), W=(), eng="dve"):
        if eng == "act":
            return self.op("act", lambda e: e.copy(out, in_), R, W)
        return self.op(eng, lambda e: e.tensor_copy(out, in_), R, W)

    def memset(self, ap, val, W=(), eng="dve"):
        return self.op(eng, lambda e: e.memset(ap, val), (), W)

    def recip(self, out, in_, R=(), W=()):
        return self.op("dve", lambda e: e.reciprocal(out, in_), R, W)


def _axial(n_tokens, rot_dim):
    rows = n_tokens // 64
    r = np.repeat(np.arange(rows, dtype=np.float64), 64)
    col = np.tile(np.arange(64, dtype=np.float64), rows)
    n_freq = rot_dim // 4
    freqs = 10000.0 ** (-np.arange(n_freq, dtype=np.float64) / n_freq)
    ang = np.concatenate([r[:, None] * freqs, col[:, None] * freqs], axis=-1)
    return np.concatenate([np.cos(ang), np.sin(ang)], axis=-1).astype(np.float32)


def _hy_consts(L):
    N = 2 * L
    pos = np.zeros(N, dtype=np.int64)
    pos[:L] = np.arange(L)
    pos[L + 1:] = N - np.arange(L + 1, N)
    t = np.linspace(0.0, 1.0, L)[pos]
    bands = 16
    w = 2.0 * np.pi * pos / L
    fr = np.linspace(1e-4, bands - 1, bands)
    ang = w[:, None] * fr[None, :]
    z = np.concatenate([t[:, None], np.cos(ang), -np.sin(ang)], axis=-1)
    deltas = np.abs(np.linspace(math.log(1e-2) / 1.5, math.log(1e-2) / 0.3, 256))
    dec = np.exp(-t[:, None] * deltas[None, :])
    dec[L] = 0.0
    return np.ascontiguousarray(z.T).astype(np.float32), dec.astype(np.float32)


def _fft_consts(N1):
    N = 128 * N1
    a = np.arange(N1)
    ph = 2.0 * np.pi * ((a[:, None] * a[None, :]) % N1) / N1
    w1f = np.concatenate([np.cos(ph), -np.sin(ph)], axis=1)
    w1i = np.zeros((2 * N1, 2 * N1))
    w1i[:N1, :N1] = np.cos(ph)
    w1i[N1:, :N1] = -np.sin(ph)
    w1i[:N1, N1:] = np.sin(ph)
    w1i[N1:, N1:] = np.cos(ph)
    tl = np.arange(128)[:, None, None]
    f1 = np.arange(N1)[None, :, None]
    f2 = np.arange(128)[None, None, :]
    phi = 2.0 * np.pi * ((tl * (f1 + N1 * f2)) % N) / N
    mt = np.stack([np.cos(phi), -np.sin(phi), np.sin(phi)], axis=2)
    return w1f.astype(np.float32), w1i.astype(np.float32), np.ascontiguousarray(mt).astype(np.float32)


def _w1ib():
    w = np.zeros((128, 2, 2, 128))
    t1 = np.arange(64)
    for f2 in range(128):
        b = f2 % 2
        fh = f2 // 2
        ph = 2.0 * np.pi * ((t1 * fh) % 64) / 64.0
        w[f2, b, 0, 0:64] = np.cos(ph)
        w[f2, b, 0, 64:128] = np.sin(ph)
        w[f2, b, 1, 0:64] = -np.sin(ph)
        w[f2, b, 1, 64:128] = np.cos(ph)
    return w.astype(np.float32)


def host_consts():
    C = {}
    C["c_ident"] = np.eye(128, dtype=np.float32)
    C["c_ropem"] = _axial(SEQ, 32)
    C["c_ropew"] = _axial(SEQ, 64)
    kk = np.arange(128)[:, None, None]
    rel = np.arange(6)[None, :, None]
    qq = np.arange(512)[None, None, :]
    C["c_mask"] = (np.abs(qq - (128 * (rel - 1) + kk)) <= 128).astype(np.float32)
    C["c_w1ib"] = _w1ib()
    nn = np.arange(512)
    phd = 2.0 * np.pi * ((nn[:, None] * nn[None, :]) % 512) / 512.0
    C["c_dft512"] = np.stack([np.cos(phd), -np.sin(phd)], axis=1).astype(np.float32)
    for nm, L in (("l", SEQ), ("c", CTX)):
        zt, dec = _hy_consts(L)
        C["c_zt" + nm] = zt
        C["c_dec" + nm] = dec
        w1f, w1i, mt = _fft_consts(2 * L // 128)
        C["c_w1f" + nm] = w1f
        C["c_w1i" + nm] = w1i
        C["c_mt" + nm] = mt
    return C


WEIGHT_SPECS = [
    ("w_ada", (DEPTH, D, 9 * D)), ("b_ada", (DEPTH, 9 * D)), ("norm_pre", (DEPTH, 3, D)), ("norm_post", (DEPTH, 3, D)),
    ("ffn1_up", (DEPTH, D, 2 * DFF)), ("ffn1_down", (DEPTH, DFF, D)), ("ffn2_up", (DEPTH, D, 2 * DFF)),
    ("ffn2_down", (DEPTH, DFF, D)), ("w_in", (DEPTH, D, N_IN)), ("mla_q_norm", (DEPTH, 192)), ("mla_kv_norm", (DEPTH, 128)),
    ("mla_w_uq", (DEPTH, 192, 384)), ("mla_w_ukv", (DEPTH, 128, 512)), ("hy_conv_w", (DEPTH, 3, 768)),
    ("hy_conv_b", (DEPTH, 768)), ("hy_f_w1", (DEPTH, 33, 64)), ("hy_f_b1", (DEPTH, 64)), ("hy_f_freq", (DEPTH, 2, 64)),
    ("hy_f_w2", (DEPTH, 64, 64)), ("hy_f_b2", (DEPTH, 64)), ("hy_f_w3", (DEPTH, 64, 1024)), ("hy_bias", (DEPTH, 2, 256)),
    ("swa_sink", (DEPTH, 4)), ("s5_a_re", (DEPTH, 2, 16, 64)), ("s5_a_im", (DEPTH, 2, 16, 64)), ("s5_log_dt", (DEPTH, 2, 16)),
    ("s5_b_re", (DEPTH, 2, 16, 64, 16)), ("s5_b_im", (DEPTH, 2, 16, 64, 16)), ("s5_c_re", (DEPTH, 2, 16, 16, 64)),
    ("s5_c_im", (DEPTH, 2, 16, 16, 64)), ("s5_d", (DEPTH, 256)), ("s5_glu_w", (DEPTH, 256, 256)), ("s5_glu_b", (DEPTH, 256)),
    ("w_gate", (DEPTH, 4, D, D)), ("b_gate", (DEPTH, 4, D)), ("w_br_mla", (DEPTH, 256, D)), ("w_br_hy", (DEPTH, 256, D)),
    ("w_br_swa", (DEPTH, 256, D)), ("w_br_s5", (DEPTH, 256, D)), ("w_out", (DEPTH, D, D)),
]


class MK:
    def __init__(self, dbg=(), stop_after=None, only=None):
        self.only = only
        self.k = K()
        self.k.dbg = set(dbg)
        self.stop_after = stop_after
        k = self.k
        self.es = ExitStack()
        self.es.enter_context(k.nc.allow_low_precision("bf16 matmul operands, fp32 accumulation"))
        self.es.enter_context(k.nc.allow_non_contiguous_dma("small strided parameter loads"))
        I = lambda n, s: k.dram(n, s, F32, kind="ExternalInput")
        self.x = I("x", (SEQ, D))
        self.c = I("c", (D,))
        self.ctx = I("ctx", (CTX, D))
        self.c_ctx = I("c_ctx", (D,))
        self.w = {n: I(n, s) for n, s in WEIGHT_SPECS}
        self.cst = {n: I(n, v.shape) for n, v in host_consts().items()}
        self.out = k.dram("out", (SEQ, D), F32, kind="ExternalOutput")
        S = lambda n, s, dt=F32: k.dram(n, s, dt)
        self.Xs = S("Xs", (NTOK, D))
        self.MODV = [S("MODV%d" % l, (2, 9, D)) for l in range(DEPTH)]
        self.UT = S("UT", (8, 128, NTOK), BF16)
        self.QTm = S("QTm", (4, 96, NTOK), BF16)
        self.KTmN = S("KTmN", (4, 64, NTOK), BF16)
        self.KTmR = S("KTmR", (32, NTOK), BF16)
        self.Vm = S("Vm", (NTOK, 256), BF16)
        self.QTs = S("QTs", (4, 64, NTOK), BF16)
        self.KTs = S("KTs", (2, 64, NTOK), BF16)
        self.Vs = S("Vs", (NTOK, 128), BF16)
        self.U5T = S("U5T", (256, NTOK))
        self.HYP = {"l": S("HYPl", (SEQ + 2, 768)), "c": S("HYPc", (CTX + 2, 768))}
        self.SC = {"l": S("SCl", (SEQ, 768)), "c": S("SCc", (CTX, 768))}
        self.KFULL = {"l": S("KFULLl", (2, 2 * SEQ, 256)), "c": S("KFULLc", (2, 2 * CTX, 256))}
        self.KF = {"l": S("KFl", (2, 128, 64, 512)), "c": S("KFc", (2, 128, 4, 512))}
        self.A1 = S("A1", (128, 128 * 256), BF16)
        self.B1 = S("B1", (128, 128 * 256), BF16)
        self.Y = S("Yf", (2 * SEQ, 512), BF16)
        self.Z1 = {"l": S("Z1l", (SEQ, 256)), "c": S("Z1c", (CTX, 256))}
        self.OHY = S("OHY", (NTOK, 256))
        self.OTm = S("OTm", (256, NTOK), BF16)
        self.OTs = S("OTs", (256, NTOK), BF16)
        self.OT5 = S("OT5", (256, NTOK), BF16)
        self.YG = S("YG", (256, NTOK))
        self.S5FIN = S("S5FIN", (2, 8, 128, 2))

    def rstd_of(self, s, pieces, n, junk, ss, eps=1e-6):
        k = self.k
        rows = pieces[0][0].shape[0]
        k.memset(ss[:rows, :], 0.0, W=[ss])
        for i, (ap, b) in enumerate(pieces):
            m = ap.shape[-1]
            k.act(junk[:rows, :m], ap, AF.Square, R=[b], W=[junk, ss], accum_out=ss[:rows, 1 + i:2 + i])
        if len(pieces) == 2:
            k.tt(ss[:rows, 1:2], ss[:rows, 1:2], ss[:rows, 2:3], ALU.add, R=[ss], W=[ss])
        k.ts(ss[:rows, 0:1], ss[:rows, 1:2], 1.0 / n, eps, ALU.mult, ALU.add, R=[ss], W=[ss])
        k.recip(ss[:rows, 0:1], ss[:rows, 0:1], R=[ss], W=[ss])
        k.act(ss[:rows, 0:1], ss[:rows, 0:1], AF.Sqrt, R=[ss], W=[ss])

    def load_ident(self, s):
        k = self.k
        idf = s.sb([128, 128], F32, "idf")
        idb = s.sb([128, 128], BF16, "idb")
        k.dma("sp", idf[:], self.cst["c_ident"][:, :], writes=[idf])
        k.dma("pool", idb[:], self.cst["c_ident"][:, :], writes=[idb])
        return idf, idb

    def bvec(self, s, dst, src_row_ap, rows=128):
        self.k.dma("sp", dst[:rows], src_row_ap.partition_broadcast(rows), writes=[dst])

    def range_reduce(self, s, arg, shape, tmpf, tmpi):
        k = self.k
        k.ts(tmpf[:], arg[:], 1.0 / TWO_PI, None, ALU.mult, R=[arg], W=[tmpf])
        k.cp(tmpi[:], tmpf[:], R=[tmpf], W=[tmpi])
        k.cp(tmpf[:], tmpi[:], R=[tmpi], W=[tmpf])
        k.stt(arg[:], tmpf[:], -TWO_PI, arg[:], ALU.mult, ALU.add, R=[tmpf, arg], W=[arg])

    def stage_init(self):
        k = self.k
        with k.stage() as s:
            k.dma("sp", self.Xs[0:CTX, :], self.ctx[:, :])
            k.dma("sp", self.Xs[CTX:NTOK, :], self.x[:, :])
            z = s.sb([2, 768], F32)
            k.memset(z[:], 0.0, W=[z])
            for nm, L in (("l", SEQ), ("c", CTX)):
                k.dma("sp", self.HYP[nm][0:1, :], z[0:1, :], reads=[z])
                k.dma("sp", self.HYP[nm][L + 1:L + 2, :], z[1:2, :], reads=[z])

    def stage_mod(self, l):
        k = self.k
        with k.stage() as s:
            cin = s.sb([128, 8, 2], F32)
            k.dma("sp", cin[:, :, 0], self.c.rearrange("(kc p) -> p kc", p=128), writes=[cin])
            k.dma("sp", cin[:, :, 1], self.c_ctx.rearrange("(kc p) -> p kc", p=128), writes=[cin])
            scb = s.sb([128, 8, 2], BF16)
            k.act(scb[:], cin[:], AF.Silu, R=[cin], W=[scb])
            M = s.sb([2, 9 * D], F32)
            pp = [s.ps([2, 512], F32) for _ in range(2)]
            CG = 3 * D
            wsets = [[s.sb([128, CG], BF16, "wada") for _ in range(8)] for _ in range(2)]
            bbs = [s.sb([2, CG], F32, "bb") for _ in range(2)]
            for cg in range(3):
                wa = wsets[cg % 2]
                bb = bbs[cg % 2]
                for kc in range(8):
                    k.dma("pool", wa[kc][:], self.w["w_ada"][l, kc * 128:(kc + 1) * 128, cg * CG:(cg + 1) * CG],
                          writes=[wa[kc]])
                self.bvec(s, bb, self.w["b_ada"][l:l + 1, cg * CG:(cg + 1) * CG], rows=2)
                for cb in range(6):
                    p = pp[cb % 2]
                    for kc in range(8):
                        k.mm(p[:], scb[:, kc, :], wa[kc][:, cb * 512:(cb + 1) * 512], start=kc == 0, stop=kc == 7,
                             R=[scb, wa[kc]], W=[p])
                    c0 = cg * CG + cb * 512
                    k.tt(M[:, c0:c0 + 512], p[:], bb[:, cb * 512:(cb + 1) * 512], ALU.add, R=[p, bb], W=[M])
            npre = s.sb([2, 3 * D], F32)
            npost = s.sb([2, 3 * D], F32)
            self.bvec(s, npre, self.w["norm_pre"][l:l + 1].rearrange("o i d -> o (i d)"), rows=2)
            self.bvec(s, npost, self.w["norm_post"][l:l + 1].rearrange("o i d -> o (i d)"), rows=2)
            MV = M
            for i in range(3):
                resw = 1.0 if i == 1 else 0.5
                sl = lambda j: slice((3 * i + j) * D, (3 * i + j + 1) * D)
                k.stt(MV[:, sl(1)], M[:, sl(1)], 1.0, npre[:, i * D:(i + 1) * D], ALU.add, ALU.mult, R=[M, npre], W=[MV])
                k.stt(MV[:, sl(2)], M[:, sl(2)], resw, npost[:, i * D:(i + 1) * D], ALU.mult, ALU.mult, R=[M, npost], W=[MV])
            k.dma("sp", self.MODV[l].rearrange("c i d -> c (i d)"), MV[:], reads=[MV])

    def stage_ffn(self, l, sub, wup_name, wdn_name, tiles, final=False):
        k = self.k
        w_up = self.w[wup_name]
        w_dn = self.w[wdn_name]
        with k.stage() as s:
            idf, idb = self.load_ident(s)
            wup = []
            for kc in range(8):
                t = s.sb([128, 2 * DFF], BF16, "wup")
                k.dma("pool", t[:], w_up[l, kc * 128:(kc + 1) * 128, :], writes=[t])
                wup.append(t)
            wdn = []
            for j in range(22):
                t = s.sb([128, D], BF16, "wdn")
                k.dma("pool", t[:], w_dn[l, j * 128:(j + 1) * 128, :], writes=[t])
                wdn.append(t)

            def WA(j, kc):
                return wup[kc][:, j * 128:(j + 1) * 128], wup[kc]

            def WB(j, kc):
                return wup[kc][:, DFF + j * 128:DFF + (j + 1) * 128], wup[kc]

            def WD(j, half):
                return wdn[j][:, half * 512:(half + 1) * 512], wdn[j]

            G = s.sb([128, D], F32, "G")
            Sh = s.sb([128, D], F32, "Sh")
            GP = s.sb([128, D], F32, "GP")
            xa = [s.sb([128, D], F32, "xa") for _ in range(2)]
            xe = [s.sb([128, D], F32, "xe") for _ in range(1)]
            u32 = s.sb([128, D], F32, "u32")
            ub = s.sb([128, D], BF16, "ub")
            MT_ = 512
            uT = s.sb([128, 8, MT_], BF16, "uT")
            gT = s.sb([128, 22, MT_], BF16, "gT")
            sa = [s.sb([128, MT_], F32, "sa") for _ in range(2)]
            ss = s.sb([128, 4], F32, "ss")
            ptr = s.ps([128, 4, 128], BF16, "ptr")
            pU = [s.ps([128, MT_], F32, "pU") for _ in range(3)]
            pfb = [s.ps([128, D], F32, "pf") for _ in range(2)]
            macros = []
            for ti in tiles:
                cls = 1 if ti < 2 else 0
                if macros and macros[-1][0] == cls and len(macros[-1][1]) < 4:
                    macros[-1][1].append(ti)
                else:
                    macros.append((cls, [ti]))
            uTs = [uT, s.sb([128, 8, MT_], BF16, "uT2")]
            state = dict(xcnt=0, ucnt=0, cls_pre=None, cls_down=None)

            def pre_tile(mi, t_i):
                cls, grp = macros[mi]
                ti = grp[t_i]
                if cls != state["cls_pre"]:
                    state["cls_pre"] = cls
                    mv = self.MODV[l]
                    self.bvec(s, Sh, mv[cls, 3 * sub + 0:3 * sub + 1, :])
                    self.bvec(s, G, mv[cls, 3 * sub + 1:3 * sub + 2, :])
                ut = uTs[mi % 2]
                xt = xa[state["xcnt"] % 2]; state["xcnt"] += 1
                k.dma("sp", xt[:], self.Xs[ti * 128:(ti + 1) * 128, :], writes=[xt])
                self.rstd_of(s, [(xt[:, 0:512], xt), (xt[:, 512:1024], xt)], D, u32, ss)
                k.stt(u32[:], xt[:], ss[:, 0:1], G[:], ALU.mult, ALU.mult, R=[xt, ss, G], W=[u32])
                k.tt(ub[:], u32[:], Sh[:], ALU.add, R=[u32, Sh], W=[ub])
                for g in range(2):
                    for q in range(4):
                        kc = g * 4 + q
                        k.tr(ptr[:, q, :], ub[:, kc * 128:(kc + 1) * 128], idb[:], R=[ub, idb], W=[ptr])
                    k.cp(ut[:, g * 4:(g + 1) * 4, t_i * 128:(t_i + 1) * 128], ptr[:], R=[ptr], W=[ut], eng="act")

            for t_i in range(len(macros[0][1])):
                pre_tile(0, t_i)
            for mi, (cls, grp) in enumerate(macros):
                n = 128 * len(grp)
                ut = uTs[mi % 2]
                nxt = list(range(len(macros[mi + 1][1]))) if mi + 1 < len(macros) else []
                for j in range(22):
                    pa = pU[state["ucnt"] % 3]; pb = pU[(state["ucnt"] + 1) % 3]; state["ucnt"] += 2
                    for kc in range(8):
                        wap, wbuf = WA(j, kc)
                        k.mm(pa[:, 0:n], wap, ut[:, kc, 0:n], start=kc == 0, stop=kc == 7, R=[wbuf, ut], W=[pa])
                    for kc in range(8):
                        wbp, wbuf = WB(j, kc)
                        k.mm(pb[:, 0:n], wbp, ut[:, kc, 0:n], start=kc == 0, stop=kc == 7, R=[wbuf, ut], W=[pb])
                    sj = sa[j % 2]
                    k.act(sj[:, 0:n], pa[:, 0:n], AF.Silu, R=[pa], W=[sj])
                    k.tt(gT[:, j, 0:n], sj[:, 0:n], pb[:, 0:n], ALU.mult, R=[sj, pb], W=[gT])
                    if nxt and j in (3, 8, 13, 18):
                        pre_tile(mi + 1, nxt.pop(0))
                while nxt:
                    pre_tile(mi + 1, nxt.pop(0))
                if cls != state["cls_down"]:
                    state["cls_down"] = cls
                    self.bvec(s, GP, self.MODV[l][cls, 3 * sub + 2:3 * sub + 3, :])
                for t_i, ti in enumerate(grp):
                    xt = xe[0]
                    k.dma("sp", xt[:], self.Xs[ti * 128:(ti + 1) * 128, :], writes=[xt])
                    pf = pfb[t_i % 2]
                    for half in range(2):
                        for j in range(22):
                            wdp, wbuf = WD(j, half)
                            k.mm(pf[:, half * 512:(half + 1) * 512], gT[:, j, t_i * 128:(t_i + 1) * 128],
                                 wdp, start=j == 0, stop=j == 21, R=[gT, wbuf], W=[pf])
                    self.rstd_of(s, [(pf[:, 0:512], pf), (pf[:, 512:1024], pf)], D, u32, ss)
                    for half in range(2):
                        hs = slice(half * 512, (half + 1) * 512)
                        k.stt(u32[:, hs], pf[:, hs], ss[:, 0:1], GP[:, hs], ALU.mult, ALU.mult, R=[pf, ss, GP], W=[u32])
                    k.tt(xt[:], xt[:], u32[:], ALU.add, R=[xt, u32], W=[xt])
                    if final and ti >= 2:
                        k.dma("sp", self.out[(ti - 2) * 128:(ti - 1) * 128, :], xt[:], reads=[xt])
                    else:
                        k.dma("sp", self.Xs[ti * 128:(ti + 1) * 128, :], xt[:], reads=[xt])

    def rope_tm(self, s, src, dst, off, nh, hd, half, cs, tmp1, tmp2):
        k = self.k
        v = lambda b, a, w: b[:, off:off + nh * hd].rearrange("p (h d) -> p h d", h=nh)[:, :, a:a + w]
        x1 = v(src, 0, half); x2 = v(src, half, half)
        o1 = v(dst, 0, half); o2 = v(dst, half, half)
        t1 = tmp1[:, 0:nh * half].rearrange("p (h d) -> p h d", h=nh)
        t2 = tmp2[:, 0:nh * half].rearrange("p (h d) -> p h d", h=nh)
        cb = cs[:, 0:half].unsqueeze(1).broadcast_to([128, nh, half])
        sb_ = cs[:, half:2 * half].unsqueeze(1).broadcast_to([128, nh, half])
        k.tt(t1, x1, cb, ALU.mult, R=[src, cs], W=[tmp1])
        k.tt(t2, x2, sb_, ALU.mult, R=[src, cs], W=[tmp2])
        k.tt(o1, t1, t2, ALU.subtract, R=[tmp1, tmp2], W=[dst])
        k.tt(t1, x1, sb_, ALU.mult, R=[src, cs], W=[tmp1])
        k.tt(t2, x2, cb, ALU.mult, R=[src, cs], W=[tmp2])
        k.tt(o2, t1, t2, ALU.add, R=[tmp1, tmp2], W=[dst])

    def stage_proj(self, l, tiles, ctx_q):
        k = self.k
        W = self.w
        with k.stage() as s:
            idf, idb = self.load_ident(s)
            win = []
            for kc in range(8):
                t = s.sb([128, N_IN], BF16, "win")
                k.dma("pool", t[:], W["w_in"][l, kc * 128:(kc + 1) * 128, :], writes=[t])
                win.append(t)
            gkv = s.sb([128, 1], F32); gq = s.sb([128, 2], F32)
            k.dma("sp", gkv[:], W["mla_kv_norm"][l].rearrange("(p o) -> p o", o=1), writes=[gkv])
            k.dma("sp", gq[:, 0:1], W["mla_q_norm"][l, 0:128].rearrange("(p o) -> p o", o=1), writes=[gq])
            k.dma("sp", gq[0:64, 1:2], W["mla_q_norm"][l, 128:192].rearrange("(p o) -> p o", o=1), writes=[gq])
            wtmp = s.sb([128, 512], F32)
            wukv = s.sb([128, 512], BF16)
            k.dma("sp", wtmp[:], W["mla_w_ukv"][l], writes=[wtmp])
            k.ts(wukv[:], wtmp[:], gkv[:, 0:1], None, ALU.mult, R=[wtmp, gkv], W=[wukv])
            wuq = s.sb([128, 2, 384], BF16)
            k.dma("sp", wtmp[:, 0:384], W["mla_w_uq"][l, 0:128, :], writes=[wtmp])
            k.ts(wuq[:, 0, :], wtmp[:, 0:384], gq[:, 0:1], None, ALU.mult, R=[wtmp, gq], W=[wuq])
            k.dma("sp", wtmp[0:64, 0:384], W["mla_w_uq"][l, 128:192, :], writes=[wtmp])
            k.ts(wuq[0:64, 1, :], wtmp[0:64, 0:384], gq[0:64, 1:2], None, ALU.mult, R=[wtmp, gq], W=[wuq])
            G = s.sb([128, D], F32, "G"); Sh = s.sb([128, D], F32, "Sh")
            xb = [s.sb([128, D], F32, "x") for _ in range(2)]
            u32 = s.sb([128, D], F32, "u32")
            ub = s.sb([128, D], BF16, "ub")
            uT = [s.sb([128, 8, 128], BF16, "uT") for _ in range(2)]
            P = s.sb([128, 2048], F32, "P")
            junk = s.sb([128, 512], F32, "junk")
            ss = s.sb([128, 4], F32, "ss")
            t1 = s.sb([128, 128], F32, "t1"); t2 = s.sb([128, 128], F32, "t2")
            cm = s.sb([128, 32], F32, "cm"); cw = s.sb([128, 64], F32, "cw")
            ckvn = s.sb([128, 128], BF16, "ckvn"); ckvT = s.sb([128, 128], BF16, "ckvT")
            kTn = s.sb([64, 4, 128], BF16, "kTn")
            vmb = s.sb([128, 256], BF16, "vmb")
            krb = s.sb([128, 32], BF16, "krb"); krT = s.sb([32, 128], BF16, "krT")
            cqn = s.sb([128, 192], BF16, "cqn"); cqT = s.sb([128, 2, 128], BF16, "cqT")
            q32 = s.sb([128, 384], F32, "q32"); qb = s.sb([128, 384], BF16, "qb"); qT = s.sb([96, 4, 128], BF16, "qT")
            ksb = s.sb([128, 128], BF16, "ksb"); ksT = s.sb([64, 2, 128], BF16, "ksT")
            vsb = s.sb([128, 128], BF16, "vsb")
            qsb = s.sb([128, 256], BF16, "qsb"); qsT = s.sb([64, 4, 128], BF16, "qsT")
            u5T = s.sb([128, 2, 128], F32, "u5T")
            ptb = [s.ps([128, 4, 128], BF16, "ptb") for _ in range(2)]
            ptf = s.ps([128, 2, 128], F32, "ptf")
            pp = [s.ps([128, 512], F32, "pp") for _ in range(2)]
            pm = [s.ps([128, 512], F32, "pm") for _ in range(2)]
            curcls = None
            for it, ti in enumerate(tiles):
                cls = 1 if ti < 2 else 0
                lat = cls == 0
                tok = slice(ti * 128, (ti + 1) * 128)
                if cls != curcls:
                    curcls = cls
                    self.bvec(s, Sh, self.MODV[l][cls, 3:4, :])
                    self.bvec(s, G, self.MODV[l][cls, 4:5, :])
                xt = xb[it % 2]
                k.dma("sp", xt[:], self.Xs[tok, :], writes=[xt])
                if lat:
                    k.dma("sp", cm[:], self.cst["c_ropem"][(ti - 2) * 128:(ti - 1) * 128, :], writes=[cm])
                    k.dma("sp", cw[:], self.cst["c_ropew"][(ti - 2) * 128:(ti - 1) * 128, :], writes=[cw])
                self.rstd_of(s, [(xt[:, 0:512], xt), (xt[:, 512:1024], xt)], D, junk, ss)
                k.stt(u32[:], xt[:], ss[:, 0:1], G[:], ALU.mult, ALU.mult, R=[xt, ss, G], W=[u32])
                k.tt(ub[:], u32[:], Sh[:], ALU.add, R=[u32, Sh], W=[ub])
                ut = uT[it % 2]
                for g in range(2):
                    pt = ptb[g]
                    for q in range(4):
                        kc = g * 4 + q
                        k.tr(pt[:, q, :], ub[:, kc * 128:(kc + 1) * 128], idb[:], R=[ub, idb], W=[pt])
                    k.cp(ut[:, g * 4:(g + 1) * 4, :], pt[:], R=[pt], W=[ut], eng="act")
                k.dma("sp", self.UT[:, :, tok].rearrange("kc p t -> p kc t"), ut[:], reads=[ut])
                for cb in range(4):
                    c0 = cb * 512; c1 = min(N_IN, c0 + 512)
                    p = pp[cb % 2]
                    for kc in range(8):
                        k.mm(p[:, 0:c1 - c0], ut[:, kc, :], win[kc][:, c0:c1], start=kc == 0, stop=kc == 7, R=[ut, win[kc]], W=[p])
                    k.cp(P[:, c0:c1], p[:, 0:c1 - c0], R=[p], W=[P], eng="act" if cb % 2 else "dve")
                self.rstd_of(s, [(P[:, 0:128], P)], 128, junk, ss)
                k.ts(ckvn[:], P[:, 0:128], ss[:, 0:1], None, ALU.mult, R=[P, ss], W=[ckvn])
                pt = ptb[0]
                k.tr(pt[:, 0, :], ckvn[:], idb[:], R=[ckvn, idb], W=[pt])
                k.cp(ckvT[:], pt[:, 0, :], R=[pt], W=[ckvT], eng="act")
                pk = pm[0]
                for h in range(4):
                    k.mm(pk[:, h * 128:(h + 1) * 128], wukv[:, h * 128:(h + 1) * 128], ckvT[:], R=[wukv, ckvT], W=[pk])
                k.cp(kTn[:], pk[0:64, :].rearrange("p (h t) -> p h t", h=4), R=[pk], W=[kTn], eng="act")
                k.dma("sp", self.KTmN[:, :, tok].rearrange("h d t -> d h t"), kTn[:], reads=[kTn])
                pv = pm[1]
                k.mm(pv[:], ckvT[:], wukv[:], R=[ckvT, wukv], W=[pv])
                k.cp(vmb[:].rearrange("p (h d) -> p h d", h=4), pv[:].rearrange("p (h d) -> p h d", h=4)[:, :, 64:128],
                     R=[pv], W=[vmb])
                k.dma("sp", self.Vm[tok, :], vmb[:], reads=[vmb])
                if lat:
                    kr32 = u32
                    self.rope_tm2(P, 128, kr32, 0, 1, 32, 16, cm, t1, t2)
                    k.cp(krb[:], kr32[:, 0:32], R=[kr32], W=[krb], eng="act")
                else:
                    k.cp(krb[:], P[:, 128:160], R=[P], W=[krb])
                pt = ptb[1]
                k.tr(pt[0:32, 0, :], krb[:], idb[:], R=[krb, idb], W=[pt])
                k.cp(krT[:], pt[0:32, 0, :], R=[pt], W=[krT], eng="act")
                k.dma("sp", self.KTmR[:, tok], krT[:], reads=[krT])
                if lat or ctx_q:
                    self.rstd_of(s, [(P[:, 672:864], P)], 192, junk, ss)
                    k.ts(cqn[:], P[:, 672:864], ss[:, 0:1], None, ALU.mult, R=[P, ss], W=[cqn])
                    pt = ptb[0]
                    k.tr(pt[:, 0, :], cqn[:, 0:128], idb[:], R=[cqn, idb], W=[pt])
                    k.tr(pt[0:64, 1, :], cqn[:, 128:192], idb[:], R=[cqn, idb], W=[pt])
                    k.cp(cqT[:, 0, :], pt[:, 0, :], R=[pt], W=[cqT], eng="act")
                    k.cp(cqT[0:64, 1, :], pt[0:64, 1, :], R=[pt], W=[cqT], eng="act")
                    pq = pm[0]
                    k.mm(pq[:, 0:384], cqT[:, 0, :], wuq[:, 0, :], start=True, stop=False, R=[cqT, wuq], W=[pq])
                    k.mm(pq[:, 0:384], cqT[0:64, 1, :], wuq[0:64, 1, :], start=False, stop=True, R=[cqT, wuq], W=[pq])
                    k.cp(q32[:], pq[:, 0:384], R=[pq], W=[q32], eng="act")
                    if lat:
                        k.cp(qb[:], q32[:], R=[q32], W=[qb])
                        q32r = u32
                        self.rope_tm2(q32, 0, q32r, 0, 4, 96, 16, cm, t1, t2)
                        k.cp(qb[:].rearrange("p (h d) -> p h d", h=4)[:, :, 64:96],
                             q32r[:, 0:384].rearrange("p (h d) -> p h d", h=4)[:, :, 64:96], R=[q32r], W=[qb])
                    else:
                        k.cp(qb[:], q32[:], R=[q32], W=[qb])
                    pt = ptb[1]
                    for h in range(4):
                        k.tr(pt[0:96, h, :], qb[:, h * 96:(h + 1) * 96], idb[:], R=[qb, idb], W=[pt])
                    k.cp(qT[:], pt[0:96, :, :], R=[pt], W=[qT], eng="act")
                    k.dma("sp", self.QTm[:, :, tok].rearrange("h d t -> d h t"), qT[:], reads=[qT])
                if lat:
                    k32 = u32
                    self.rope_tm2(P, 160, k32, 0, 2, 64, 32, cw, t1, t2)
                    k.cp(ksb[:], k32[:, 0:128], R=[k32], W=[ksb], eng="act")
                else:
                    k.cp(ksb[:], P[:, 160:288], R=[P], W=[ksb])
                pt = ptb[0]
                for h in range(2):
                    k.tr(pt[0:64, h, :], ksb[:, h * 64:(h + 1) * 64], idb[:], R=[ksb, idb], W=[pt])
                k.cp(ksT[:], pt[0:64, 0:2, :], R=[pt], W=[ksT], eng="act")
                k.dma("sp", self.KTs[:, :, tok].rearrange("h d t -> d h t"), ksT[:], reads=[ksT])
                k.cp(vsb[:], P[:, 288:416], R=[P], W=[vsb], eng="act")
                k.dma("sp", self.Vs[tok, :], vsb[:], reads=[vsb])
                if lat or ctx_q:
                    if lat:
                        qs32 = u32
                        self.rope_tm2(P, 864, qs32, 0, 4, 64, 32, cw, t1, t2)
                        k.cp(qsb[:], qs32[:, 0:256], R=[qs32], W=[qsb], eng="act")
                    else:
                        k.cp(qsb[:], P[:, 864:1120], R=[P], W=[qsb])
                    pt = ptb[1]
                    for h in range(4):
                        k.tr(pt[0:64, h, :], qsb[:, h * 64:(h + 1) * 64], idb[:], R=[qsb, idb], W=[pt])
                    k.cp(qsT[:], pt[0:64, :, :], R=[pt], W=[qsT], eng="act")
                    k.dma("sp", self.QTs[:, :, tok].rearrange("h d t -> d h t"), qsT[:], reads=[qsT])
                for c2 in range(2):
                    k.tr(ptf[:, c2, :], P[:, 416 + c2 * 128:416 + (c2 + 1) * 128], idf[:], R=[P, idf], W=[ptf])
                k.cp(u5T[:], ptf[:], R=[ptf], W=[u5T], eng="act")
                k.dma("sp", self.U5T[:, tok].rearrange("(c p) t -> p c t", p=128), u5T[:], reads=[u5T])
                if lat:
                    r0 = (ti - 2) * 128 + 1
                    k.dma("sp", self.HYP["l"][r0:r0 + 128, :], P[:, 1120:1888], reads=[P])
                else:
                    r0 = ti * 128 + 1
                    k.dma("sp", self.HYP["c"][r0:r0 + 128, :], P[:, 1120:1888], reads=[P])

    def rope_tm2(self, src, soff, dst, doff, nh, hd, half, cs, tmp1, tmp2):
        k = self.k
        r0 = hd - 2 * half
        v = lambda b, off, a: b[:, off:off + nh * hd].rearrange("p (h d) -> p h d", h=nh)[:, :, r0 + a:r0 + a + half]
        x1 = v(src, soff, 0); x2 = v(src, soff, half)
        o1 = v(dst, doff, 0); o2 = v(dst, doff, half)
        t1 = tmp1[:, 0:nh * half].rearrange("p (h d) -> p h d", h=nh)
        t2 = tmp2[:, 0:nh * half].rearrange("p (h d) -> p h d", h=nh)
        cb = cs[:, 0:half].unsqueeze(1).broadcast_to([128, nh, half])
        sb_ = cs[:, half:2 * half].unsqueeze(1).broadcast_to([128, nh, half])
        k.tt(t1, x1, cb, ALU.mult, R=[src, cs], W=[tmp1])
        k.tt(t2, x2, sb_, ALU.mult, R=[src, cs], W=[tmp2])
        k.tt(o1, t1, t2, ALU.subtract, R=[tmp1, tmp2], W=[dst])
        k.tt(t1, x1, sb_, ALU.mult, R=[src, cs], W=[tmp1])
        k.tt(t2, x2, cb, ALU.mult, R=[src, cs], W=[tmp2])
        k.tt(o2, t1, t2, ALU.add, R=[tmp1, tmp2], W=[dst])

    def stage_attn(self, l, kind, ctx_q):
        k = self.k
        with k.stage() as s:
            if kind == "mla":
                dq, scale, OT = 96, 96.0 ** -0.5, self.OTm
            else:
                dq, scale, OT = 64, 0.125, self.OTs
            if kind == "swa":
                mask = s.sb([128, 6, 512], BF16)
                k.dma("pool", mask[:], self.cst["c_mask"][:, :, :], writes=[mask])
                sk = s.sb([64, 4], F32)
                self.bvec(s, sk, self.w["swa_sink"][l:l + 1, :], rows=64)
                es = s.sb([64, 4], F32)
                k.act(es[:], sk[:], AF.Exp, R=[sk], W=[es])
            KT = [s.sb([dq, NTOK], BF16, "KT") for _ in range(2)]
            QT = [s.sb([dq, NTOK], BF16, "QT") for _ in range(2)]
            V = [s.sb([128, NT, 128], BF16, "V") for _ in range(2)]
            for v_ in V:
                k.memset(v_[:, :, 64:128], 1.0, W=[v_])
            NS = 4
            pS = [s.ps([128, 512], F32, "pS") for _ in range(NS)]
            pO = [s.ps([128, 512], F32, "pO") for _ in range(2)]
            PT = [s.sb([128, 512], BF16, "PT") for _ in range(NS)]
            rd = [s.sb([64, 512], F32, "rd") for _ in range(2)]
            on = [s.sb([64, 512], BF16, "on") for _ in range(2)]
            qblocks = []
            if ctx_q:
                qblocks.append((0, CTX, [(0, None), (1, None)]))
            for qb in range(SEQ // 512):
                if kind == "mla":
                    keys = [(t, None) for t in range(NT)]
                else:
                    keys = [(0, None), (1, None)]
                    n0 = qb * 4
                    for idx in range(n0 - 1, n0 + 5):
                        if 0 <= idx < SEQ // 128:
                            keys.append((2 + idx, idx - n0 + 1))
                qblocks.append((CTX + qb * 512, 512, keys))

            def load_head(h):
                kt, qt, v = KT[h % 2], QT[h % 2], V[h % 2]
                if kind == "mla":
                    k.dma("sp", kt[0:64, :], self.KTmN[h], writes=[kt])
                    k.dma("sp", kt[64:96, :], self.KTmR[:, :], writes=[kt])
                    k.dma("sp", qt[:], self.QTm[h], writes=[qt])
                    k.dma("sp", v[:, :, 0:64], self.Vm[:, h * 64:(h + 1) * 64].rearrange("(n p) d -> p n d", p=128), writes=[v])
                else:
                    kv = h // 2
                    k.dma("sp", kt[:], self.KTs[kv], writes=[kt])
                    k.dma("sp", qt[:], self.QTs[h], writes=[qt])
                    k.dma("sp", v[:, :, 0:64], self.Vs[:, kv * 64:(kv + 1) * 64].rearrange("(n p) d -> p n d", p=128), writes=[v])

            load_head(0)
            bi = 0
            cnt = 0
            LOOK = 2
            for h in range(4):
                if h + 1 < 4:
                    load_head(h + 1)
                kt_, qt_, v_ = KT[h % 2], QT[h % 2], V[h % 2]
                for (q0, nq, keys) in qblocks:
                    po = pO[bi % 2]
                    nk = len(keys)
                    slots = {}
                    for step in range(nk + LOOK):
                        if step < nk:
                            ktile, rel = keys[step]
                            sl_ = cnt % NS
                            cnt += 1
                            slots[step] = sl_
                            pst = pS[sl_]; pt = PT[sl_]
                            k.mm(pst[:, 0:nq], kt_[:, ktile * 128:(ktile + 1) * 128], qt_[:, q0:q0 + nq], R=[kt_, qt_], W=[pst])
                            k.act(pt[:, 0:nq], pst[:, 0:nq], AF.Exp, R=[pst], W=[pt], scale=scale)
                            if rel is not None:
                                k.tt(pt[:, 0:nq], pt[:, 0:nq], mask[:, rel, 0:nq], ALU.mult, R=[pt, mask], W=[pt])
                        j = step - LOOK
                        if j >= 0:
                            ktile, rel = keys[j]
                            pt = PT[slots[j]]
                            k.mm(po[:, 0:nq], v_[:, ktile, :], pt[:, 0:nq], start=j == 0, stop=j == nk - 1, R=[v_, pt], W=[po])
                    r = rd[bi % 2]; o = on[bi % 2]
                    if kind == "swa":
                        k.ts(r[:, 0:nq], po[64:128, 0:nq], es[:, h:h + 1], None, ALU.add, R=[po, es], W=[r])
                    else:
                        k.cp(r[:, 0:nq], po[64:128, 0:nq], R=[po], W=[r])
                    k.recip(r[:, 0:nq], r[:, 0:nq], R=[r], W=[r])
                    k.tt(o[:, 0:nq], po[0:64, 0:nq], r[:, 0:nq], ALU.mult, R=[po, r], W=[o])
                    k.dma("sp", OT[h * 64:(h + 1) * 64, q0:q0 + nq], o[:, 0:nq], reads=[o])
                    bi += 1

    def stage_shortconv(self, l, nm, L):
        k = self.k
        with k.stage() as s:
            w = [s.sb([128, 768], F32, "scw") for _ in range(3)]
            b = s.sb([128, 768], F32, "scb")
            for j in range(3):
                self.bvec(s, w[j], self.w["hy_conv_w"][l, j:j + 1, :])
            self.bvec(s, b, self.w["hy_conv_b"][l:l + 1, :])
            xs = [[s.sb([128, 768], F32, "scx") for _ in range(3)] for _ in range(2)]
            acc = [s.sb([128, 768], F32, "acc") for _ in range(2)]
            tmp = s.sb([128, 768], F32, "tmp")
            for ti in range(L // 128):
                x = xs[ti % 2]; a = acc[ti % 2]
                for j in range(3):
                    k.dma("sp", x[j][:], self.HYP[nm][ti * 128 + j:ti * 128 + j + 128, :], writes=[x[j]])
                k.tt(a[:], x[0][:], w[0][:], ALU.mult, R=[x[0], w[0]], W=[a])
                k.tt(tmp[:], x[1][:], w[1][:], ALU.mult, R=[x[1], w[1]], W=[tmp])
                k.tt(a[:], a[:], tmp[:], ALU.add, R=[a, tmp], W=[a])
                k.tt(tmp[:], x[2][:], w[2][:], ALU.mult, R=[x[2], w[2]], W=[tmp])
                k.tt(a[:], a[:], tmp[:], ALU.add, R=[a, tmp], W=[a])
                k.tt(a[:], a[:], b[:], ALU.add, R=[a, b], W=[a])
                k.dma("sp", self.SC[nm][ti * 128:(ti + 1) * 128, :], a[:], reads=[a])

    def stage_filter(self, l, nm, L):
        k = self.k
        N = 2 * L
        with k.stage() as s:
            W = self.w
            w1 = s.sb([33, 64], F32); w2 = s.sb([64, 64], F32); w3 = s.sb([64, 1024], BF16)
            k.dma("sp", w1[:], W["hy_f_w1"][l], writes=[w1])
            k.dma("sp", w2[:], W["hy_f_w2"][l], writes=[w2])
            k.dma("pool", w3[:], W["hy_f_w3"][l], writes=[w3])
            h2b = s.sb([64, N], BF16, "h2b")
            bf = s.sb([64, 4], F32)
            k.dma("sp", bf[:, 0:1], W["hy_f_b1"][l].rearrange("(p o) -> p o", o=1), writes=[bf])
            k.dma("sp", bf[:, 1:2], W["hy_f_b2"][l].rearrange("(p o) -> p o", o=1), writes=[bf])
            k.dma("sp", bf[:, 2:4], W["hy_f_freq"][l].rearrange("j p -> p j"), writes=[bf])
            zt = s.sb([33, N], F32, "zt")
            a1 = s.sb([64, N], F32, "a1"); h1 = s.sb([64, N], F32, "h1")
            tf = s.sb([64, N], F32, "tf"); tiq = s.sb([64, N], I32, "tiq")
            dec = [s.sb([128, 256], F32, "dec") for _ in range(3)]
            kt = [s.sb([128, 2, 256], F32, "kt") for _ in range(3)]
            pq = [s.ps([64, 512], F32, "pq") for _ in range(2)]
            p3 = [[s.ps([128, 512], F32, "p3") for _ in range(2)] for _ in range(2)]
            zc = self.cst["c_zt" + nm]; dc = self.cst["c_dec" + nm]
            k.dma("sp", zt[:], zc[:, :], writes=[zt])
            nb = N // 512
            for layer_i in range(2):
                wm = w1 if layer_i == 0 else w2
                src = zt if layer_i == 0 else h1
                for blk in range(nb):
                    p = pq[blk % 2]
                    k.mm(p[:], wm[:], src[:, blk * 512:(blk + 1) * 512], R=[wm, src], W=[p])
                    k.ts(a1[:, blk * 512:(blk + 1) * 512], p[:], bf[:, layer_i:layer_i + 1], bf[:, 2 + layer_i:3 + layer_i],
                         ALU.add, ALU.mult, R=[p, bf], W=[a1])
                self.range_reduce(s, a1, None, tf, tiq)
                if layer_i == 0:
                    k.act(h1[:], a1[:], AF.Sin, R=[a1], W=[h1])
                else:
                    k.act(h2b[:], a1[:], AF.Sin, R=[a1], W=[h2b])
            cnt = 0
            for n0 in range(0, N, 128):
                dirn = 0 if n0 < L else 1
                dt_ = dec[cnt % 3]; ko = kt[cnt % 3]; pp = p3[cnt % 2]
                k.dma("sp", dt_[:], dc[n0:n0 + 128, :], writes=[dt_])
                for o in range(2):
                    k.mm(pp[o][:], h2b[:, n0:n0 + 128], w3[:, o * 512:(o + 1) * 512], R=[h2b, w3], W=[pp[o]])
                    k.tt(ko[:, o, :], pp[o][:, dirn * 256:(dirn + 1) * 256], dt_[:], ALU.mult, R=[pp[o], dt_], W=[ko])
                k.dma("sp", self.KFULL[nm][:, n0:n0 + 128, :].rearrange("o n c -> n o c"), ko[:], reads=[ko])
                cnt += 1

    def load_fft_tabs(self, s, nm, N1):
        k = self.k
        w1f = s.sb([N1, 2 * N1], F32)
        k.dma("sp", w1f[:], self.cst["c_w1f" + nm][:, :], writes=[w1f])
        w1i = s.sb([2 * N1, 2 * N1], BF16)
        k.dma("pool", w1i[:], self.cst["c_w1i" + nm][:, :], writes=[w1i])
        w1fb = s.sb([N1, 2 * N1], BF16)
        k.dma("pool", w1fb[:], self.cst["c_w1f" + nm][:, :], writes=[w1fb])
        mt = s.sb([128, N1, 3, 128], BF16)
        k.dma("pool", mt[:], self.cst["c_mt" + nm][:, :, :, :], writes=[mt])
        self.tabs = dict(w1f=w1f, w1i=w1i, mt=mt, w1fb=w1fb)

    def stage_fft_s1(self, nm, N1, pieces, tabname, dst):
        k = self.k
        with k.stage() as s:
            tab = self.tabs[tabname]
            idt = BF16 if tabname == "w1i" else F32
            Kt = sum(p.shape[0] for p in pieces)
            M = 2 * N1
            xin = [s.sb([Kt, 8, 256], idt, "xin") for _ in range(2)]
            xo = [s.sb([M, 2048], BF16, "xo") for _ in range(2)]
            pp = [s.ps([M, 512], F32, "pp") for _ in range(4)]
            for ch in range(16):
                xi = xin[ch % 2]
                r0 = 0
                for p in pieces:
                    k.dma("sp", xi[r0:r0 + p.shape[0], :, :], p[:, ch * 8:(ch + 1) * 8, :], writes=[xi])
                    r0 += p.shape[0]
                o = xo[ch % 2]
                xf = xi[:, :, :].rearrange("p a c -> p (a c)")
                for q in range(4):
                    k.mm(pp[q][:], tab[0:Kt, :], xf[:, q * 512:(q + 1) * 512], R=[tab, xi], W=[pp[q]])
                    k.cp(o[:, q * 512:(q + 1) * 512], pp[q][:], R=[pp[q]], W=[o], eng="act" if q % 2 else "dve")
                k.dma("sp", dst[0:M, ch * 2048:(ch + 1) * 2048], o[:], reads=[o])

    def stage_fft_s2(self, nm, N1, src, mode, **kw):
        k = self.k
        N = 128 * N1
        FC = min(8, N1)
        with k.stage() as s:
            MT = self.tabs["mt"]
            ab = [s.sb([128, FC, 2, 256], BF16, "ab") for _ in range(2)]
            pX = [s.ps([128, 512], F32, "pX") for _ in range(4)]
            if mode in ("filter", "conv"):
                XS = [s.sb([128, FC, 512], F32, "XS") for _ in range(2)]
            if mode == "conv":
                KFt = [s.sb([128, FC, 512], F32, "KFt") for _ in range(2)]
                YS = [s.sb([128, FC, 512], BF16, "YS") for _ in range(2)]
                ta = s.sb([128, FC, 256], F32, "ta"); tb = s.sb([128, FC, 256], F32, "tb")
            if mode == "inv":
                zt = [s.sb([64, FC, 256], F32, "zt") for _ in range(2)]
                gt = [s.sb([64, FC, 256], F32, "gt") for _ in range(2)]
                yv = [s.sb([64, FC, 256], F32, "yv") for _ in range(2)]
                bz = s.sb([64, 256], F32, "bz")
                self.bvec(s, bz, kw["bias"], rows=64)
                zsrc = kw["z"].rearrange("(t2 t1) c -> t2 t1 c", t1=N1)
                gsrc = kw["gate"].rearrange("(t2 t1) c -> t2 t1 c", t1=N1)
                dstv = kw["dst"].rearrange("(t2 t1) c -> t2 t1 c", t1=N1)
            for ci in range(N1 // FC):
                f0 = ci * FC
                a = ab[ci % 2]
                for r in range(2):
                    k.dma("sp", a[:, :, r, :], src[r * N1 + f0:r * N1 + f0 + FC, :].rearrange("f (tl c) -> tl f c", c=256),
                          writes=[a])
                if mode == "conv":
                    kf = KFt[ci % 2]
                    k.dma("sp", kf[:], kw["kf"][:, f0:f0 + FC, :], writes=[kf])
                if mode == "inv":
                    z = zt[ci % 2]; g = gt[ci % 2]
                    k.dma("sp", z[:], zsrc[:, f0:f0 + FC, :], writes=[z])
                    k.dma("sp", g[:], gsrc[:, f0:f0 + FC, :], writes=[g])
                for fi in range(FC):
                    f1 = f0 + fi
                    p = pX[fi % 4]
                    if mode == "inv":
                        k.mm(p[0:64, 0:256], MT[:, f1, 0, 0:64], a[:, fi, 0, :], start=True, stop=False, R=[MT, a], W=[p])
                        k.mm(p[0:64, 0:256], MT[:, f1, 1, 0:64], a[:, fi, 1, :], start=False, stop=True, R=[MT, a], W=[p])
                        k.act(yv[ci % 2][:, fi, :], p[0:64, 0:256], AF.Copy, R=[p], W=[yv[ci % 2]], scale=1.0 / N)
                    else:
                        k.mm(p[:, 0:256], MT[:, f1, 0, :], a[:, fi, 0, :], start=True, stop=False, R=[MT, a], W=[p])
                        k.mm(p[:, 0:256], MT[:, f1, 2, :], a[:, fi, 1, :], start=False, stop=True, R=[MT, a], W=[p])
                        k.mm(p[:, 256:512], MT[:, f1, 1, :], a[:, fi, 0, :], start=True, stop=False, R=[MT, a], W=[p])
                        k.mm(p[:, 256:512], MT[:, f1, 0, :], a[:, fi, 1, :], start=False, stop=True, R=[MT, a], W=[p])
                        k.cp(XS[ci % 2][:, fi, :], p[:], R=[p], W=[XS[ci % 2]], eng="act")
                if mode == "filter":
                    k.dma("sp", kw["dst"][:, f0:f0 + FC, :], XS[ci % 2][:], reads=[XS[ci % 2]])
                elif mode == "conv":
                    X = XS[ci % 2]; Yt = YS[ci % 2]
                    Xr, Xi = X[:, :, 0:256], X[:, :, 256:512]
                    Kr, Ki = kf[:, :, 0:256], kf[:, :, 256:512]
                    k.tt(ta[:], Xr, Kr, ALU.mult, R=[X, kf], W=[ta])
                    k.tt(tb[:], Xi, Ki, ALU.mult, R=[X, kf], W=[tb])
                    k.tt(Yt[:, :, 0:256], ta[:], tb[:], ALU.subtract, R=[ta, tb], W=[Yt])
                    k.tt(ta[:], Xr, Ki, ALU.mult, R=[X, kf], W=[ta])
                    k.tt(tb[:], Xi, Kr, ALU.mult, R=[X, kf], W=[tb])
                    k.tt(Yt[:, :, 256:512], ta[:], tb[:], ALU.add, R=[ta, tb], W=[Yt])
                    k.dma("sp", self.Y[0:N, :].rearrange("(f2 f1) x -> f2 f1 x", f1=N1)[:, f0:f0 + FC, :], Yt[:], reads=[Yt])
                else:
                    y = yv[ci % 2]
                    k.tt(z[:], z[:], bz[:].unsqueeze(1).broadcast_to([64, FC, 256]), ALU.mult, R=[z, bz], W=[z])
                    k.tt(y[:], y[:], z[:], ALU.add, R=[y, z], W=[y])
                    k.tt(y[:], y[:], g[:], ALU.mult, R=[y, g], W=[y])
                    k.dma("sp", dstv[:, f0:f0 + FC, :], y[:], reads=[y])

    def stage_hconv(self, l, mode, src3, K, o, **kw):
        k = self.k
        N1, N, CGW = 64, 8192, 64
        NG = 256 // CGW
        with k.stage() as s:
            T = self.tabs
            MT = T["mt"]; w1f = T["w1fb"]; idb = T["idb"]; w1ib = T["w1ib"]
            zins = [s.sb([K, 64, CGW], BF16, "zin") for _ in range(2)]
            A_sb = s.sb([128, 128 * CGW], BF16, "A_sb")
            AT = s.sb([128, CGW, 128], BF16, "AT")
            XS = [s.sb([128, 16, 128], F32, "XS") for _ in range(2)]
            pA = [s.ps([128, 512], F32, "pA") for _ in range(2)]
            pT = [s.ps([128, 4, 128], BF16, "pT") for _ in range(2)]
            pX = [s.ps([128, 4, 128], F32, "pX") for _ in range(2)]
            if mode == "conv":
                B_sb = s.sb([128, 128 * CGW], BF16, "B_sb")
                BT = s.sb([128, CGW, 128], BF16, "BT")
                KFq = [s.sb([128, 16, 128], F32, "KFq") for _ in range(2)]
                ta = s.sb([128, 16, 64], F32, "ta"); tb = s.sb([128, 16, 64], F32, "tb")
                YS = s.sb([128, 64, 128], BF16, "YS")
                yv = s.sb([64, 16, CGW], F32, "yv")
                zt = s.sb([64, 16, CGW], F32, "zt")
                gt = s.sb([64, 16, CGW], F32, "gt")
                pY = [s.ps([64, 8, CGW], F32, "pY") for _ in range(2)]
                bz = s.sb([64, 256], F32, "bz")
                self.bvec(s, bz, kw["bias"], rows=64)
                zsrc = kw["z"].rearrange("(t2 t1) c -> t2 t1 c", t1=N1)
                gsrc = kw["gate"].rearrange("(t2 t1) c -> t2 t1 c", t1=N1)
                dstv = kw["dst"].rearrange("(t2 t1) c -> t2 t1 c", t1=N1)
            KFo = self.KF["l"][o]
            st = dict(cnt=0)

            def step_AB(cg):
                cs = slice(cg * CGW, (cg + 1) * CGW)
                for hf in range(2):
                    zin = zins[hf]
                    k.dma("pool", zin[:], src3[:, hf * 64:(hf + 1) * 64, cs], writes=[zin])
                    zf = zin[:, :, :].rearrange("p t c -> p (t c)")
                    for q in range(8):
                        p = pA[q % 2]
                        k.mm(p[:], w1f[0:K, :], zf[:, q * 512:(q + 1) * 512], R=[w1f, zin], W=[p])
                        c0 = hf * 4096 + q * 512
                        k.cp(A_sb[:, c0:c0 + 512], p[:], R=[p], W=[A_sb], eng="act" if q % 2 else "dve")
                Av = A_sb[:, :].rearrange("p (t c) -> p t c", c=CGW)
                for c4 in range(CGW // 4):
                    pt = pT[c4 % 2]
                    for q in range(4):
                        k.tr(pt[:, q, :], Av[:, :, c4 * 4 + q], idb[:], R=[A_sb, idb], W=[pt])
                    k.cp(AT[:, c4 * 4:(c4 + 1) * 4, :], pt[:], R=[pt], W=[AT], eng="act" if c4 % 2 else "dve")

            def step_CD(cg):
                for fq in range(4):
                    X = XS[st["cnt"] % 2]
                    if mode == "conv":
                        Kq = KFq[st["cnt"] % 2]
                        k.dma("sp", Kq[:, :, 0:64], KFo[:, fq * 16:(fq + 1) * 16, cg * CGW:(cg + 1) * CGW], writes=[Kq])
                        k.dma("sp", Kq[:, :, 64:128], KFo[:, fq * 16:(fq + 1) * 16, 256 + cg * CGW:256 + (cg + 1) * CGW], writes=[Kq])
                    for f4 in range(4):
                        p = pX[f4 % 2]
                        for q in range(4):
                            f1 = fq * 16 + f4 * 4 + q
                            Are = AT[:, :, f1]; Aim = AT[:, :, 64 + f1]
                            k.mm(p[:, q, 0:64], MT[:, f1, 0, :], Are, start=True, stop=False, R=[MT, AT], W=[p])
                            k.mm(p[:, q, 0:64], MT[:, f1, 2, :], Aim, start=False, stop=True, R=[MT, AT], W=[p])
                            k.mm(p[:, q, 64:128], MT[:, f1, 1, :], Are, start=True, stop=False, R=[MT, AT], W=[p])
                            k.mm(p[:, q, 64:128], MT[:, f1, 0, :], Aim, start=False, stop=True, R=[MT, AT], W=[p])
                        k.cp(X[:, f4 * 4:(f4 + 1) * 4, :], p[:], R=[p], W=[X], eng="act")
                    if mode == "filter":
                        k.dma("sp", KFo[:, fq * 16:(fq + 1) * 16, cg * CGW:(cg + 1) * CGW], X[:, :, 0:64], reads=[X])
                        k.dma("sp", KFo[:, fq * 16:(fq + 1) * 16, 256 + cg * CGW:256 + (cg + 1) * CGW], X[:, :, 64:128], reads=[X])
                    else:
                        Xr, Xi = X[:, :, 0:64], X[:, :, 64:128]
                        Kr, Ki = Kq[:, :, 0:64], Kq[:, :, 64:128]
                        Yq = YS[:, fq * 16:(fq + 1) * 16, :]
                        k.tt(ta[:], Xr, Kr, ALU.mult, R=[X, Kq], W=[ta])
                        k.tt(tb[:], Xi, Ki, ALU.mult, R=[X, Kq], W=[tb])
                        k.tt(Yq[:, :, 0:64], ta[:], tb[:], ALU.subtract, R=[ta, tb], W=[YS])
                        k.tt(ta[:], Xr, Ki, ALU.mult, R=[X, Kq], W=[ta])
                        k.tt(tb[:], Xi, Kr, ALU.mult, R=[X, Kq], W=[tb])
                        k.tt(Yq[:, :, 64:128], ta[:], tb[:], ALU.add, R=[ta, tb], W=[YS])
                    st["cnt"] += 1

            def step_EFGH(cg):
                cs = slice(cg * CGW, (cg + 1) * CGW)
                for b in range(2):
                    for g8 in range(8):
                        p = pA[g8 % 2]
                        k.mm(p[:], w1ib[:, b, 0, :], YS[:, g8 * 8:(g8 + 1) * 8, 0:64], start=True, stop=False, R=[w1ib, YS], W=[p])
                        k.mm(p[:], w1ib[:, b, 1, :], YS[:, g8 * 8:(g8 + 1) * 8, 64:128], start=False, stop=True, R=[w1ib, YS], W=[p])
                        c0 = b * 4096 + g8 * 512
                        k.cp(B_sb[:, c0:c0 + 512], p[:], R=[p], W=[B_sb], eng="act" if g8 % 2 else "dve")
                Bv = B_sb[:, :].rearrange("p (t c) -> p t c", c=CGW)
                for c4 in range(CGW // 4):
                    pt = pT[c4 % 2]
                    for q in range(4):
                        k.tr(pt[:, q, :], Bv[:, :, c4 * 4 + q], idb[:], R=[B_sb, idb], W=[pt])
                    k.cp(BT[:, c4 * 4:(c4 + 1) * 4, :], pt[:], R=[pt], W=[BT], eng="act" if c4 % 2 else "dve")
                for tq in range(4):
                    k.dma("sp", zt[:], zsrc[:, tq * 16:(tq + 1) * 16, cs], writes=[zt])
                    k.dma("sp", gt[:], gsrc[:, tq * 16:(tq + 1) * 16, cs], writes=[gt])
                    for t8 in range(2):
                        p = pY[t8 % 2]
                        for q in range(8):
                            t1 = tq * 16 + t8 * 8 + q
                            k.mm(p[:, q, :], MT[:, t1, 0, 0:64], BT[:, :, t1], start=True, stop=False, R=[MT, BT], W=[p])
                            k.mm(p[:, q, :], MT[:, t1, 1, 0:64], BT[:, :, 64 + t1], start=False, stop=True, R=[MT, BT], W=[p])
                        k.act(yv[:, t8 * 8:(t8 + 1) * 8, :], p[:], AF.Copy, R=[p], W=[yv], scale=1.0 / N)
                    k.tt(zt[:], zt[:], bz[:, cs].unsqueeze(1).broadcast_to([64, 16, CGW]), ALU.mult, R=[zt, bz], W=[zt])
                    k.tt(yv[:], yv[:], zt[:], ALU.add, R=[yv, zt], W=[yv])
                    k.tt(yv[:], yv[:], gt[:], ALU.mult, R=[yv, gt], W=[yv])
                    k.dma("sp", dstv[:, tq * 16:(tq + 1) * 16, cs], yv[:], reads=[yv])

            if mode == "filter":
                for cg in range(NG):
                    step_AB(cg)
                    step_CD(cg)
            else:
                step_AB(0)
                for cg in range(NG):
                    step_CD(cg)
                    if cg + 1 < NG:
                        step_AB(cg + 1)
                    step_EFGH(cg)

    def stage_hconv_ctx(self, l, ohy_rows):
        k = self.k
        N = 512
        with k.stage() as s:
            DF = s.sb([128, 4, 2, 512], BF16, "DF")
            k.dma("pool", DF[:], self.cst["c_dft512"].rearrange("(nc p) r f -> p nc r f", p=128), writes=[DF])
            kin = s.sb([128, 4, 256], BF16, "kin")
            KFc = [s.sb([128, 4, 512], F32, "KFc") for _ in range(2)]
            sc = s.sb([128, 2, 768], F32, "sc")
            k.dma("sp", sc[:], self.SC["c"].rearrange("(tc p) c -> p tc c", p=128), writes=[sc])
            zb = s.sb([128, 2, 256], BF16, "zb")
            z32 = s.sb([128, 2, 256], F32, "z32")
            XS = s.sb([128, 4, 512], F32, "XS")
            YS = s.sb([128, 4, 512], BF16, "YS")
            ta = s.sb([128, 4, 256], F32, "ta"); tb = s.sb([128, 4, 256], F32, "tb")
            yv = s.sb([128, 2, 256], F32, "yv")
            b0 = s.sb([128, 256], F32, "b0"); b1 = s.sb([128, 256], F32, "b1")
            self.bvec(s, b0, self.w["hy_bias"][l, 0:1, :])
            self.bvec(s, b1, self.w["hy_bias"][l, 1:2, :])
            bzs = [b0, b1]
            pX = [s.ps([128, 512], F32, "pX") for _ in range(4)]
            pYo = [s.ps([128, 256], F32, "pYo") for _ in range(2)]

            def fwd(src, nchunks, dst_copy):
                for fc in range(4):
                    p = pX[fc]
                    for r in range(2):
                        for nc_ in range(nchunks):
                            k.mm(p[:, r * 256:(r + 1) * 256], DF[:, nc_, r, fc * 128:(fc + 1) * 128], src[:, nc_, :],
                                 start=nc_ == 0, stop=nc_ == nchunks - 1, R=[DF, src], W=[p])
                    k.cp(dst_copy[:, fc, :], p[:], R=[p], W=[dst_copy], eng="act")

            for o in range(2):
                k.dma("pool", kin[:], self.KFULL["c"][o].rearrange("(nc p) c -> p nc c", p=128), writes=[kin])
                fwd(kin, 4, KFc[o])
            for o in range(2):
                if o == 0:
                    k.cp(z32[:], sc[:, :, 0:256], R=[sc], W=[z32])
                k.cp(zb[:], z32[:], R=[z32], W=[zb])
                fwd(zb, 2, XS)
                Kf = KFc[o]
                Xr, Xi = XS[:, :, 0:256], XS[:, :, 256:512]
                Kr, Ki = Kf[:, :, 0:256], Kf[:, :, 256:512]
                k.tt(ta[:], Xr, Kr, ALU.mult, R=[XS, Kf], W=[ta])
                k.tt(tb[:], Xi, Ki, ALU.mult, R=[XS, Kf], W=[tb])
                k.tt(YS[:, :, 0:256], ta[:], tb[:], ALU.subtract, R=[ta, tb], W=[YS])
                k.tt(ta[:], Xr, Ki, ALU.mult, R=[XS, Kf], W=[ta])
                k.tt(tb[:], Xi, Kr, ALU.mult, R=[XS, Kf], W=[tb])
                k.tt(YS[:, :, 256:512], ta[:], tb[:], ALU.add, R=[ta, tb], W=[YS])
                for tc in range(2):
                    p = pYo[tc]
                    for fc in range(4):
                        k.mm(p[:], DF[:, fc, 0, tc * 128:(tc + 1) * 128], YS[:, fc, 0:256], start=fc == 0, stop=False, R=[DF, YS], W=[p])
                        k.mm(p[:], DF[:, fc, 1, tc * 128:(tc + 1) * 128], YS[:, fc, 256:512], start=False, stop=fc == 3, R=[DF, YS], W=[p])
                    k.act(yv[:, tc, :], p[:], AF.Copy, R=[p], W=[yv], scale=1.0 / N)
                    k.tt(ta[:, tc, :], z32[:, tc, :], bzs[o][:], ALU.mult, R=[z32, bzs[o]], W=[ta])
                    k.tt(yv[:, tc, :], yv[:, tc, :], ta[:, tc, :], ALU.add, R=[yv, ta], W=[yv])
                    k.tt(yv[:, tc, :], yv[:, tc, :], sc[:, tc, 256 * (o + 1):256 * (o + 2)], ALU.mult, R=[yv, sc], W=[yv])
                if o == 0:
                    k.cp(z32[:], yv[:], R=[yv], W=[z32])
                else:
                    k.dma("sp", ohy_rows.rearrange("(tc p) c -> p tc c", p=128), yv[:], reads=[yv])

    def hyena_seq(self, l, nm, L, ohy_rows):
        N1 = 2 * L // 128
        ns = self.k.nc.named_scope
        with ns("hsc_%s%d" % (nm, l)):
            self.stage_shortconv(l, nm, L)
        with ns("hflt_%s%d" % (nm, l)):
            self.stage_filter(l, nm, L)
        v3 = lambda ap: ap.rearrange("(th tl) c -> th tl c", tl=128)
        if nm == "c":
            with ns("hcc_%d" % l):
                self.stage_hconv_ctx(l, ohy_rows)
            return
        with self.k.stage() as outer:
            self.load_fft_tabs(outer, nm, N1)
            if nm == "l":
                k = self.k
                idf, idb = self.load_ident(outer)
                w1ib = outer.sb([128, 2, 2, 128], BF16)
                k.dma("pool", w1ib[:], self.cst["c_w1ib"][:, :, :, :], writes=[w1ib])
                self.tabs["idb"] = idb
                self.tabs["w1ib"] = w1ib
                for o in range(2):
                    with ns("hff%d_%d" % (o, l)):
                        self.stage_hconv(l, "filter", v3(self.KFULL[nm][o]), 64, o)
                for o in range(2):
                    zin = self.SC[nm][:, 0:256] if o == 0 else self.Z1[nm][:, :]
                    gate = self.SC[nm][:, 256 * (o + 1):256 * (o + 2)]
                    dst = self.Z1[nm][:, :] if o == 0 else ohy_rows
                    with ns("hcv%d_%d" % (o, l)):
                        self.stage_hconv(l, "conv", v3(zin), 32, o, bias=self.w["hy_bias"][l, o:o + 1, :], z=zin, gate=gate, dst=dst)
                return
            for o in range(2):
                self.stage_fft_s1(nm, N1, [v3(self.KFULL[nm][o])], "w1f", self.A1)
                self.stage_fft_s2(nm, N1, self.A1, "filter", dst=self.KF[nm][o])
            for o in range(2):
                zin = self.SC[nm][:, 0:256] if o == 0 else self.Z1[nm][:, :]
                gate = self.SC[nm][:, 256 * (o + 1):256 * (o + 2)]
                dst = self.Z1[nm][:, :] if o == 0 else ohy_rows
                self.stage_fft_s1(nm, N1, [v3(zin)], "w1f", self.A1)
                self.stage_fft_s2(nm, N1, self.A1, "conv", kf=self.KF[nm][o])
                yv = self.Y[0:128 * N1, :].rearrange("(fh fl) (r c) -> r fh fl c", fl=128, r=2)
                self.stage_fft_s1(nm, N1, [yv[0], yv[1]], "w1i", self.B1)
                self.stage_fft_s2(nm, N1, self.B1, "inv", bias=self.w["hy_bias"][l, o:o + 1, :], z=zin, gate=gate, dst=dst)

    def stage_s5(self, l, ctx_out):
        k = self.k
        W = self.w
        PI2 = math.pi / 2.0
        with k.stage() as s:
            idf, idb = self.load_ident(s)
            NLV = 12
            prm = {}
            pers = {}
            for d in range(2):
                pers[d] = dict(r=s.sb([128, 8], F32), CK=s.sb([128, 8, NLV + 1], F32), SK=s.sb([128, 8, NLV + 1], F32),
                               FBT=[s.sb([32, 8, 128], F32) for _ in range(2)], CX=[s.sb([128, 8, 32], F32) for _ in range(2)])
            s_outer = s
            s2cm = k.stage()
            s = s2cm.__enter__()
            tq = s.sb([128, 8], F32); tq2 = s.sb([128, 8], F32); tqi = s.sb([128, 8], I32)
            t3 = s.sb([128, 8, 16], F32); t4 = s.sb([128, 8, 16], F32)
            ptf = s.ps([32, 128], F32, "ptf")
            ptc = s.ps([128, 32], F32, "ptc")
            for d in range(2):
                are = s.sb([128, 8], F32); aim = s.sb([128, 8], F32); ldt = s.sb([128, 8], F32)
                k.dma("sp", are[:], W["s5_a_re"][l, d].rearrange("(j two) p -> (two p) j", two=2), writes=[are])
                k.dma("sp", aim[:], W["s5_a_im"][l, d].rearrange("(j two) p -> (two p) j", two=2), writes=[aim])
                ldv = W["s5_log_dt"][l, d:d + 1, :].rearrange("o (j two) -> o two j", two=2)
                for two in range(2):
                    k.dma("sp", ldt[two * 64:(two + 1) * 64, :], ldv[:, two, :].partition_broadcast(64), writes=[ldt])
                dt = s.sb([128, 8], F32); r = pers[d]['r']; th = s.sb([128, 8], F32)
                k.act(dt[:], ldt[:], AF.Exp, R=[ldt], W=[dt])
                k.tt(tq[:], dt[:], are[:], ALU.mult, R=[dt, are], W=[tq])
                k.act(r[:], tq[:], AF.Exp, R=[tq], W=[r])
                k.tt(th[:], dt[:], aim[:], ALU.mult, R=[dt, aim], W=[th])
                CK = pers[d]['CK']; SK = pers[d]['SK']
                k.cp(tq[:], th[:], R=[th], W=[tq])
                self.range_reduce(s, tq, None, tq2, tqi)
                k.act(SK[:, :, 0], tq[:], AF.Sin, R=[tq], W=[SK])
                k.ts(tq[:], th[:], PI2, None, ALU.add, R=[th], W=[tq])
                self.range_reduce(s, tq, None, tq2, tqi)
                k.act(CK[:, :, 0], tq[:], AF.Sin, R=[tq], W=[CK])
                for lv in range(NLV):
                    k.tt(tq[:], CK[:, :, lv], CK[:, :, lv], ALU.mult, R=[CK], W=[tq])
                    k.tt(tq2[:], SK[:, :, lv], SK[:, :, lv], ALU.mult, R=[SK], W=[tq2])
                    k.tt(CK[:, :, lv + 1], tq[:], tq2[:], ALU.subtract, R=[tq, tq2], W=[CK])
                    k.tt(tq[:], CK[:, :, lv], SK[:, :, lv], ALU.mult, R=[CK, SK], W=[tq])
                    k.ts(SK[:, :, lv + 1], tq[:], 2.0, None, ALU.mult, R=[tq], W=[SK])
                abre = s.sb([128, 8], F32); abim = s.sb([128, 8], F32)
                k.tt(abre[:], r[:], CK[:, :, 0], ALU.mult, R=[r, CK], W=[abre])
                k.ts(abre[:], abre[:], -1.0, None, ALU.add, R=[abre], W=[abre])
                k.tt(abim[:], r[:], SK[:, :, 0], ALU.mult, R=[r, SK], W=[abim])
                den = s.sb([128, 8], F32)
                k.tt(den[:], are[:], are[:], ALU.mult, R=[are], W=[den])
                k.tt(tq[:], aim[:], aim[:], ALU.mult, R=[aim], W=[tq])
                k.tt(den[:], den[:], tq[:], ALU.add, R=[den, tq], W=[den])
                k.recip(den[:], den[:], R=[den], W=[den])
                fre = s.sb([128, 8], F32); fim = s.sb([128, 8], F32)
                k.tt(fre[:], abre[:], are[:], ALU.mult, R=[abre, are], W=[fre])
                k.tt(tq[:], abim[:], aim[:], ALU.mult, R=[abim, aim], W=[tq])
                k.tt(fre[:], fre[:], tq[:], ALU.add, R=[fre, tq], W=[fre])
                k.tt(fre[:], fre[:], den[:], ALU.mult, R=[fre, den], W=[fre])
                k.tt(fim[:], abim[:], are[:], ALU.mult, R=[abim, are], W=[fim])
                k.tt(tq[:], abre[:], aim[:], ALU.mult, R=[abre, aim], W=[tq])
                k.tt(fim[:], fim[:], tq[:], ALU.subtract, R=[fim, tq], W=[fim])
                k.tt(fim[:], fim[:], den[:], ALU.mult, R=[fim, den], W=[fim])
                Bre = s.sb([128, 8, 16], F32); Bim = s.sb([128, 8, 16], F32)
                k.dma("sp", Bre[:], W["s5_b_re"][l, d].rearrange("(j two) p c -> (two p) j c", two=2), writes=[Bre])
                k.dma("sp", Bim[:], W["s5_b_im"][l, d].rearrange("(j two) p c -> (two p) j c", two=2), writes=[Bim])
                frb = fre[:].unsqueeze(2).broadcast_to([128, 8, 16])
                fib = fim[:].unsqueeze(2).broadcast_to([128, 8, 16])
                FBX = [s.sb([128, 8, 32], F32) for _ in range(2)]
                for t_ in FBX:
                    k.memset(t_[:], 0.0, W=[t_])
                k.tt(t3[:], Bre[:], frb, ALU.mult, R=[Bre, fre], W=[t3])
                k.tt(t4[:], Bim[:], fib, ALU.mult, R=[Bim, fim], W=[t4])
                k.tt(t3[:], t3[:], t4[:], ALU.subtract, R=[t3, t4], W=[t3])
                k.cp(FBX[0][0:64, :, 0:16], t3[0:64], R=[t3], W=[FBX[0]])
                k.cp(FBX[0][64:128, :, 16:32], t3[64:128], R=[t3], W=[FBX[0]])
                k.tt(t3[:], Bim[:], frb, ALU.mult, R=[Bim, fre], W=[t3])
                k.tt(t4[:], Bre[:], fib, ALU.mult, R=[Bre, fim], W=[t4])
                k.tt(t3[:], t3[:], t4[:], ALU.add, R=[t3, t4], W=[t3])
                k.cp(FBX[1][0:64, :, 0:16], t3[0:64], R=[t3], W=[FBX[1]])
                k.cp(FBX[1][64:128, :, 16:32], t3[64:128], R=[t3], W=[FBX[1]])
                FBT = pers[d]['FBT']
                for ri in range(2):
                    for j in range(8):
                        k.tr(ptf[:], FBX[ri][:, j, :], idf[:], R=[FBX[ri], idf], W=[ptf])
                        k.cp(FBT[ri][:, j, :], ptf[:], R=[ptf], W=[FBT[ri]])
                CX = pers[d]['CX']
                for ri, nmc in enumerate(("s5_c_re", "s5_c_im")):
                    CXT = s.sb([32, 8, 128], F32)
                    k.memset(CXT[:], 0.0, W=[CXT])
                    cv = W[nmc][l, d].rearrange("(j two) c p -> two c j p", two=2)
                    for two in range(2):
                        k.dma("sp", CXT[two * 16:(two + 1) * 16, :, two * 64:(two + 1) * 64], cv[two], writes=[CXT])
                    for j in range(8):
                        k.tr(ptc[:], CXT[:, j, :], idf[0:32, 0:32], R=[CXT, idf], W=[ptc])
                        if ri == 0:
                            k.cp(CX[0][:, j, :], ptc[:], R=[ptc], W=[CX[0]])
                        else:
                            k.ts(CX[1][:, j, :], ptc[:], -1.0, None, ALU.mult, R=[ptc], W=[CX[1]])
                prm[d] = dict(r=r, CK=CK, SK=SK, FBT=FBT, CX=CX)
            s2cm.__exit__(None, None, None)
            s = s_outer
            Dv = s.sb([32, 8], F32)
            k.dma("sp", Dv[:], W["s5_d"][l].rearrange("(j c) -> c j", c=32), writes=[Dv])
            Cts = [s.sb([128, SEQ], F32, "Ct") for _ in range(1)]
            Sts = [s.sb([128, SEQ], F32, "St") for _ in range(1)]
            tA = s.sb([128, SEQ], F32, "tA"); tB = s.sb([128, SEQ], F32, "tB")
            mr = s.sb([128, SEQ], F32, "mr"); mi = s.sb([128, SEQ], F32, "mi")
            gr = s.sb([128, SEQ], F32, "gr"); gi = s.sb([128, SEQ], F32, "gi")
            u5 = {"c": s.sb([32, CTX], F32, "u5c"), "l": s.sb([32, SEQ], F32, "u5l")}
            Ya = {"c": s.sb([32, CTX], F32, "Yac"), "l": s.sb([32, SEQ], F32, "Yal")}
            ta = s.sb([128, 512], F32, "ta"); tb = s.sb([128, 512], F32, "tb")
            tc = s.sb([128, 512], F32, "tc"); td = s.sb([128, 512], F32, "td")
            fin = s.sb([128, 2], F32, "fin"); g0 = s.sb([128, 2], F32, "g0"); tg = s.sb([128, 2], F32, "tg")
            pxr = [s.ps([128, 512], F32, "pxr") for _ in range(2)]
            pxi = [s.ps([128, 512], F32, "pxi") for _ in range(2)]
            py = [s.ps([32, 512], F32, "py") for _ in range(2)]
            seqs = (("c", 0, CTX), ("l", CTX, SEQ))
            it = 0
            for j in range(8):
                for nm, t0, Ls in seqs:
                    k.dma("sp", u5[nm][:], self.U5T[32 * j:32 * j + 32, t0:t0 + Ls], writes=[u5[nm]])
                for d in range(2):
                    P = prm[d]
                    Ct = Cts[0]; St = Sts[0]
                    it += 1
                    k.memset(Ct[:, 0:1], 1.0, W=[Ct])
                    k.memset(St[:, 0:1], 0.0, W=[St])
                    for lv in range(NLV):
                        n = 1 << lv
                        ck = P["CK"][:, j, lv:lv + 1]; sk = P["SK"][:, j, lv:lv + 1]
                        if n >= 512:
                            k.act(tA[:, 0:n], St[:, 0:n], AF.Copy, R=[St, P["SK"]], W=[tA], scale=sk)
                            k.act(tB[:, 0:n], Ct[:, 0:n], AF.Copy, R=[Ct, P["SK"]], W=[tB], scale=sk)
                        else:
                            k.ts(tA[:, 0:n], St[:, 0:n], sk, None, ALU.mult, R=[St, P["SK"]], W=[tA])
                            k.ts(tB[:, 0:n], Ct[:, 0:n], sk, None, ALU.mult, R=[Ct, P["SK"]], W=[tB])
                        k.stt(Ct[:, n:2 * n], Ct[:, 0:n], ck, tA[:, 0:n], ALU.mult, ALU.subtract, R=[Ct, P["CK"], tA], W=[Ct])
                        k.stt(St[:, n:2 * n], St[:, 0:n], ck, tB[:, 0:n], ALU.mult, ALU.add, R=[St, P["CK"], tB], W=[St])
                    for nm, t0, Ls in seqs:
                        rev = d == 1
                        tv = (lambda T, a, n: T[:, Ls - a - n:Ls - a][:, ::-1]) if rev else (lambda T, a, n: T[:, a:a + n])
                        nb = (Ls + 511) // 512
                        for b in range(nb):
                            b0 = b * 512; n = min(512, Ls - b0)
                            xr = pxr[b % 2]; xi = pxi[b % 2]
                            k.mm(xr[:, 0:n], P["FBT"][0][:, j, :], u5[nm][:, b0:b0 + n], R=[P["FBT"][0], u5[nm]], W=[xr])
                            k.mm(xi[:, 0:n], P["FBT"][1][:, j, :], u5[nm][:, b0:b0 + n], R=[P["FBT"][1], u5[nm]], W=[xi])
                            Cv = tv(Ct, b0, n); Sv = tv(St, b0, n)
                            k.tt(ta[:, 0:n], xr[:, 0:n], Cv, ALU.mult, R=[xr, Ct], W=[ta])
                            k.tt(tb[:, 0:n], xi[:, 0:n], Sv, ALU.mult, R=[xi, St], W=[tb])
                            k.tt(tc[:, 0:n], xi[:, 0:n], Cv, ALU.mult, R=[xi, Ct], W=[tc])
                            k.tt(td[:, 0:n], xr[:, 0:n], Sv, ALU.mult, R=[xr, St], W=[td])
                            k.tt(mr[:, b0:b0 + n], ta[:, 0:n], tb[:, 0:n], ALU.add, R=[ta, tb], W=[mr])
                            k.tt(mi[:, b0:b0 + n], tc[:, 0:n], td[:, 0:n], ALU.subtract, R=[tc, td], W=[mi])
                        if nm == "c":
                            i0r, i0i = 0.0, 0.0
                            rds = []
                        else:
                            c1 = P["CK"][:, j, 0:1]; s1 = P["SK"][:, j, 0:1]
                            k.ts(tg[:, 0:1], fin[:, 1:2], s1, None, ALU.mult, R=[fin, P["SK"]], W=[tg])
                            k.stt(g0[:, 0:1], fin[:, 0:1], c1, tg[:, 0:1], ALU.mult, ALU.subtract, R=[fin, P["CK"], tg], W=[g0])
                            k.ts(tg[:, 1:2], fin[:, 0:1], s1, None, ALU.mult, R=[fin, P["SK"]], W=[tg])
                            k.stt(g0[:, 1:2], fin[:, 1:2], c1, tg[:, 1:2], ALU.mult, ALU.add, R=[fin, P["CK"], tg], W=[g0])
                            i0r, i0i = g0[:, 0:1], g0[:, 1:2]
                            rds = [g0]
                        sv = (lambda T: T[:, 0:Ls][:, ::-1]) if rev else (lambda T: T[:, 0:Ls])
                        rbc = P["r"][:, j:j + 1].to_broadcast([128, Ls])
                        k.op("dve", lambda e, a=sv(gr), b_=rbc, c_=sv(mr), i_=i0r: e.tensor_tensor_scan(a, b_, c_, i_, ALU.mult, ALU.add),
                             [P["r"], mr] + rds, [gr])
                        k.op("dve", lambda e, a=sv(gi), b_=rbc, c_=sv(mi), i_=i0i: e.tensor_tensor_scan(a, b_, c_, i_, ALU.mult, ALU.add),
                             [P["r"], mi] + rds, [gi])
                        Cv = tv(Ct, 0, Ls); Sv = tv(St, 0, Ls)
                        sl = slice(0, Ls)
                        k.tt(tA[:, sl], gr[:, sl], Cv, ALU.mult, R=[gr, Ct], W=[tA])
                        k.tt(tB[:, sl], gi[:, sl], Sv, ALU.mult, R=[gi, St], W=[tB])
                        k.tt(mr[:, sl], tA[:, sl], tB[:, sl], ALU.subtract, R=[tA, tB], W=[mr])
                        k.tt(tA[:, sl], gi[:, sl], Cv, ALU.mult, R=[gi, Ct], W=[tA])
                        k.tt(tB[:, sl], gr[:, sl], Sv, ALU.mult, R=[gr, St], W=[tB])
                        k.tt(mi[:, sl], tA[:, sl], tB[:, sl], ALU.add, R=[tA, tB], W=[mi])
                        if nm == "c":
                            col = 0 if rev else Ls - 1
                            k.cp(fin[:, 0:1], mr[:, col:col + 1], R=[mr], W=[fin])
                            k.cp(fin[:, 1:2], mi[:, col:col + 1], R=[mi], W=[fin])
                            if "S5FIN" in self.k.dbg:
                                k.dma("sp", self.S5FIN[d, j], fin[:], reads=[fin])
                        if nm == "l" or ctx_out:
                            for b in range(nb):
                                b0 = b * 512; n = min(512, Ls - b0)
                                p = py[b % 2]
                                k.mm(p[:, 0:n], P["CX"][0][:, j, :], mr[:, b0:b0 + n], start=True, stop=False, R=[P["CX"][0], mr], W=[p])
                                k.mm(p[:, 0:n], P["CX"][1][:, j, :], mi[:, b0:b0 + n], start=False, stop=True, R=[P["CX"][1], mi], W=[p])
                                if d == 0:
                                    k.cp(Ya[nm][:, b0:b0 + n], p[:, 0:n], R=[p], W=[Ya[nm]], eng="act")
                                else:
                                    k.tt(Ya[nm][:, b0:b0 + n], Ya[nm][:, b0:b0 + n], p[:, 0:n], ALU.add, R=[Ya[nm], p], W=[Ya[nm]])
                for nm, t0, Ls in seqs:
                    if nm == "c" and not ctx_out:
                        continue
                    y = Ya[nm]
                    k.stt(y[:], u5[nm][:], Dv[:, j:j + 1], y[:], ALU.mult, ALU.add, R=[u5[nm], Dv, y], W=[y])
                    k.act(y[:], y[:], AF.Gelu_apprx_tanh, R=[y], W=[y])
                    k.dma("sp", self.YG[32 * j:32 * j + 32, t0:t0 + Ls], y[:], reads=[y])

    def stage_glu(self, l, t0, t1):
        k = self.k
        with k.stage() as s:
            wg = [s.sb([128, 256], BF16, "wg") for _ in range(2)]
            for kc in range(2):
                k.dma("pool", wg[kc][:], self.w["s5_glu_w"][l, kc * 128:(kc + 1) * 128, :], writes=[wg[kc]])
            bg = s.sb([128, 2], F32)
            k.dma("sp", bg[:], self.w["s5_glu_b"][l].rearrange("(c p) -> p c", p=128), writes=[bg])
            g32 = [s.sb([128, 2, 512], F32, "g32") for _ in range(2)]
            gb = [s.sb([128, 2, 512], BF16, "gb") for _ in range(2)]
            sg = s.sb([128, 512], F32, "sg")
            ob = [s.sb([128, 2, 512], BF16, "ob") for _ in range(2)]
            pz = [s.ps([128, 512], F32, "pz") for _ in range(2)]
            bi = 0
            for b0 in range(t0, t1, 512):
                n = min(512, t1 - b0)
                g = g32[bi % 2]; gbb = gb[bi % 2]; o = ob[bi % 2]
                k.dma("sp", g[:, :, 0:n], self.YG[:, b0:b0 + n].rearrange("(c p) t -> p c t", p=128), writes=[g])
                k.cp(gbb[:, :, 0:n], g[:, :, 0:n], R=[g], W=[gbb])
                for oc in range(2):
                    p = pz[oc]
                    for kc in range(2):
                        k.mm(p[:, 0:n], wg[kc][:, oc * 128:(oc + 1) * 128], gbb[:, kc, 0:n], start=kc == 0, stop=kc == 1,
                             R=[wg[kc], gbb], W=[p])
                    k.act(sg[:, 0:n], p[:, 0:n], AF.Sigmoid, R=[p, bg], W=[sg], bias=bg[:, oc:oc + 1], scale=1.0)
                    k.tt(o[:, oc, 0:n], g[:, oc, 0:n], sg[:, 0:n], ALU.mult, R=[g, sg], W=[o])
                k.dma("sp", self.OT5[:, b0:b0 + n].rearrange("(c p) t -> p c t", p=128), o[:, :, 0:n], reads=[o])
                bi += 1

    def stage_merge(self, l, t0, t1):
        k = self.k
        W = self.w
        with k.stage() as s:
            idf, idb = self.load_ident(s)
            wgt = [[None] * 8 for _ in range(4)]
            for i in range(4):
                for kc in range(8):
                    t = s.sb([128, D], BF16, "wgt")
                    k.dma("pool", t[:], W["w_gate"][l, i, kc * 128:(kc + 1) * 128, :], writes=[t])
                    wgt[i][kc] = t
            wbr = [[None] * 2 for _ in range(4)]
            for i, nm in enumerate(("w_br_mla", "w_br_hy", "w_br_swa", "w_br_s5")):
                for kc in range(2):
                    t = s.sb([128, D], BF16, "wbr")
                    k.dma("pool", t[:], W[nm][l, kc * 128:(kc + 1) * 128, :], writes=[t])
                    wbr[i][kc] = t
            wo = []
            for kc in range(8):
                t = s.sb([128, D], BF16, "wo")
                k.dma("pool", t[:], W["w_out"][l, kc * 128:(kc + 1) * 128, :], writes=[t])
                wo.append(t)
            bgt = s.sb([128, 4, 8], F32)
            k.dma("sp", bgt[:], W["b_gate"][l].rearrange("i (fc p) -> p i fc", p=128), writes=[bgt])
            GP = s.sb([128, D], F32, "GP")
            NB = 512
            ut = s.sb([128, 8, NB], BF16, "ut")
            ot = [s.sb([128, 2, NB], BF16, "ot") for _ in range(4)]
            hy32 = s.sb([128, 256], F32, "hy32"); hyb = s.sb([128, 256], BF16, "hyb")
            mT = s.sb([128, 8, NB], BF16, "mT")
            macc = s.sb([128, NB], F32, "macc")
            sg = [s.sb([128, NB], F32, "sg") for _ in range(2)]
            tm = s.sb([128, NB], F32, "tm")
            xt = [s.sb([128, D], F32, "xt") for _ in range(2)]
            u32 = s.sb([128, D], F32, "u32")
            junk = s.sb([128, 512], F32, "junk"); ss = s.sb([128, 4], F32, "ss")
            pg = [s.ps([128, NB], F32, "pg") for _ in range(2)]
            ppj = [s.ps([128, NB], F32, "ppj") for _ in range(2)]
            po = s.ps([128, D], F32, "po")
            ptb = s.ps([128, 2, 128], BF16, "ptb")
            OTs_ = (self.OTm, None, self.OTs, self.OT5)
            curcls = None
            blocks = []
            if t0 < CTX:
                blocks.append((t0, CTX - t0))
            for b0 in range(max(t0, CTX), t1, NB):
                blocks.append((b0, min(NB, t1 - b0)))
            for b0, n in blocks:
                cls = 1 if b0 < CTX else 0
                if cls != curcls:
                    curcls = cls
                    self.bvec(s, GP, self.MODV[l][cls, 5:6, :])
                k.dma("sp", ut[:, :, 0:n], self.UT[:, :, b0:b0 + n].rearrange("kc p t -> p kc t"), writes=[ut])
                for i in (0, 2, 3):
                    k.dma("sp", ot[i][:, :, 0:n], OTs_[i][:, b0:b0 + n].rearrange("(c p) t -> p c t", p=128), writes=[ot[i]])
                for st in range(n // 128):
                    k.dma("sp", hy32[:], self.OHY[b0 + st * 128:b0 + (st + 1) * 128, :], writes=[hy32])
                    k.cp(hyb[:], hy32[:], R=[hy32], W=[hyb])
                    for c2 in range(2):
                        k.tr(ptb[:, c2, :], hyb[:, c2 * 128:(c2 + 1) * 128], idb[:], R=[hyb, idb], W=[ptb])
                    k.cp(ot[1][:, :, st * 128:(st + 1) * 128], ptb[:], R=[ptb], W=[ot[1]], eng="act")
                for fc in range(8):
                    fs = slice(fc * 128, (fc + 1) * 128)
                    for i in range(4):
                        g = pg[i % 2]; pj = ppj[i % 2]
                        for kc in range(8):
                            k.mm(g[:, 0:n], wgt[i][kc][:, fs], ut[:, kc, 0:n], start=kc == 0, stop=kc == 7, R=[wgt[i][kc], ut], W=[g])
                        for kc in range(2):
                            k.mm(pj[:, 0:n], wbr[i][kc][:, fs], ot[i][:, kc, 0:n], start=kc == 0, stop=kc == 1, R=[wbr[i][kc], ot[i]], W=[pj])
                        sgi = sg[i % 2]
                        k.act(sgi[:, 0:n], g[:, 0:n], AF.Sigmoid, R=[g, bgt], W=[sgi], bias=bgt[:, i, fc:fc + 1], scale=1.0)
                        if i == 0:
                            k.tt(macc[:, 0:n], sgi[:, 0:n], pj[:, 0:n], ALU.mult, R=[sgi, pj], W=[macc])
                        else:
                            k.tt(tm[:, 0:n], sgi[:, 0:n], pj[:, 0:n], ALU.mult, R=[sgi, pj], W=[tm])
                            if i < 3:
                                k.tt(macc[:, 0:n], macc[:, 0:n], tm[:, 0:n], ALU.add, R=[macc, tm], W=[macc])
                            else:
                                k.tt(mT[:, fc, 0:n], macc[:, 0:n], tm[:, 0:n], ALU.add, R=[macc, tm], W=[mT])
                for st in range(n // 128):
                    tok = slice(b0 + st * 128, b0 + (st + 1) * 128)
                    x = xt[st % 2]
                    k.dma("sp", x[:], self.Xs[tok, :], writes=[x])
                    for half in range(2):
                        for fc in range(8):
                            k.mm(po[:, half * 512:(half + 1) * 512], mT[:, fc, st * 128:(st + 1) * 128],
                                 wo[fc][:, half * 512:(half + 1) * 512], start=fc == 0, stop=fc == 7, R=[mT, wo[fc]], W=[po])
                    self.rstd_of(s, [(po[:, 0:512], po), (po[:, 512:1024], po)], D, junk, ss)
                    for half in range(2):
                        hs = slice(half * 512, (half + 1) * 512)
                        k.stt(u32[:, hs], po[:, hs], ss[:, 0:1], GP[:, hs], ALU.mult, ALU.mult, R=[po, ss, GP], W=[u32])
                    k.tt(x[:], x[:], u32[:], ALU.add, R=[x, u32], W=[x])
                    k.dma("sp", self.Xs[tok, :], x[:], reads=[x])

    def build(self):
        self.stage_init()
        all_tiles = list(range(NT))
        lat_tiles = list(range(2, NT))
        for l in range(DEPTH):
            ctx_out = l < DEPTH - 1
            stages = [
                ("mod", lambda: self.stage_mod(l)),
                ("ffn1", lambda: self.stage_ffn(l, 0, "ffn1_up", "ffn1_down", all_tiles)),
                ("proj", lambda: self.stage_proj(l, all_tiles, ctx_q=ctx_out)),
                ("mla", lambda: self.stage_attn(l, "mla", ctx_out)),
                ("swa", lambda: self.stage_attn(l, "swa", ctx_out)),
                ("s5", lambda: self.stage_s5(l, ctx_out)),
                ("glu", lambda: self.stage_glu(l, 0 if ctx_out else CTX, NTOK)),
                ("hyl", lambda: self.hyena_seq(l, "l", SEQ, self.OHY[CTX:NTOK, :])),
                ("hyc", (lambda: self.hyena_seq(l, "c", CTX, self.OHY[0:CTX, :])) if ctx_out else None),
                ("merge", lambda: self.stage_merge(l, 0 if ctx_out else CTX, NTOK)),
                ("ffn2", lambda: self.stage_ffn(l, 2, "ffn2_up", "ffn2_down", all_tiles if ctx_out else lat_tiles,
                                                final=(l == DEPTH - 1))),
            ]
            for nm, fn in stages:
                if self.only is not None and (nm, l) not in self.only:
                    continue
                if fn is not None:
                    if nm in ("hyl", "hyc"):
                        fn()
                    else:
                        with self.k.nc.named_scope("%s_%d" % (nm, l)):
                            fn()
                snap = "Xs_%s_%d" % (nm, l)
                if snap in self.k.dbg:
                    t = self.k.dram(snap, (NTOK, D), F32)
                    with self.k.stage():
                        self.k.dma("sp", t[:, :], self.Xs[:, :])
                if self.stop_after == (nm, l):
                    self.es.close()
                    return
        self.es.close()


def build_program(dbg=(), stop_after=None, only=None):
    m = MK(dbg=dbg, stop_after=stop_after, only=only)
    m.build()
    return m


_CACHE = {}


def kernel(**inputs):
    x = np.ascontiguousarray(np.asarray(inputs["x"], dtype=np.float32))
    c = np.asarray(inputs["c"], dtype=np.float32)
    ctx = np.asarray(inputs["ctx"], dtype=np.float32)
    B = x.shape[0]
    consts = host_consts()
    shared = {"c_ctx": np.ascontiguousarray(np.asarray(inputs["c_ctx"], dtype=np.float32))}
    for n, _ in WEIGHT_SPECS:
        shared[n] = np.ascontiguousarray(np.asarray(inputs[n], dtype=np.float32))
    shared.update(consts)
    in_maps = []
    for b in range(B):
        m = {"x": np.ascontiguousarray(x[b]), "c": np.ascontiguousarray(c[b]), "ctx": np.ascontiguousarray(ctx[b])}
        m.update(shared)
        in_maps.append(m)
    prog = build_program()
    res = run_bass_kernel_spmd(prog.k.nc, in_maps, core_ids=list(range(B)))
    out = np.stack([np.asarray(r["out"], dtype=np.float32) for r in res.results], axis=0)
    return out
```
